# Optimizing a Trainium2 kernel written in Bass

```python
import math
import jax, jax.numpy as jnp
from jax import lax
import numpy as np

D_MODEL = 2048
BATCH = 1
SEQ = 8192
DEPTH = 4

GRID_W = 64
CTX_LEN = 256
HEAD_DIM = 128
N_HEADS_TOTAL = D_MODEL // HEAD_DIM
A_HEADS = N_HEADS_TOTAL // 2
A_KV_HEADS = A_HEADS // 4
B_HEADS = N_HEADS_TOTAL // 4
C_HEADS = N_HEADS_TOTAL // 4
C_HALF = HEAD_DIM // 2
NA_KH_MAX = 8
NA_KW = 16
D_FF = int(math.ceil(8 * D_MODEL / 3 / 128)) * 128
CONV_W = 3
Q_BLOCK = 128
ROPE_THETA = 10000.0
EPS = 1e-6
N_BRANCH = 3

A_Q = A_HEADS * HEAD_DIM
A_KV = A_KV_HEADS * HEAD_DIM
B_W = B_HEADS * HEAD_DIM
C_QK = C_HEADS * 2 * C_HALF
C_V = C_HEADS * HEAD_DIM
PART_SIZES = (A_Q, A_KV, A_KV, B_W, B_W, B_W, C_QK, C_QK, C_V)
SPLITS = tuple(int(v) for v in np.cumsum(PART_SIZES))
D_IN = SPLITS[-1] + N_BRANCH * D_MODEL

kernel_name = "hybrid_gqa_natten_diffattn_convffn_dit"


def rms_norm(x, g):
    xf = x.astype(jnp.float32)
    y = xf * lax.rsqrt(jnp.mean(xf * xf, axis=-1, keepdims=True) + EPS)
    return (y * g.astype(jnp.float32)).astype(x.dtype)


def modulate(x, shift, scale):
    return x * (1 + scale) + shift


def axial_rope_tables(n_tokens, dim):
    n_freq = dim // 4
    inv_freq = ROPE_THETA ** (-jnp.arange(n_freq, dtype=jnp.float32) / n_freq)
    t = jnp.arange(n_tokens, dtype=jnp.int32)
    pos = jnp.stack([t // GRID_W, t % GRID_W], axis=-1).astype(jnp.float32)
    ang = (pos[:, :, None] * inv_freq).reshape(n_tokens, 2 * n_freq)
    return jnp.cos(ang), jnp.sin(ang)


def apply_axial_rope(x, cos, sin):
    S, dim = x.shape[1], x.shape[-1]
    nf = dim // 4
    xr = x.astype(jnp.float32).reshape(x.shape[:-1] + (2, 2, nf))
    bshape = (S,) + (1,) * (x.ndim - 3) + (2, 1, nf)
    c = cos.reshape(bshape)
    s = sin.reshape(bshape)
    x1, x2 = xr[..., 0:1, :], xr[..., 1:2, :]
    out = jnp.concatenate([x1 * c - x2 * s, x2 * c + x1 * s], axis=-2)
    return out.reshape(x.shape).astype(x.dtype)


def project_heads(u, w_in, gq_a, gk_a, gq_b, gk_b, gq_c, gk_c, rope_a, rope_c):
    B, T, _ = u.shape
    parts = jnp.split(u @ w_in, SPLITS, axis=-1)
    qa = rms_norm(parts[0].reshape(B, T, A_HEADS, HEAD_DIM), gq_a)
    ka = rms_norm(parts[1].reshape(B, T, A_KV_HEADS, HEAD_DIM), gk_a)
    va = parts[2].reshape(B, T, A_KV_HEADS, HEAD_DIM)
    qb = rms_norm(parts[3].reshape(B, T, B_HEADS, HEAD_DIM), gq_b)
    kb = rms_norm(parts[4].reshape(B, T, B_HEADS, HEAD_DIM), gk_b)
    vb = parts[5].reshape(B, T, B_HEADS, HEAD_DIM)
    qc = rms_norm(parts[6].reshape(B, T, C_HEADS, 2, C_HALF), gq_c)
    kc = rms_norm(parts[7].reshape(B, T, C_HEADS, 2, C_HALF), gk_c)
    vc = parts[8].reshape(B, T, C_HEADS, HEAD_DIM)
    gates = jax.nn.sigmoid(parts[9].astype(jnp.float32)).astype(u.dtype).reshape(B, T, N_BRANCH, D_MODEL)
    if rope_a is not None:
        qa = apply_axial_rope(qa, *rope_a)
        ka = apply_axial_rope(ka, *rope_a)
        qc = apply_axial_rope(qc, *rope_c)
        kc = apply_axial_rope(kc, *rope_c)
    return qa, ka, va, qb, kb, vb, qc, kc, vc, gates


def sweep_query_blocks(attend, q):
    B, S = q.shape[:2]
    nb = S // Q_BLOCK
    qb = jnp.moveaxis(q.reshape((B, nb, Q_BLOCK) + q.shape[2:]), 1, 0)
    o = lax.map(attend, qb)
    return jnp.moveaxis(o, 0, 1).reshape((B, S) + o.shape[3:])


def gqa_attend(q, k, v):
    B, Q, H, dh = q.shape
    hkv = k.shape[2]
    qg = q.reshape(B, Q, hkv, H // hkv, dh)
    s = jnp.einsum('bqkgd,btkd->bkgqt', qg, k).astype(jnp.float32) * (dh ** -0.5)
    p = jax.nn.softmax(s, axis=-1).astype(v.dtype)
    o = jnp.einsum('bkgqt,btkd->bqkgd', p, v)
    return o.reshape(B, Q, H * dh)


def diff_attend(q, k, v, lam):
    dc = q.shape[-1]
    s = jnp.einsum('bqhcd,bthcd->bhcqt', q, k).astype(jnp.float32) * (dc ** -0.5)
    p = jax.nn.softmax(s, axis=-1)
    a = (p[:, :, 0] - lam * p[:, :, 1]).astype(v.dtype)
    return jnp.einsum('bhqt,bthd->bqhd', a, v)


def neighbourhood_attention(q, k, v, k_ctx, v_ctx, rpb):
    B, S, H, dh = q.shape
    rows = S // GRID_W
    kh = min(NA_KH_MAX, rows)
    kw = NA_KW
    scale = dh ** -0.5
    col = jnp.arange(GRID_W)
    col_start = jnp.clip(col - kw // 2, 0, GRID_W - kw)
    col_idx = col_start[:, None] + jnp.arange(kw)[None, :]
    dc = col_idx - col[:, None] + (NA_KW - 1)
    qg = jnp.moveaxis(q.reshape(B, rows, GRID_W, H, dh), 1, 0)
    kg = k.reshape(B, rows, GRID_W, H, dh)
    vg = v.reshape(B, rows, GRID_W, H, dh)

    def one_row(args):
        r, q_row = args
        rs = jnp.clip(r - kh // 2, 0, rows - kh)
        k_rows = lax.dynamic_slice_in_dim(kg, rs, kh, axis=1)
        v_rows = lax.dynamic_slice_in_dim(vg, rs, kh, axis=1)
        k_win = k_rows[:, :, col_idx]
        v_win = v_rows[:, :, col_idx]
        dr = rs + jnp.arange(kh) - r + (NA_KH_MAX - 1)
        bias = rpb[:, dr[:, None, None], dc[None, :, :]]
        bias = jnp.transpose(bias, (0, 2, 1, 3)).astype(jnp.float32)
        s_win = jnp.einsum('bqhd,bjqwhd->bhqjw', q_row, k_win).astype(jnp.float32) * scale + bias[None]
        s_win = s_win.reshape(B, H, GRID_W, kh * kw)
        s_ctx = jnp.einsum('bqhd,blhd->bhql', q_row, k_ctx).astype(jnp.float32) * scale
        p = jax.nn.softmax(jnp.concatenate([s_win, s_ctx], axis=-1), axis=-1).astype(v.dtype)
        p_win = p[..., :kh * kw].reshape(B, H, GRID_W, kh, kw)
        p_ctx = p[..., kh * kw:]
        return (jnp.einsum('bhqjw,bjqwhd->bqhd', p_win, v_win)
                + jnp.einsum('bhql,blhd->bqhd', p_ctx, v_ctx))

    o = lax.map(one_row, (jnp.arange(rows), qg))
    return jnp.moveaxis(o, 0, 1).reshape(B, S, H * dh)


def merge_branches(oa, ob, oc, gates, w_br_a, w_br_b, w_br_c, w_o):
    merged = (gates[:, :, 0] * (oa @ w_br_a)
              + gates[:, :, 1] * (ob @ w_br_b)
              + gates[:, :, 2] * (oc @ w_br_c))
    return merged @ w_o


def conv_ffn(u, w_up, conv_w, conv_b, w_down):
    T = u.shape[1]
    h = u @ w_up
    pad = CONV_W // 2
    hp = jnp.pad(h, ((0, 0), (pad, pad), (0, 0)))
    h = conv_b + sum(hp[:, j:j + T] * conv_w[j] for j in range(CONV_W))
    a, g = jnp.split(h, 2, axis=-1)
    return (jax.nn.silu(g) * a) @ w_down


def setup_inputs(seed: int = 0) -> dict:
    key = jax.random.key(seed)
    ks = jax.random.split(key, 32)

    def nrm(k, shape, scale):
        return jax.random.normal(k, shape, jnp.float32) * scale

    D = D_MODEL
    return {
        "x": nrm(ks[0], (BATCH, SEQ, D), 1.0),
        "c": nrm(ks[1], (BATCH, D), 1.0),
        "ctx": nrm(ks[2], (BATCH, CTX_LEN, D), 1.0),
        "c_ctx": nrm(ks[3], (D,), 1.0),
        "w_mod": nrm(ks[4], (DEPTH, D, 6 * D), 0.5 * D ** -0.5),
        "b_mod": nrm(ks[5], (DEPTH, 6 * D), 0.02),
        "g_norm1": 1.0 + nrm(ks[6], (DEPTH, D), 0.05),
        "w_in": nrm(ks[7], (DEPTH, D, D_IN), D ** -0.5),
        "gq_a": 1.0 + nrm(ks[8], (DEPTH, HEAD_DIM), 0.05),
        "gk_a": 1.0 + nrm(ks[9], (DEPTH, HEAD_DIM), 0.05),
        "gq_b": 1.0 + nrm(ks[10], (DEPTH, HEAD_DIM), 0.05),
        "gk_b": 1.0 + nrm(ks[11], (DEPTH, HEAD_DIM), 0.05),
        "rpb_b": nrm(ks[12], (DEPTH, B_HEADS, 2 * NA_KH_MAX - 1, 2 * NA_KW - 1), 0.5),
        "gq_c": 1.0 + nrm(ks[13], (DEPTH, C_HALF), 0.05),
        "gk_c": 1.0 + nrm(ks[14], (DEPTH, C_HALF), 0.05),
        "lam_q1": nrm(ks[15], (DEPTH, C_HALF), 0.1),
        "lam_k1": nrm(ks[16], (DEPTH, C_HALF), 0.1),
        "lam_q2": nrm(ks[17], (DEPTH, C_HALF), 0.1),
        "lam_k2": nrm(ks[18], (DEPTH, C_HALF), 0.1),
        "g_subln_c": 1.0 + nrm(ks[19], (DEPTH, HEAD_DIM), 0.05),
        "w_br_a": nrm(ks[20], (DEPTH, A_Q, D), A_Q ** -0.5),
        "w_br_b": nrm(ks[21], (DEPTH, B_W, D), B_W ** -0.5),
        "w_br_c": nrm(ks[22], (DEPTH, C_V, D), C_V ** -0.5),
        "w_o": nrm(ks[23], (DEPTH, D, D), D ** -0.5),
        "g_norm2": 1.0 + nrm(ks[24], (DEPTH, D), 0.05),
        "w_up": nrm(ks[25], (DEPTH, D, 2 * D_FF), D ** -0.5),
        "conv_w": nrm(ks[26], (DEPTH, CONV_W, 2 * D_FF), CONV_W ** -0.5),
        "conv_b": nrm(ks[27], (DEPTH, 2 * D_FF), 0.02),
        "w_down": nrm(ks[28], (DEPTH, D_FF, D), D_FF ** -0.5),
    }


def reference(x, c, ctx, c_ctx, w_mod, b_mod, g_norm1, w_in, gq_a, gk_a, gq_b, gk_b, rpb_b,
              gq_c, gk_c, lam_q1, lam_k1, lam_q2, lam_k2, g_subln_c, w_br_a, w_br_b, w_br_c,
              w_o, g_norm2, w_up, conv_w, conv_b, w_down):
    B, S, D = x.shape
    rope_a = axial_rope_tables(S, HEAD_DIM)
    rope_c = axial_rope_tables(S, C_HALF)
    hx, hc = x, ctx
    for l in range(DEPTH):
        last = l == DEPTH - 1
        mod_x = (jax.nn.silu(c) @ w_mod[l] + b_mod[l]).reshape(B, 1, 6, D)
        mod_c = (jax.nn.silu(c_ctx) @ w_mod[l] + b_mod[l]).reshape(6, D)
        lam_init = 0.8 - 0.6 * math.exp(-0.3 * l)
        lam = (jnp.exp(jnp.sum(lam_q1[l].astype(jnp.float32) * lam_k1[l].astype(jnp.float32)))
               - jnp.exp(jnp.sum(lam_q2[l].astype(jnp.float32) * lam_k2[l].astype(jnp.float32)))
               + lam_init)

        ux = modulate(rms_norm(hx, g_norm1[l]), mod_x[:, :, 0], mod_x[:, :, 1])
        uc = modulate(rms_norm(hc, g_norm1[l]), mod_c[0], mod_c[1])
        qa, ka, va, qb, kb, vb, qc, kc, vc, gx = project_heads(
            ux, w_in[l], gq_a[l], gk_a[l], gq_b[l], gk_b[l], gq_c[l], gk_c[l], rope_a, rope_c)
        qa_t, ka_t, va_t, qb_t, kb_t, vb_t, qc_t, kc_t, vc_t, gt = project_heads(
            uc, w_in[l], gq_a[l], gk_a[l], gq_b[l], gk_b[l], gq_c[l], gk_c[l], None, None)

        k_all_a = jnp.concatenate([ka, ka_t], axis=1)
        v_all_a = jnp.concatenate([va, va_t], axis=1)
        oa = sweep_query_blocks(lambda qblk: gqa_attend(qblk, k_all_a, v_all_a), qa)
        ob = neighbourhood_attention(qb, kb, vb, kb_t, vb_t, rpb_b[l])
        k_all_c = jnp.concatenate([kc, kc_t], axis=1)
        v_all_c = jnp.concatenate([vc, vc_t], axis=1)
        oc = sweep_query_blocks(lambda qblk: diff_attend(qblk, k_all_c, v_all_c, lam), qc)
        oc = (rms_norm(oc, g_subln_c[l]) * (1 - lam_init)).reshape(B, S, C_V)
        hx = hx + mod_x[:, :, 2] * merge_branches(oa, ob, oc, gx, w_br_a[l], w_br_b[l], w_br_c[l], w_o[l])

        if not last:
            T = hc.shape[1]
            oa_t = gqa_attend(qa_t, ka_t, va_t)
            ob_t = gqa_attend(qb_t, kb_t, vb_t)
            oc_t = diff_attend(qc_t, kc_t, vc_t, lam)
            oc_t = (rms_norm(oc_t, g_subln_c[l]) * (1 - lam_init)).reshape(B, T, C_V)
            hc = hc + mod_c[2] * merge_branches(oa_t, ob_t, oc_t, gt, w_br_a[l], w_br_b[l], w_br_c[l], w_o[l])

        vx = modulate(rms_norm(hx, g_norm2[l]), mod_x[:, :, 3], mod_x[:, :, 4])
        hx = hx + mod_x[:, :, 5] * conv_ffn(vx, w_up[l], conv_w[l], conv_b[l], w_down[l])
        if not last:
            vt = modulate(rms_norm(hc, g_norm2[l]), mod_c[3], mod_c[4])
            hc = hc + mod_c[5] * conv_ffn(vt, w_up[l], conv_w[l], conv_b[l], w_down[l])
    return hx
```

```python
import math
from contextlib import ExitStack
import numpy as np
import ml_dtypes
import concourse.bass as bass
import concourse.mybir as mybir
from concourse.bass_utils import run_bass_kernel_spmd

F32 = mybir.dt.float32
BF16 = mybir.dt.bfloat16
AF = mybir.ActivationFunctionType
ALU = mybir.AluOpType
NPBF = ml_dtypes.bfloat16

NR = 8
P = 128
D = 2048
KC = 16
GRID_W = 64
CTX = 256
TC = CTX // NR
HD = 128
NEG = -30000.0
EPS = 1e-6
LIM = 30000


class Cfg:
    def __init__(self, S=8192, DFF=5504, L=4):
        self.S, self.DFF, self.L = S, DFF, L
        self.TL = S // NR
        self.RPC = self.TL // GRID_W
        self.ROWS = S // GRID_W
        self.T = self.TL + TC
        self.T4 = self.T + 4
        self.FC = DFF // P
        self.JT = self.TL // P
        self.NKT = NR * self.JT + CTX // P
        self.NTB = (self.RPC + 8) // 2
        self.E = 2 * self.RPC + 6
        self.EMIN = -self.RPC - 2
        self.DIN = 4608 + 3 * D
        self.NB_IN = self.DIN // 256
        self.LS = [(s, min(512, self.TL - s)) for s in range(0, self.TL, 512)]
        gs = []
        left = self.FC
        ng = (self.FC + 15) // 16
        for g in range(ng):
            n = (left + (ng - g) - 1) // (ng - g)
            gs.append(n)
            left -= n
        self.GS = gs


class Stream:
    def __init__(self, name, kind, serial=False):
        self.name, self.kind, self.count, self.sems, self.serial = name, kind, 0, {}, serial


class Op:
    __slots__ = ("q", "fn", "stream", "n", "needed", "edeps", "ddeps", "idx")


class Sched:
    def __init__(self, nc, stack):
        self.nc, self.stack = nc, stack
        self.queues = {q: [] for q in ("pe", "act", "dve", "pool", "sp")}
        self.es = {q: Stream(q, "eng") for q in ("pe", "act", "dve", "pool", "sp")}
        self.res = {}
        self.base = {}
        self.nsem = 0

    def sem(self, stream, e):
        if e not in stream.sems:
            self.nsem += 1
            stream.sems[e] = self.stack.enter_context(self.nc.semaphore(f"s_{stream.name}_{e}"))
        return stream.sems[e]

    def dstream(self, name, kind="dma", serial=False):
        return Stream(name, kind, serial)

    def _state(self, key):
        st = self.res.get(key)
        if st is None:
            b = self.base.get(key[0] if isinstance(key, tuple) else key)
            st = [dict(b) if b else {}, {}]
            self.res[key] = st
        return st

    def reset(self, buf):
        b = dict(self.base.get(buf, {}))
        for key in [k for k in self.res if (k[0] if isinstance(k, tuple) else k) == buf]:
            st = self.res.pop(key)
            for d in (st[0], st[1]):
                for s, o in d.items():
                    if s not in b or b[s].idx < o.idx:
                        b[s] = o
        self.base[buf] = b

    def add(self, q, fn, r=(), w=(), stream=None):
        op = Op()
        op.q, op.fn = q, fn
        op.stream = stream if stream is not None else self.es[q]
        op.needed = False
        op.n = None
        op.idx = len(self.queues[q]) + 1
        ed, dd = {}, {}

        def need(o):
            s = o.stream
            if s.kind == "eng":
                if s is op.stream and q == "pe":
                    return
                if s not in ed or ed[s].idx < o.idx:
                    ed[s] = o
                o.needed = True
            else:
                dd[s] = s.count

        for k in r:
            st = self._state(k)
            for o in st[0].values():
                need(o)
        for k in w:
            st = self._state(k)
            for o in st[0].values():
                need(o)
            for o in st[1].values():
                need(o)
        if op.stream.kind != "eng":
            if op.stream.serial and op.stream.count > 0:
                dd[op.stream] = op.stream.count
            op.stream.count += 1
            op.n = op.stream.count
            op.idx = op.n
        for k in r:
            self._state(k)[1][op.stream] = op
        for k in w:
            self.res[k] = [{op.stream: op}, {}]
        op.edeps, op.ddeps = ed, dd
        self.queues[q].append(op)
        return op

    def _waits(self, stream, n):
        mult = 16 if stream.kind == "dma" else 1
        e, v = (n - 1) // LIM, (n - 1) % LIM + 1
        out = []
        if stream.kind != "eng":
            for ee in range(e):
                out.append((self.sem(stream, ee), LIM * mult))
        out.append((self.sem(stream, e), v * mult))
        return out

    def _emit_q(self, q, eng):
        waited = {}
        if q in ("sp", "pool"):
            pid = eng.partition_id()
            if not hasattr(self, "spv"):
                self.spv = {}
            self.spv[q] = {
                -1: eng.snap(((pid + (NR - 1)) % NR), min_val=0, max_val=NR - 1),
                0: eng.snap(pid + 0, min_val=0, max_val=NR - 1),
                1: eng.snap(((pid + 1) % NR), min_val=0, max_val=NR - 1),
            }
        for op in self.queues[q]:
            ws = []
            for s, o in op.edeps.items():
                ws += self._waits(s, o.n)
            for s, n in op.ddeps.items():
                if n > 0:
                    ws += self._waits(s, n)
            for sem, val in ws:
                key = id(sem)
                if waited.get(key, 0) < val:
                    eng.wait_ge(sem, val)
                    waited[key] = val
            if op.fn is None:
                continue
            ins = op.fn(eng)
            s = op.stream
            if s.kind == "dma":
                ins.then_inc(self.sem(s, (op.n - 1) // LIM), 16)
            elif s.kind == "cc":
                ins.then_inc(self.sem(s, (op.n - 1) // LIM))
            elif op.needed:
                ins.then_inc(self.sem(s, (op.n - 1) // LIM), 1)

    def emit(self):
        for q, ops in self.queues.items():
            cnt = 0
            for op in ops:
                if op.stream.kind == "eng" and op.needed:
                    cnt += 1
                    op.n = cnt
        for q, ops in self.queues.items():
            for op in ops:
                for s, o in op.edeps.items():
                    self._waits(s, o.n)
                for s, n in op.ddeps.items():
                    if n > 0:
                        self._waits(s, n)
                if op.fn is not None and (op.stream.kind != "eng" or op.needed):
                    self.sem(op.stream, (op.n - 1) // LIM)
        nc = self.nc
        with nc.Block() as block:
            @block.tensor
            def _(e):
                self._emit_q("pe", e)

            @block.scalar
            def _(e):
                self._emit_q("act", e)

            @block.vector
            def _(e):
                self._emit_q("dve", e)

            @block.gpsimd
            def _(e):
                self._emit_q("pool", e)

            @block.sync
            def _(e):
                self._emit_q("sp", e)


def fm(v):
    v = np.asarray(v, np.float32)
    n = v.shape[-1] // P
    v = v.reshape(v.shape[:-1] + (n, P))
    return np.ascontiguousarray(np.moveaxis(v, -1, 0))


def tile_w(w, cols=256):
    K, N = w.shape
    kc, nb = K // P, N // cols
    return np.ascontiguousarray(w.reshape(kc, P, nb, cols).transpose(2, 1, 0, 3)).reshape(nb * P, kc * cols)


def rope_tables(cfg, core, dim):
    nf = dim // 4
    inv = (10000.0 ** (-np.arange(nf, dtype=np.float32) / np.float32(nf))).astype(np.float32)
    t = core * cfg.TL + np.arange(cfg.TL)
    pos = np.stack([t // GRID_W, t % GRID_W], -1).astype(np.float32)
    cos = np.ones((P, cfg.T), np.float32)
    sin = np.zeros((P, cfg.T), np.float32)
    for p in range(P):
        d = p % dim
        axis = d // (dim // 2)
        f = d % nf
        ang = (pos[:, axis] * inv[f]).astype(np.float32)
        cos[p, :cfg.TL] = np.cos(ang)
        sin[p, :cfg.TL] = np.sin(ang)
    return cos, sin


def rot_matrix(dim):
    q = dim // 4
    R = np.zeros((P, P), np.float32)
    for i in range(P):
        d = i % dim
        half = (d % (dim // 2)) // q
        if half == 0:
            R[i + q, i] = -1.0
        else:
            R[i - q, i] = 1.0
    return R


def prep_inputs(cfg, inp):
    L, TL, T = cfg.L, cfg.TL, cfg.T
    f32 = np.float32
    x = np.asarray(inp["x"], f32)[0]
    ctx = np.asarray(inp["ctx"], f32)[0]
    shared = {}
    cvec = np.stack([fm(np.asarray(inp["c"], f32)[0]), fm(np.asarray(inp["c_ctx"], f32))], -1)
    shared["cvec"] = np.ascontiguousarray(cvec)
    shared["bmod"] = np.ascontiguousarray(fm(np.asarray(inp["b_mod"], f32)))
    def dup64(v):
        v = np.asarray(v, f32)
        return np.concatenate([v, v], -1)
    gcols = [fm(inp["g_norm1"]), fm(inp["g_norm2"])]
    singles = [np.asarray(inp["gq_a"], f32), np.asarray(inp["gk_a"], f32), np.asarray(inp["gq_b"], f32),
               np.asarray(inp["gk_b"], f32), dup64(inp["gq_c"]), dup64(inp["gk_c"]),
               np.asarray(inp["g_subln_c"], f32)]
    gs = np.stack([s.T for s in singles], -1)
    shared["gvec"] = np.ascontiguousarray(np.concatenate(gcols + [gs], -1))
    lam = np.zeros((P, L, 4), f32)
    for i, k in enumerate(["lam_q1", "lam_k1", "lam_q2", "lam_k2"]):
        lam[:64, :, i] = np.asarray(inp[k], f32).T
    shared["lamv"] = lam
    FC, DFF = cfg.FC, cfg.DFF
    perm = np.concatenate([np.concatenate([np.arange(j * P, (j + 1) * P), DFF + np.arange(j * P, (j + 1) * P)])
                           for j in range(FC)])
    cw = np.asarray(inp["conv_w"], f32)[:, :, perm]
    cb = np.asarray(inp["conv_b"], f32)[:, perm]
    shared["convw"] = np.ascontiguousarray(fm(cw))
    shared["convb"] = np.ascontiguousarray(fm(cb))
    shared["ident"] = np.eye(P, dtype=f32).astype(NPBF)
    shared["ones"] = np.ones((P, P), f32).astype(NPBF)
    oc = np.zeros((P, P), f32)
    oc[:64, :64] = 1
    oc[64:, 64:] = 1
    shared["onesc"] = oc.astype(NPBF)
    shared["rota"] = rot_matrix(128).astype(NPBF)
    shared["rotc"] = rot_matrix(64).astype(NPBF)
    rm = np.zeros((cfg.RPC, TL), f32)
    for i in range(cfg.RPC):
        rm[i, i * 64:(i + 1) * 64] = 1
    shared["rmq"] = rm.astype(NPBF)
    rpb = np.asarray(inp["rpb_b"], f32)
    K2 = cfg.E + 1
    dr_top = 1 - cfg.EMIN
    jc = np.arange(64)[:, None]
    qc = np.arange(64)[None, :]
    cs = np.clip(qc - 8, 0, 64 - 16)
    colok = (jc >= cs) & (jc < cs + 16)
    dcidx = np.clip(jc - qc + 15, 0, 30)
    ttr = np.full((L, 4, K2, 64, 64), NEG, f32)
    for kk in range(K2):
        dr = dr_top - kk
        if -7 <= dr <= 7:
            blk = rpb[:, :, dr + 7, :][:, :, dcidx]
            ttr[:, :, kk] = np.where(colok[None, None], blk, f32(NEG))
    shared["ttr"] = np.ascontiguousarray(ttr.transpose(0, 1, 3, 2, 4))
    w_in = np.asarray(inp["w_in"], f32)
    w_o = np.asarray(inp["w_o"], f32)
    w_up = np.asarray(inp["w_up"], f32)[:, :, perm]
    w_down = np.asarray(inp["w_down"], f32)
    wbr = [np.asarray(inp[k], f32) for k in ("w_br_a", "w_br_b", "w_br_c")]
    tw = {"w_in": [], "w_o": [], "w_up": [], "w_br": [], "w_down": []}
    for l in range(L):
        tw["w_in"].append(tile_w(w_in[l]))
        tw["w_o"].append(tile_w(w_o[l]))
        tw["w_up"].append(tile_w(w_up[l]))
        parts = [w.reshape(-1, P, 8, 256).transpose(2, 1, 0, 3) for w in (wbr[0][l], wbr[1][l], wbr[2][l])]
        tw["w_br"].append(np.ascontiguousarray(np.concatenate(parts, 2)).reshape(8 * P, 16 * 256))
        tw["w_down"].append(np.ascontiguousarray(
            w_down[l].reshape(FC, P, 8, 256).transpose(2, 1, 0, 3)).reshape(8 * P, FC * 256))
    w_mod = np.asarray(inp["w_mod"], f32)
    maps = []
    for c in range(NR):
        m = dict(shared)
        m["xT"] = np.ascontiguousarray(x[c * TL:(c + 1) * TL].reshape(TL, KC, P).transpose(2, 1, 0))
        m["cT"] = np.ascontiguousarray(ctx[c * TC:(c + 1) * TC].reshape(TC, KC, P).transpose(2, 1, 0))
        m["wmod"] = np.ascontiguousarray(w_mod[:, :, c * 1536:(c + 1) * 1536])
        ca, sa = rope_tables(cfg, c, 128)
        cc_, sc_ = rope_tables(cfg, c, 64)
        m["ropet"] = np.ascontiguousarray(np.stack([ca, sa, cc_, sc_], 1))
        lm = np.full((cfg.RPC, cfg.NTB * P), NEG, f32)
        for i in range(cfg.RPC):
            r = c * cfg.RPC + i
            rs = min(max(r - 4, 0), cfg.ROWS - 8)
            for ar in range(rs, rs + 8):
                rel = ar - (c * cfg.RPC - 4)
                lm[i, rel * 64:(rel + 1) * 64] = 0.0
        m["lmk"] = lm.astype(NPBF)
        fl = np.ones((P, 2), f32)
        if c == 0:
            fl[:, 0] = 0
        if c == NR - 1:
            fl[:, 1] = 0
        m["hflag"] = fl
        for k, lst in tw.items():
            rows = lst[0].shape[0] // NR
            m[k + "_s"] = np.ascontiguousarray(np.stack([a[c * rows:(c + 1) * rows] for a in lst], 0))
        maps.append(m)
    return maps


def build(cfg):
    L, TL, T, T4, FC, JT = cfg.L, cfg.TL, cfg.T, cfg.T4, cfg.FC, cfg.JT
    RPC, NTB, E, NKT = cfg.RPC, cfg.NTB, cfg.E, cfg.NKT
    nc = bass.Bass("TRN2", target_bir_lowering=False)
    stack = ExitStack()

    def din(name, shape, dt=F32):
        return nc.dram_tensor(name, list(shape), dt, kind="ExternalInput").ap()

    def dint(name, shape, dt=BF16):
        return nc.dram_tensor(name, list(shape), dt, kind="Internal").ap()

    xT = din("xT", [P, KC, TL])
    cT = din("cT", [P, KC, TC])
    cvec = din("cvec", [P, KC, 2])
    bmod = din("bmod", [P, L, 96])
    gvec = din("gvec", [P, L, 39])
    lamv = din("lamv", [P, L, 4])
    convw = din("convw", [P, L, 3, 2 * FC])
    convb = din("convb", [P, L, 2 * FC])
    ident_d = din("ident", [P, P], BF16)
    ones_d = din("ones", [P, P], BF16)
    onesc_d = din("onesc", [P, P], BF16)
    rota_d = din("rota", [P, P], BF16)
    rotc_d = din("rotc", [P, P], BF16)
    rmq_d = din("rmq", [RPC, TL], BF16)
    ttr = din("ttr", [L, 4, 64, E + 1, 64])
    wmod = din("wmod", [L, D, 1536])
    ropet = din("ropet", [P, 4, T])
    lmk_d = din("lmk", [RPC, NTB * P], BF16)
    hflag_d = din("hflag", [P, 2])
    wshapes = {"w_in": (cfg.NB_IN * P, KC * 256), "w_o": (8 * P, KC * 256), "w_up": (FC * P, KC * 256),
               "w_br": (8 * P, 16 * 256), "w_down": (8 * P, FC * 256)}
    w_s = {k: din(k + "_s", [L, v[0] // NR, v[1]]) for k, v in wshapes.items()}
    outT = nc.dram_tensor("outT", [P, KC, TL], F32, kind="ExternalOutput").ap()

    w_b = {k: [dint(f"{k}_b{l}", [v[0] // NR, v[1]]) for l in range(L)] for k, v in wshapes.items()}
    w_f = {k: [dint(f"{k}_f{l}", [v[0], v[1]]) for l in range(L)] for k, v in wshapes.items()}
    modp_d = dint("modp", [P, L * 24], F32)
    moda_d = dint("moda", [NR * P, L * 24], F32)
    kb_d = dint("kb", [10 * P, T])
    kall_d = dint("kall", [NR * 10 * P, T])
    vb_d = dint("vb", [10 * P, JT * P])
    vall_d = dint("vall", [NR * 10 * P, JT * P])
    vcb_d = dint("vcb", [10 * TC, P])
    vcall_d = dint("vcall", [NR * 10 * TC, P])
    gsp_d = dint("gsp", [48 * P, T])
    hb_d = dint("hb", [P, 64])
    hall_d = dint("hall", [NR * P, 64])

    sb = lambda name, shape, dt: stack.enter_context(nc.sbuf_tensor(name, list(shape), dt))
    HX = sb("HX", [P, KC, T], F32)
    BA = sb("BA", [P, KC * T4], BF16)
    BQ = sb("BQ", [P, KC * T4], BF16)
    BO = sb("BO", [P, KC * T4], BF16)
    WR = [sb(f"WR{i}", [P, 4096], BF16) for i in range(2)]
    SCQ = 2048
    SC = sb("SC", [P, SCQ], F32)
    CON = sb("CON", [P, 5 * P], BF16)
    MOD = sb("MOD", [P, 2, L, 96], F32)
    GV = sb("GV", [P, L, 39], F32)
    LAMV = sb("LAMV", [P, L, 4], F32)
    SML = sb("SML", [P, 64], F32)
    CVS = sb("CVS", [P, KC, 2], F32)
    DER = sb("DER", [P, 2, 6, KC], F32)
    CW = sb("CW", [P, 4, 2 * FC], F32)
    HFL = sb("HFL", [P, 2], F32)
    EPSB = sb("EPSB", [P, 4], F32)
    HST = sb("HST", [P, KC, 4], BF16)
    HST2 = sb("HST2", [P, 2, 64], BF16)
    ps = [stack.enter_context(nc.psum_tensor(f"ps{i}", [P, 512], F32)) for i in range(8)]

    ident, ones, onesc, rota, rotc = (CON[:, i * P:(i + 1) * P] for i in range(5))
    S = Sched(nc, stack)
    st_ld = [S.dstream(f"wr{i}") for i in range(2)]
    st_misc = S.dstream("misc", serial=True)
    st_cast = {k: S.dstream("cast_" + k) for k in ("w_in", "w_br", "w_o", "w_up", "w_down")}
    st_cc = S.dstream("cc", "cc")
    st_kst = [S.dstream(f"kst{i}") for i in range(2)]
    st_vst = [S.dstream(f"vst{i}") for i in range(2)]
    st_kvl = S.dstream("kvld")
    st_gb = [S.dstream(f"gb{i}") for i in range(2)]
    st_gt = S.dstream("gt")
    st_h = S.dstream("halo", serial=True)
    st_out = S.dstream("out")
    ttrb_d = dint("ttrb", [L * 4 * 64, (E + 1) * 64])
    kloc_d = dint("kloc", [3 * 10 * P, T])
    vloc_d = dint("vloc", [3 * 10 * P, JT * P])
    st_loc = {"sp": S.dstream("loc_sp"), "pool": S.dstream("loc_pool")}
    st_hq = {"sp": st_h, "pool": S.dstream("halo_pool", serial=True)}

    def pe(fn, r=(), w=()):
        return S.add("pe", fn, r, w)

    def act(fn, r=(), w=()):
        return S.add("act", fn, r, w)

    def dve(fn, r=(), w=()):
        return S.add("dve", fn, r, w)

    def pool(fn, r=(), w=()):
        return S.add("pool", fn, r, w)

    def dma(q, stream, out, in_, r=(), w=()):
        return S.add(q, lambda e: e.dma_start(out=out, in_=in_), r, w, stream=stream)

    def ag(in_ap, out_ap, r=(), w=()):
        return S.add("pool", lambda e: e.collective_compute(
            "AllGather", ALU.bypass, replica_groups=[list(range(NR))],
            ins=[in_ap.opt()], outs=[out_ap.opt()]), r, w, stream=st_cc)

    SPL = [(s, n, "x") for (s, n) in cfg.LS] + [(TL, TC, "c")]
    NLS = len(cfg.LS)

    def BAu(kc, s, n):
        return BA[:, kc * T4 + s: kc * T4 + s + n]

    def BQc(ch, s, n):
        return BQ[:, ch * T4 + s: ch * T4 + s + n]

    def BOc(ch, s, n):
        return BO[:, ch * T4 + s: ch * T4 + s + n]

    for i_, v_ in enumerate((EPS * D, EPS * 128.0, EPS * 64.0)):
        S.add("pool", lambda e, i_=i_, v_=v_: e.memset(EPSB[:, i_:i_ + 1], float(v_)), [], [("EPSB", i_)])
    for i, src in enumerate((ident_d, ones_d, onesc_d, rota_d, rotc_d)):
        dma("sp", st_misc, CON[:, i * P:(i + 1) * P], src, w=[("CON", i)])
    dma("sp", st_misc, GV[:], gvec, w=["GV"])
    dma("sp", st_misc, LAMV[:], lamv, w=["LAMV"])
    dma("sp", st_misc, CVS[:], cvec, w=["CVS"])
    dma("sp", st_misc, HFL[:], hflag_d, w=["HFL"])
    dma("sp", st_misc, HX[:, :, 0:TL], xT, w=[("HX", kc) for kc in range(KC)])
    dma("sp", st_misc, HX[:, :, TL:T], cT, w=[("HX", kc) for kc in range(KC)])

    worder = ["w_in", "w_br", "w_o", "w_up", "w_down"]

    def issue_casts(l):
        for k in worder:
            dma("pool", st_cast[k], w_b[k][l], w_s[k][l], w=[("wb", k, l)])

    def issue_wag(l):
        for k in worder:
            ag(w_b[k][l], w_f[k][l], r=[("wb", k, l)], w=[("wf", k, l)])

    issue_casts(0)
    dma("pool", S.dstream("ttrc"), ttrb_d, ttr.rearrange("l h j k q -> (l h j) (k q)"), w=["ttrb_d"])

    SIL = SC[:, 0:KC * 2]
    act(lambda e: e.activation(out=SIL, in_=CVS[:].rearrange("p k w -> p (k w)"), func=AF.Sigmoid), r=["CVS"], w=[("SC", "sil")])
    dve(lambda e: e.tensor_tensor(out=SIL, in0=SIL, in1=CVS[:].rearrange("p k w -> p (k w)"), op=ALU.mult),
        r=[("SC", "sil"), "CVS"], w=[("SC", "sil")])
    MODP = SC[:, 64:64 + L * 24]
    WMv = [w.bitcast(F32) for w in WR]
    mm_list = [(l, j) for l in range(L) for j in range(12)]

    def mm_issue(i):
        l_, j_ = mm_list[i]
        sl_ = i % 2
        dma("sp", st_ld[sl_], WMv[sl_][:, 0:KC * P].rearrange("p (k n) -> p k n", k=KC),
            wmod[l_].rearrange("(k p) n -> p k n", p=P)[:, :, j_ * P:(j_ + 1) * P], w=[("WR", sl_)])
    mm_issue(0)
    for blk, (l, j) in enumerate(mm_list):
        if blk + 1 < len(mm_list):
            mm_issue(blk + 1)
        sl = blk % 2
        pcol = (blk % 16) * 2
        for kc in range(KC):
            pe(lambda e, sl=sl, kc=kc, pcol=pcol: e.matmul(
                ps[7][:, pcol:pcol + 2], WMv[sl][:, kc * P:(kc + 1) * P], SIL[:, kc * 2:kc * 2 + 2],
                start=(kc == 0), stop=(kc == KC - 1)),
               r=[("WR", sl), ("SC", "sil")], w=[("ps", 7)])
        act(lambda e, pcol=pcol, l=l, j=j: e.activation(
            out=MODP[:, (l * 12 + j) * 2:(l * 12 + j) * 2 + 2], in_=ps[7][:, pcol:pcol + 2], func=AF.Copy),
            r=[("ps", 7)], w=[("SC", "modp")])
    wr_ctr_init = len(mm_list)
    dma("sp", st_misc, modp_d, MODP, r=[("SC", "modp")], w=["modp_d"])
    ag(modp_d, moda_d, r=["modp_d"], w=["moda_d"])
    MODA = SC[:, 512:512 + NR * L * 24].rearrange("p (r f) -> p r f", r=NR)
    dma("sp", st_misc, MODA, moda_d.rearrange("(r p) f -> p r f", p=P), r=["moda_d"], w=[("SC", "moda")])
    BMS = SC[:, 1400:1400 + L * 96].rearrange("p (l c) -> p l c", l=L)
    dma("sp", st_misc, BMS, bmod, w=[("SC", "bms")])
    for l in range(L):
        for r_ in range(NR):
            for wch in range(2):
                src = MODA[:, r_, l * 24 + wch: l * 24 + 24: 2]
                dve(lambda e, l=l, r_=r_, wch=wch, src=src: e.tensor_tensor(
                    out=MOD[:, wch, l, r_ * 12:(r_ + 1) * 12], in0=src, in1=BMS[:, l, r_ * 12:(r_ + 1) * 12], op=ALU.add),
                    r=[("SC", "moda"), ("SC", "bms")], w=["MOD"])
    issue_wag(0)

    def layer_scalars(l):
        lam_init = 0.8 - 0.6 * math.exp(-0.3 * l)
        sq = math.sqrt(128.0)
        mults = [sq, sq, sq * (128.0 ** -0.5), sq, 8.0, 8.0, sq * (1.0 - lam_init)]
        for i, m_ in enumerate(mults):
            dve(lambda e, i=i, m_=m_: e.tensor_scalar(out=SML[:, i:i + 1], in0=GV[:, l, 32 + i:33 + i], scalar1=float(m_),
                                                      scalar2=None, op0=ALU.mult), r=["GV"], w=[("SML", i)])
        dve(lambda e: e.tensor_tensor(out=SML[:, 8:10], in0=LAMV[:, l, 0:4:2], in1=LAMV[:, l, 1:4:2], op=ALU.mult),
            r=["LAMV"], w=[("SML", 8)])
        dve(lambda e: e.tensor_copy(out=SML[:, 16:18].bitcast(BF16)[:, 0:2], in_=SML[:, 8:10]), r=[("SML", 8)], w=[("SML", 16)])
        lamb = SML[:, 16:18].bitcast(BF16)
        dve(lambda e: e.tensor_copy(out=SML[:, 10:12], in_=lamb[:, 0:2]), r=[("SML", 16)], w=[("SML", 10)])
        dve(lambda e: e.tensor_tensor(out=SML[:, 12:14], in0=SML[:, 8:10], in1=SML[:, 10:12], op=ALU.subtract),
            r=[("SML", 8), ("SML", 10)], w=[("SML", 12)])
        dve(lambda e: e.tensor_copy(out=lamb[:, 2:4], in_=SML[:, 12:14]), r=[("SML", 12)], w=[("SML", 17)])
        pe(lambda e: e.matmul(ps[7][:, 64:66], ones, lamb[:, 0:2], start=True, stop=False), r=[("SML", 16), ("CON", 1)], w=[("ps", 7)])
        pe(lambda e: e.matmul(ps[7][:, 64:66], ones, lamb[:, 2:4], start=False, stop=True), r=[("SML", 17), ("CON", 1)], w=[("ps", 7)])
        act(lambda e: e.activation(out=SML[:, 20:22], in_=ps[7][:, 64:66], func=AF.Exp), r=[("ps", 7)], w=[("SML", 20)])
        dve(lambda e: e.scalar_tensor_tensor(out=SML[:, 7:8], in0=SML[:, 21:22], scalar=float(-lam_init), in1=SML[:, 20:21],
                                             op0=ALU.add, op1=ALU.subtract), r=[("SML", 20)], w=[("SML", 7)])
        for wch in range(2):
            for which, (sc_i, sh_i, ga_i, goff) in enumerate([(1, 0, 2, 0), (4, 3, 5, 16)]):
                dve(lambda e, wch=wch, which=which, sc_i=sc_i, goff=goff: e.scalar_tensor_tensor(
                    out=DER[:, wch, which * 3 + 0, :], in0=MOD[:, wch, l, sc_i * 16:(sc_i + 1) * 16], scalar=1.0,
                    in1=GV[:, l, goff:goff + 16], op0=ALU.add, op1=ALU.mult), r=["MOD", "GV"], w=["DER"])
                dve(lambda e, wch=wch, which=which, sh_i=sh_i: e.tensor_copy(
                    out=DER[:, wch, which * 3 + 1, :], in_=MOD[:, wch, l, sh_i * 16:(sh_i + 1) * 16]), r=["MOD"], w=["DER"])
                dve(lambda e, wch=wch, which=which, ga_i=ga_i: e.tensor_copy(
                    out=DER[:, wch, which * 3 + 2, :], in_=MOD[:, wch, l, ga_i * 16:(ga_i + 1) * 16]), r=["MOD"], w=["DER"])
        dma("sp", st_misc, CW[:, 0:3, :], convw[:, l], w=["CW"])
        dma("sp", st_misc, CW[:, 3, :], convb[:, l], w=["CW"])

    wr_ctr = [wr_ctr_init]

    def load_block(src_ap, nel, extra_r=()):
        sl = wr_ctr[0] % 2
        wr_ctr[0] += 1
        dma("sp", st_ld[sl], WR[sl][:, 0:nel], src_ap, r=list(extra_r), w=[("WR", sl)])
        return sl

    def ring_blocks(descs):
        slots = {}

        def issue(i):
            src, nel, er = descs[i]
            slots[i] = load_block(src, nel, er)
        issue(0)
        for i in range(len(descs)):
            if i + 1 < len(descs):
                issue(i + 1)
            yield i, slots[i]

    CTXB = 6

    def pst(banks, slot):
        out = []
        for i, (s, n, kind) in enumerate(SPL):
            if kind == "x":
                out.append((ps[banks[i]][:, 0:n], ("ps", banks[i]), s, n, kind))
            else:
                out.append((ps[CTXB][:, slot * 32: slot * 32 + n], ("ps", CTXB), s, n, kind))
        return out

    def rmsnorm_modulate(l, which, dst_fn):
        SQ = SC[:, 0:T // 2 + 16].bitcast(BF16)
        RSTD = SC[:, 600:600 + T]
        TMP = SC[:, 600 + T + 8: 600 + 2 * T + 8] if 600 + 2 * T + 8 <= SCQ else None
        tile = pst([4, 5], 8 + which)
        for kc in range(KC):
            act(lambda e, kc=kc: e.activation(out=SQ[:, 0:T], in_=HX[:, kc, :], func=AF.Square), r=[("HX", kc)], w=[("SC", "sq")])
            for (pap, pk, s, n, kind) in tile:
                pe(lambda e, pap=pap, s=s, n=n, kc=kc: e.matmul(pap, ones, SQ[:, s:s + n], start=(kc == 0), stop=(kc == KC - 1)),
                   r=[("SC", "sq"), ("CON", 1)], w=[pk])
        for (pap, pk, s, n, kind) in tile:
            act(lambda e, pap=pap, s=s, n=n: e.activation(out=RSTD[:, s:s + n], in_=pap, func=AF.Sqrt, bias=EPSB[:, 0:1], scale=1.0),
                r=[pk], w=[("SC", "rstd")])
            dve(lambda e, s=s, n=n: e.reciprocal(out=RSTD[:, s:s + n], in_=RSTD[:, s:s + n]), r=[("SC", "rstd")], w=[("SC", "rstd")])
        sqd = math.sqrt(float(D))
        for kc in range(KC):
            for (s, n, kind) in SPL:
                wch = 0 if kind == "x" else 1
                d = dst_fn(kc, s, n)
                tmp = BO[:, 0:2 * T].bitcast(F32)[:, s:s + n] if which == 0 else BO[:, 0:2 * T].bitcast(F32)[:, s:s + n]
                dve(lambda e, kc=kc, s=s, n=n, wch=wch, tmp=tmp: e.scalar_tensor_tensor(
                    out=tmp, in0=HX[:, kc, s:s + n], scalar=DER[:, wch, which * 3, kc:kc + 1], in1=RSTD[:, s:s + n],
                    op0=ALU.mult, op1=ALU.mult), r=[("HX", kc), ("SC", "rstd"), "DER"], w=[("BO", "tmp", s)])
                act(lambda e, kc=kc, s=s, n=n, wch=wch, tmp=tmp, d=d: e.activation(
                    out=d, in_=tmp, func=AF.Identity, bias=DER[:, wch, which * 3 + 1, kc:kc + 1], scale=float(sqd)),
                    r=[("BO", "tmp", s), "DER"], w=[("BA", kc)])

    def linear_fm(slot, kcs, wcol0, act_fn, tile, act_keys):
        nk = len(kcs)
        for part in (tile[:-1], tile[-1:]):
            for i, (kw, ka) in enumerate(kcs):
                for (pap, pk, s, n, kind) in part:
                    pe(lambda e, pap=pap, kw=kw, ka=ka, s=s, n=n, i=i: e.matmul(
                        pap, WR[slot][:, kw * 256 + wcol0: kw * 256 + wcol0 + P], act_fn(ka, s, n), start=(i == 0), stop=(i == nk - 1)),
                       r=[("WR", slot), act_keys(ka)], w=[pk])

    for l in range(L):
        S.reset("ps")
        layer_scalars(l)
        S.reset("ps")
        S.reset("SC")
        if l + 1 < L:
            issue_casts(l + 1)
        S.reset("BA")
        S.reset("BO")
        rmsnorm_modulate(l, 0, BAu)
        S.reset("BO")
        S.reset("SC")
        BOF = BO[:].bitcast(F32)
        RAW = BOF[:, 0:T]
        RSTD2 = BOF[:, T:2 * T]
        T1 = RSTD2
        T2 = RAW
        ROPE = BOF[:, 2 * T: 6 * T].rearrange("p (a t) -> p a t", a=4)
        dma("sp", st_misc, ROPE, ropet, w=[("BO", "rope")])
        KST = [BO[:, 12 * T + i * T: 12 * T + (i + 1) * T] for i in range(2)]
        assert 14 * T <= KC * T4
        SCB = SC[:].bitcast(BF16)
        SQ2 = SCB[:, 0:T]
        XN = SCB[:, T:2 * T]
        VST = [SCB[:, 2 * T + i * 256: 2 * T + (i + 1) * 256] for i in range(2)]
        assert 2 * T + 512 <= 2 * SCQ

        roles = []
        for i in range(8):
            roles.append(("q", "a", i))
        for i in range(2):
            roles.append(("k", "a", i))
        roles += [("v", None, None)] * 2
        for i in range(4):
            roles.append(("q", "b", 8 + i))
        for i in range(4):
            roles.append(("k", "b", 2 + i))
        roles += [("v", None, None)] * 4
        for i in range(4):
            roles.append(("q", "c", 12 + i))
        for i in range(4):
            roles.append(("k", "c", 6 + i))
        roles += [("v", None, None)] * 4
        for i in range(48):
            roles.append(("g", None, i))
        vhead = {5: 0, 10: 2, 11: 4, 16: 6, 17: 8}

        kst_ctr = [0]
        par = [0]
        descs = [(w_f["w_in"][l][b * P:(b + 1) * P, :], 4096, [("wf", "w_in", l)]) for b in range(cfg.NB_IN)]
        for b, slot in ring_blocks(descs):
            if b in vhead:
                for tt in range(JT + 1):
                    s0, m = (tt * P, P) if tt < JT else (TL, TC)
                    bank = 7 if tt % 2 == 0 else 4
                    for kc in range(KC):
                        pe(lambda e, kc=kc, s0=s0, m=m, slot=slot, bank=bank: e.matmul(
                            ps[bank][0:m, 0:256], BAu(kc, s0, m), WR[slot][:, kc * 256:(kc + 1) * 256], start=(kc == 0), stop=(kc == KC - 1)),
                           r=[("WR", slot), ("BA", kc)], w=[("ps", bank)])
                    vs = VST[tt % 2]
                    vkey = ("SC", "vst", tt % 2)
                    act(lambda e, m=m, vs=vs, bank=bank: e.activation(out=vs[0:m, :], in_=ps[bank][0:m, 0:256], func=AF.Copy), r=[("ps", bank)], w=[vkey])
                    h0 = vhead[b]
                    if tt < JT:
                        dst = vb_d.rearrange("(h p) (j f) -> p h j f", p=P, f=P)[:, h0:h0 + 2, tt, :]
                        dma("sp", st_vst[tt % 2], dst, vs[:, :].rearrange("p (h f) -> p h f", h=2), r=[vkey], w=[("vb_d", b, tt)])
                    else:
                        dst = vcb_d.rearrange("(h i) f -> i h f", i=TC)[:, h0:h0 + 2, :]
                        dma("sp", st_vst[tt % 2], dst, vs[0:TC, :].rearrange("p (h f) -> p h f", h=2), r=[vkey], w=[("vcb_d", b)])
                continue
            for half in range(2):
                ch = 2 * b + half
                kind, rk, idx = roles[ch]
                p_ = par[0] % 2
                par[0] += 1
                tile = pst([0, 1] if p_ == 0 else [2, 3], p_)
                linear_fm(slot, [(kc, kc) for kc in range(KC)], half * P, BAu, tile, lambda ka: ("BA", ka))
                if kind == "g":
                    gs_ = KST[kst_ctr[0] % 2]
                    gkey = ("BO", "kst", kst_ctr[0] % 2)
                    gstream = st_kst[kst_ctr[0] % 2]
                    kst_ctr[0] += 1
                    for (pap, pk, s, n, kd) in tile:
                        act(lambda e, pap=pap, s=s, n=n, gs_=gs_: e.activation(out=gs_[:, s:s + n], in_=pap, func=AF.Sigmoid), r=[pk], w=[gkey])
                    dma("sp", gstream, gsp_d[idx * P:(idx + 1) * P, :], gs_[:, 0:T], r=[gkey], w=[("gsp_d", idx)])
                    continue
                onesm = onesc if rk == "c" else ones
                onek = ("CON", 2) if rk == "c" else ("CON", 1)
                dim = 64.0 if rk == "c" else 128.0
                gi = {("q", "a"): 0, ("k", "a"): 1, ("q", "b"): 2, ("k", "b"): 3, ("q", "c"): 4, ("k", "c"): 5}[(kind, rk)]
                for (pap, pk, s, n, kd) in tile:
                    act(lambda e, pap=pap, s=s, n=n: e.activation(out=SQ2[:, s:s + n], in_=pap, func=AF.Square), r=[pk], w=[("SC", "sq2")])
                    act(lambda e, pap=pap, s=s, n=n: e.activation(out=RAW[:, s:s + n], in_=pap, func=AF.Copy), r=[pk], w=[("BO", "raw")])
                t2 = pst([4, 5], 2)
                for (pap, pk, s, n, kd) in t2:
                    pe(lambda e, pap=pap, s=s, n=n, onesm=onesm: e.matmul(pap, onesm, SQ2[:, s:s + n], start=True, stop=True),
                       r=[("SC", "sq2"), onek], w=[pk])
                    act(lambda e, pap=pap, s=s, n=n, dim=dim: e.activation(out=RSTD2[:, s:s + n], in_=pap, func=AF.Sqrt,
                                                                          bias=EPSB[:, (1 if dim == 128.0 else 2):(2 if dim == 128.0 else 3)], scale=1.0),
                        r=[pk], w=[("BO", "rstd2")])
                    dve(lambda e, s=s, n=n: e.reciprocal(out=RSTD2[:, s:s + n], in_=RSTD2[:, s:s + n]), r=[("BO", "rstd2")], w=[("BO", "rstd2")])
                if kind == "q":
                    dest, dkey = (lambda s, n, idx=idx: BQc(idx, s, n)), ("BQ", idx)
                else:
                    ks_ = KST[kst_ctr[0] % 2]
                    dkey = ("BO", "kst", kst_ctr[0] % 2)
                    kstream = st_kst[kst_ctr[0] % 2]
                    kst_ctr[0] += 1
                    dest = (lambda s, n, ks_=ks_: ks_[:, s:s + n])
                if rk == "b":
                    for (s, n, kd) in SPL:
                        dve(lambda e, s=s, n=n, gi=gi, dest=dest: e.scalar_tensor_tensor(
                            out=dest(s, n), in0=RAW[:, s:s + n], scalar=SML[:, gi:gi + 1], in1=RSTD2[:, s:s + n], op0=ALU.mult, op1=ALU.mult),
                            r=[("BO", "raw"), ("BO", "rstd2"), ("SML", gi)], w=[dkey])
                else:
                    rotm, rotk = (rota, ("CON", 3)) if rk == "a" else (rotc, ("CON", 4))
                    ct, st_ = (0, 1) if rk == "a" else (2, 3)
                    for (s, n, kd) in SPL:
                        dve(lambda e, s=s, n=n, gi=gi: e.scalar_tensor_tensor(
                            out=XN[:, s:s + n], in0=RAW[:, s:s + n], scalar=SML[:, gi:gi + 1], in1=RSTD2[:, s:s + n], op0=ALU.mult, op1=ALU.mult),
                            r=[("BO", "raw"), ("BO", "rstd2"), ("SML", gi)], w=[("SC", "xn")])
                    t3 = pst([4, 5], 3)
                    for (pap, pk, s, n, kd) in t3:
                        pe(lambda e, pap=pap, s=s, n=n, rotm=rotm: e.matmul(pap, rotm, XN[:, s:s + n], start=True, stop=True),
                           r=[("SC", "xn"), rotk], w=[pk])
                        dve(lambda e, s=s, n=n, ct=ct: e.tensor_tensor(out=T1[:, s:s + n], in0=XN[:, s:s + n], in1=ROPE[:, ct, s:s + n], op=ALU.mult),
                             r=[("SC", "xn"), ("BO", "rope")], w=[("BO", "rstd2")])
                        dve(lambda e, pap=pap, s=s, n=n, st_=st_: e.tensor_tensor(out=T2[:, s:s + n], in0=pap, in1=ROPE[:, st_, s:s + n], op=ALU.mult),
                            r=[pk, ("BO", "rope")], w=[("BO", "raw")])
                        dve(lambda e, s=s, n=n, dest=dest: e.tensor_tensor(out=dest(s, n), in0=T1[:, s:s + n], in1=T2[:, s:s + n], op=ALU.add),
                             r=[("BO", "rstd2"), ("BO", "raw")], w=[dkey])
                if kind == "k":
                    dma("sp", kstream, kb_d[idx * P:(idx + 1) * P, :], ks_[:, 0:T], r=[dkey], w=[("kb_d", idx)])

        ag(kb_d, kall_d, r=[("kb_d", i) for i in range(10)], w=["kall_d"])
        ag(vb_d, vall_d, r=[("vb_d", b_, t_) for b_ in vhead for t_ in range(JT)], w=["vall_d"])
        ag(vcb_d, vcall_d, r=[("vcb_d", b_) for b_ in vhead], w=["vcall_d"])
        if l + 1 < L:
            issue_wag(l + 1)
        dq = "sp" if l < 2 else "pool"
        for i_, which in enumerate((-1, 0, 1)):
            S.add(dq, lambda e, i_=i_, which=which, dq=dq: e.dma_start(
                out=kloc_d[i_ * 10 * P:(i_ + 1) * 10 * P, :], in_=kall_d[bass.ds(S.spv[dq][which] * (10 * P), 10 * P), :]),
                ["kall_d"], [("kloc", i_)], stream=st_loc[dq])
            S.add(dq, lambda e, i_=i_, which=which, dq=dq: e.dma_start(
                out=vloc_d[i_ * 10 * P:(i_ + 1) * 10 * P, :], in_=vall_d[bass.ds(S.spv[dq][which] * (10 * P), 10 * P), :]),
                ["vall_d"], [("vloc", i_)], stream=st_loc[dq])

        S.reset("ps")
        S.reset("BA")
        S.reset("BO")
        S.reset("SC")
        NK = NR * T
        KSB = BA[:, 0:NK]
        VSB = BA[:, NK:NK + NKT * P].rearrange("p (t f) -> p t f", f=P)
        PT = SC[:, 0:1024].bitcast(BF16)
        REC = SC[:, 1024:1536]
        kall_v = kall_d.rearrange("(r h p) t -> h p r t", h=10, p=P)
        vall_v = vall_d.rearrange("(r h p) (j f) -> h p r j f", h=10, p=P, f=P)
        vcall_v = vcall_d.rearrange("(r h i) f -> h r i f", h=10, i=TC)

        def load_kv(h):
            dma("sp", st_kvl, KSB[:, 0:NR * TL].rearrange("p (r t) -> p r t", r=NR), kall_v[h, :, :, 0:TL], r=["kall_d"], w=[("BA", "k")])
            dma("sp", st_kvl, KSB[:, NR * TL:NK].rearrange("p (r t) -> p r t", r=NR), kall_v[h, :, :, TL:T], r=["kall_d"], w=[("BA", "k")])
            dma("sp", st_kvl, VSB[:, 0:NR * JT, :].rearrange("p (r j) f -> p r j f", r=NR), vall_v[h], r=["vall_d"], w=[("BA", "v")])
            for r_ in range(NR):
                dma("sp", st_kvl, VSB[(r_ % 4) * TC:(r_ % 4 + 1) * TC, NR * JT + r_ // 4, :], vcall_v[h, r_], r=["vcall_d"], w=[("BA", "v")])

        def attn_pass(qfn, qkey, prange, scale, ktiles, kfn, vfn, groups, bias_fn, out_fn):
            p0, p1 = prange
            ng = len(groups)
            sb_ = lambda g, buf: ps[g * 2 + buf]
            accb = lambda g: ps[4 + g]
            denb = lambda g: ps[6 + g]
            nkt = len(ktiles)

            def qk(i):
                kt = ktiles[i]
                for g, (qs, qn) in enumerate(groups):
                    has_bias = bias_fn is not None and bias_fn(kt, g, None) is not None
                    pe(lambda e, g=g, qs=qs, qn=qn, kt=kt, i=i, has_bias=has_bias: e.matmul(
                        sb_(g, i % 2)[:, 0:qn], kfn(kt)[p0:p1, :], qfn(qs, qn)[p0:p1, :], start=True, stop=not has_bias),
                       r=[("BA", "k"), qkey], w=[("ps", g * 2 + i % 2)])
                    if has_bias:
                        for (lh, rh, keys, last) in bias_fn(kt, g, (qs, qn)):
                            pe(lambda e, g=g, qn=qn, i=i, lh=lh, rh=rh, last=last: e.matmul(
                                sb_(g, i % 2)[:, 0:qn], lh, rh, start=False, stop=last), r=keys, w=[("ps", g * 2 + i % 2)])
                    act(lambda e, g=g, qn=qn, i=i: e.activation(out=PT[:, (g * 2 + i % 2) * 512:(g * 2 + i % 2) * 512 + qn],
                                                               in_=sb_(g, i % 2)[:, 0:qn], func=AF.Exp, scale=float(scale)),
                        r=[("ps", g * 2 + i % 2)], w=[("SC", "pt", g, i % 2)])

            def pv(i):
                kt = ktiles[i]
                for g, (qs, qn) in enumerate(groups):
                    pt = PT[:, (g * 2 + i % 2) * 512:(g * 2 + i % 2) * 512 + qn]
                    pe(lambda e, g=g, qn=qn, kt=kt, i=i, pt=pt: e.matmul(accb(g)[:, 0:qn], vfn(kt), pt, start=(i == 0), stop=(i == nkt - 1)),
                       r=[("BA", "v"), ("SC", "pt", g, i % 2)], w=[("ps", 4 + g)])
                    pe(lambda e, g=g, qn=qn, i=i, pt=pt: e.matmul(denb(g)[:, 0:qn], ones, pt, start=(i == 0), stop=(i == nkt - 1)),
                       r=[("CON", 1), ("SC", "pt", g, i % 2)], w=[("ps", 6 + g)])

            qk(0)
            for i in range(nkt):
                if i + 1 < nkt:
                    qk(i + 1)
                pv(i)
            for g, (qs, qn) in enumerate(groups):
                dve(lambda e, g=g, qn=qn: e.reciprocal(out=REC[:, 0:qn], in_=denb(g)[:, 0:qn]), r=[("ps", 6 + g)], w=[("SC", "rec")])
                out_fn(g, qs, qn, accb(g)[:, 0:qn], REC[:, 0:qn], ("ps", 4 + g))

        lat_groups = [(s, n) for (s, n) in cfg.LS]
        lat_sets = [lat_groups[i:i + 2] for i in range(0, len(lat_groups), 2)]
        all_kt = list(range(NKT))
        ctx_kt = list(range(NR * JT, NKT))
        kfn_std = lambda kt: KSB[:, kt * P:(kt + 1) * P]
        vfn_std = lambda kt: VSB[:, kt, :]

        def out_plain(och):
            def f(g, qs, qn, acc, rec, acck):
                dve(lambda e: e.tensor_tensor(out=BOc(och, qs, qn), in0=acc, in1=rec, op=ALU.mult),
                    r=[acck, ("SC", "rec")], w=[("BO", och)])
            return f

        def out_diff(och, comp):
            def f(g, qs, qn, acc, rec, acck):
                if comp == 0:
                    dve(lambda e: e.tensor_tensor(out=BOc(och, qs, qn), in0=acc, in1=rec, op=ALU.mult),
                        r=[acck, ("SC", "rec")], w=[("BO", och)])
                else:
                    dve(lambda e: e.tensor_tensor(out=REC[:, 0:qn], in0=acc, in1=rec, op=ALU.mult),
                        r=[acck, ("SC", "rec")], w=[("SC", "rec")])
                    dve(lambda e: e.scalar_tensor_tensor(out=BOc(och, qs, qn), in0=REC[:, 0:qn], scalar=SML[:, 7:8], in1=BOc(och, qs, qn),
                                                         op0=ALU.mult, op1=ALU.add), r=[("SC", "rec"), ("BO", och), ("SML", 7)], w=[("BO", och)])
            return f

        sc_a = 128.0 ** -0.5
        for kvh in range(2):
            load_kv(kvh)
            for qh in range(kvh * 4, kvh * 4 + 4):
                qfn = lambda s, n, qh=qh: BQc(qh, s, n)
                for gset in lat_sets:
                    attn_pass(qfn, ("BQ", qh), (0, P), sc_a, all_kt, kfn_std, vfn_std, gset, None, out_plain(qh))
                attn_pass(qfn, ("BQ", qh), (0, P), sc_a, ctx_kt, kfn_std, vfn_std, [(TL, TC)], None, out_plain(qh))
        sc_c = 64.0 ** -0.5
        for h in range(4):
            load_kv(6 + h)
            qfn = lambda s, n, h=h: BQc(12 + h, s, n)
            for comp in range(2):
                pr = (comp * 64, comp * 64 + 64)
                for gset in lat_sets:
                    attn_pass(qfn, ("BQ", 12 + h), pr, sc_c, all_kt, kfn_std, vfn_std, gset, None, out_diff(12 + h, comp))
                attn_pass(qfn, ("BQ", 12 + h), pr, sc_c, ctx_kt, kfn_std, vfn_std, [(TL, TC)], None, out_diff(12 + h, comp))
        S.reset("BA")
        NLK = (RPC + 8) * 64
        KB = BA[:, 0:NLK + CTX]
        ob = NLK + CTX
        VB = BA[:, ob: ob + (NTB + 2) * P].rearrange("p (t f) -> p t f", f=P)
        ob += (NTB + 2) * P
        GT = BA[:, ob: ob + E * 64].rearrange("p (e q) -> p e q", q=64)
        ob += E * 64
        RMQ = BA[0:RPC, ob: ob + TL]
        ob += TL
        LMK = BA[0:RPC, ob: ob + NTB * P]
        ob += NTB * P
        assert ob <= KC * T4
        dma("sp", st_kvl, RMQ, rmq_d, w=[("BA", "rmq")])
        dma("sp", st_kvl, LMK, lmk_d, w=[("BA", "lmk")])
        kall_r = kall_d.rearrange("(r h p) t -> r h p t", h=10, p=P)
        vall_r = vall_d.rearrange("(r h p) (j f) -> r h p j f", h=10, p=P, f=P)
        for h in range(4):
            hh = 2 + h

            def kld(dst, which, c0, c1, hh=hh):
                i_ = which + 1
                dma("sp", st_kvl, dst, kloc_d[(i_ * 10 + hh) * P:(i_ * 10 + hh + 1) * P, c0:c1], r=[("kloc", i_)], w=[("BA", "k")])

            def vld(dst, which, j0, j1, hh=hh):
                i_ = which + 1
                dma("sp", st_kvl, dst.rearrange("p j f -> p (j f)"), vloc_d[(i_ * 10 + hh) * P:(i_ * 10 + hh + 1) * P, j0 * P:j1 * P],
                    r=[("vloc", i_)], w=[("BA", "v")])
            kld(KB[:, 0:256], -1, TL - 256, TL)
            kld(KB[:, 256:256 + TL], 0, 0, TL)
            kld(KB[:, 256 + TL:NLK], 1, 0, 256)
            dma("sp", st_kvl, KB[:, NLK:NLK + CTX].rearrange("p (r t) -> p r t", r=NR), kall_v[hh, :, :, TL:T], r=["kall_d"], w=[("BA", "k")])
            vld(VB[:, 0:2, :], -1, JT - 2, JT)
            vld(VB[:, 2:2 + JT, :], 0, 0, JT)
            vld(VB[:, 2 + JT:NTB, :], 1, 0, 2)
            for r_ in range(NR):
                dma("sp", st_kvl, VB[(r_ % 4) * TC:(r_ % 4 + 1) * TC, NTB + r_ // 4, :], vcall_v[hh, r_], r=["vcall_d"], w=[("BA", "v")])
            ttrb_v = ttrb_d.rearrange("(l h j) (k q) -> l h j k q", l=L, h=4, q=64)
            for jr in range(2):
                dma("sp", st_gt, GT[jr * 64:(jr + 1) * 64, :, :], ttrb_v[l, h, :, 1 - jr:1 - jr + E, :],
                    r=["ttrb_d"], w=[("BA", "gt")])
            qfn = lambda s, n, h=h: BQc(8 + h, s, n)
            kfn_b = lambda kt: KB[:, kt * P:(kt + 1) * P]
            vfn_b = lambda kt: VB[:, kt, :]
            for gset in lat_sets:
                r0 = gset[0][0] // 64
                r1 = (gset[-1][0] + gset[-1][1]) // 64
                kts = list(range(r0 // 2, (r1 - 1 + 8) // 2 + 1)) + [NTB, NTB + 1]

                def bias_fn(kt, g, q, gset=gset):
                    if kt >= NTB:
                        return None
                    if q is None:
                        return True
                    qs, qn = q
                    i0 = qs // 64
                    e0 = i0 + 4 - 2 * kt - cfg.EMIN
                    nrow = qn // 64
                    return [(LMK[:, kt * P:(kt + 1) * P], RMQ[:, qs:qs + qn], [("BA", "rmq"), ("BA", "lmk")], False),
                            (ident, GT[:, e0:e0 + nrow, :].rearrange("p e q -> p (e q)"), [("BA", "gt"), ("CON", 0)], True)]
                attn_pass(qfn, ("BQ", 8 + h), (0, P), 1.0, kts, kfn_b, vfn_b, gset, bias_fn, out_plain(8 + h))
            attn_pass(qfn, ("BQ", 8 + h), (0, P), 1.0, [NTB, NTB + 1], kfn_b, vfn_b, [(TL, TC)], None, out_plain(8 + h))

        if getattr(cfg, "dbg", None) == "attn":
            for ch in range(KC):
                dma("pool", S.dstream(f"dbg{ch}"), outT[:, ch, :], BOc(ch, 0, TL), r=[("BO", ch)], w=[("outT", ch)])
            break
        S.reset("ps")
        S.reset("BA")
        S.reset("SC")
        BAF = BA[:].bitcast(F32)
        SQ3 = BA[:, 0:T]
        RS3 = BAF[:, T:2 * T]
        for h in range(4):
            och = 12 + h
            act(lambda e, och=och: e.activation(out=SQ3, in_=BOc(och, 0, T), func=AF.Square), r=[("BO", och)], w=[("BA", "sq3")])
            t4 = pst([4, 5], 4)
            for (pap, pk, s, n, kd) in t4:
                pe(lambda e, pap=pap, s=s, n=n: e.matmul(pap, ones, SQ3[:, s:s + n], start=True, stop=True), r=[("BA", "sq3"), ("CON", 1)], w=[pk])
                act(lambda e, pap=pap, s=s, n=n: e.activation(out=RS3[:, s:s + n], in_=pap, func=AF.Sqrt, bias=EPSB[:, 1:2], scale=1.0),
                    r=[pk], w=[("BA", "rs3")])
                dve(lambda e, s=s, n=n: e.reciprocal(out=RS3[:, s:s + n], in_=RS3[:, s:s + n]), r=[("BA", "rs3")], w=[("BA", "rs3")])
            dve(lambda e, och=och: e.scalar_tensor_tensor(out=BOc(och, 0, T), in0=BOc(och, 0, T), scalar=SML[:, 6:7], in1=RS3[:, 0:T],
                                                          op0=ALU.mult, op1=ALU.mult), r=[("BO", och), ("BA", "rs3"), ("SML", 6)], w=[("BO", och)])
        gb0 = 4 * T
        GB = [BA[:, gb0 + i * 3 * T: gb0 + (i + 1) * 3 * T].rearrange("p (j t) -> p j t", j=3) for i in range(2)]
        tb0 = (gb0 + 6 * T + 1) // 2 + 1
        MT = [BAF[:, tb0 + i * T: tb0 + (i + 1) * T] for i in range(3)]
        assert 2 * (tb0 + 3 * T) <= KC * T4
        gsp_v = gsp_d.rearrange("(j c p) t -> c p j t", j=3, p=P)
        brk = [(0, 8, 0), (8, 4, 8), (12, 4, 12)]
        for dp in range(8):
            slot = load_block(w_f["w_br"][l][dp * P:(dp + 1) * P, :], 4096, extra_r=[("wf", "w_br", l)])
            for half in range(2):
                dc = dp * 2 + half
                gbi = dc % 2
                dma("sp", st_gb[gbi], GB[gbi], gsp_v[dc], r=[("gsp_d", j_ * 16 + dc) for j_ in range(3)], w=[("BA", "gb", gbi)])
                tiles = [pst([0, 1], 0), pst([2, 3], 1), pst([4, 5], 2)]
                for j, (k0, nk, oc0) in enumerate(brk):
                    linear_fm(slot, [(k0 + i, oc0 + i) for i in range(nk)], half * P, lambda ka, s, n: BOc(ka, s, n), tiles[j], lambda ka: ("BO", ka))
                for si, (s, n, kd) in enumerate(SPL):
                    for j in range(3):
                        pap, pk = tiles[j][si][0], tiles[j][si][1]
                        dve(lambda e, pap=pap, j=j, s=s, n=n, gbi=gbi: e.tensor_tensor(out=MT[j][:, s:s + n], in0=pap, in1=GB[gbi][:, j, s:s + n], op=ALU.mult),
                            r=[pk, ("BA", "gb", gbi)], w=[("BA", "mt", j)])
                    dve(lambda e, s=s, n=n: e.tensor_tensor(out=MT[0][:, s:s + n], in0=MT[0][:, s:s + n], in1=MT[1][:, s:s + n], op=ALU.add),
                         r=[("BA", "mt", 0), ("BA", "mt", 1)], w=[("BA", "mt", 0)])
                    dve(lambda e, s=s, n=n, dc=dc: e.tensor_tensor(out=BQc(dc, s, n), in0=MT[0][:, s:s + n], in1=MT[2][:, s:s + n], op=ALU.add),
                         r=[("BA", "mt", 0), ("BA", "mt", 2)], w=[("BQ", dc)])
        for dp in range(8):
            slot = load_block(w_f["w_o"][l][dp * P:(dp + 1) * P, :], 4096, extra_r=[("wf", "w_o", l)])
            for half in range(2):
                dc = dp * 2 + half
                tile = pst([0, 1] if dc % 2 == 0 else [2, 3], dc % 2)
                linear_fm(slot, [(kc, kc) for kc in range(KC)], half * P, lambda ka, s, n: BQc(ka, s, n), tile, lambda ka: ("BQ", ka))
                for (pap, pk, s, n, kd) in tile:
                    wch = 0 if kd == "x" else 1
                    dve(lambda e, pap=pap, s=s, n=n, dc=dc, wch=wch: e.scalar_tensor_tensor(
                        out=HX[:, dc, s:s + n], in0=pap, scalar=DER[:, wch, 2, dc:dc + 1], in1=HX[:, dc, s:s + n], op0=ALU.mult, op1=ALU.add),
                        r=[pk, ("HX", dc), "DER"], w=[("HX", dc)])

        S.reset("BA")
        S.reset("BO")
        S.reset("SC")

        def vx_dst(kc, s, n):
            off = 1 if s < TL else 3
            return BA[:, kc * T4 + off + s: kc * T4 + off + s + n]
        rmsnorm_modulate(l, 1, vx_dst)
        BA3 = BA[:].rearrange("p (k t) -> p k t", k=KC)
        for i, col in enumerate((1, TL, TL + 3, TL + 2 + TC)):
            dve(lambda e, i=i, col=col: e.tensor_copy(out=HST[:, :, i:i + 1], in_=BA3[:, :, col:col + 1]),
                r=[("BA", kc) for kc in range(KC)], w=["HST"])
        dma("sp", st_h, hb_d, HST[:].rearrange("p k c -> p (k c)"), r=["HST"], w=["hb_d"])
        ag(hb_d, hall_d, r=["hb_d"], w=["hall_d"])
        S.add(dq, lambda e, dq=dq: e.dma_start(out=HST2[:, 0, :], in_=hall_d[bass.ds(S.spv[dq][-1] * P, P), :]),
              ["hall_d"], [("HST2", 0)], stream=st_hq[dq])
        S.add(dq, lambda e, dq=dq: e.dma_start(out=HST2[:, 1, :], in_=hall_d[bass.ds(S.spv[dq][1] * P, P), :]),
              ["hall_d"], [("HST2", 1)], stream=st_hq[dq])
        H2 = HST2[:].rearrange("p w (k c) -> p w k c", c=4)
        for (col, wsel, c_) in ((0, 0, 1), (TL + 1, 1, 0), (TL + 2, 0, 3), (TL + 3 + TC, 1, 2)):
            dve(lambda e, col=col, wsel=wsel, c_=c_: e.tensor_scalar(out=BA3[:, :, col:col + 1], in0=H2[:, wsel, :, c_:c_ + 1],
                                                                      scalar1=HFL[:, wsel:wsel + 1], scalar2=None, op0=ALU.mult),
                r=[("HST2", wsel), "HFL"], w=[("BA", kc) for kc in range(KC)])

        FS = [(0, min(512, T4))]
        while FS[-1][0] + FS[-1][1] < T4:
            s_ = FS[-1][0] + FS[-1][1]
            FS.append((s_, min(512, T4 - s_)))
        NF = len(FS)
        assert NF <= 3
        BOF = BO[:].bitcast(F32)
        HBA = BOF[:, 0:T4]
        HBG = BOF[:, T4:2 * T4]
        AP_ = BOF[:, 2 * T4:3 * T4]
        GP_ = BOF[:, 3 * T4:4 * T4]
        SG = BOF[:, 4 * T4:5 * T4]
        TO = T + 2

        def ffn_tile(par_):
            banks = [0, 1, 2] if par_ == 0 else [3, 4, 5]
            return [(ps[banks[i]][:, 0:n], ("ps", banks[i]), s, n) for i, (s, n) in enumerate(FS)]
        pair0 = 0
        for g, gsz in enumerate(cfg.GS):
            for jj in range(gsz):
                j = pair0 + jj
                slot = load_block(w_f["w_up"][l][j * P:(j + 1) * P, :], 4096, extra_r=[("wf", "w_up", l)])
                for half, (HB, hk, OUT, ok) in enumerate(((HBA, "hba", AP_, "ap"), (HBG, "hbg", GP_, "gp"))):
                    tile = ffn_tile(half)
                    for kc in range(KC):
                        for (pap, pk, s, n) in tile:
                            pe(lambda e, pap=pap, kc=kc, s=s, n=n, slot=slot, half=half: e.matmul(
                                pap, WR[slot][:, kc * 256 + half * P: kc * 256 + half * P + P], BA[:, kc * T4 + s: kc * T4 + s + n],
                                start=(kc == 0), stop=(kc == KC - 1)), r=[("WR", slot), ("BA", kc)], w=[pk])
                    for (pap, pk, s, n) in tile:
                        act(lambda e, pap=pap, s=s, n=n, HB=HB: e.activation(out=HB[:, s:s + n], in_=pap, func=AF.Copy), r=[pk], w=[("BO", hk)])
                    ci = 2 * j + half
                    dve(lambda e, HB=HB, OUT=OUT, ci=ci: e.tensor_scalar(out=OUT[:, 0:TO], in0=HB[:, 0:TO], scalar1=CW[:, 0, ci:ci + 1],
                                                                         scalar2=CW[:, 3, ci:ci + 1], op0=ALU.mult, op1=ALU.add),
                        r=[("BO", hk), "CW"], w=[("BO", ok)])
                    for tap in (1, 2):
                        dve(lambda e, HB=HB, OUT=OUT, ci=ci, tap=tap: e.scalar_tensor_tensor(
                            out=OUT[:, 0:TO], in0=HB[:, tap:tap + TO], scalar=CW[:, tap, ci:ci + 1], in1=OUT[:, 0:TO], op0=ALU.mult, op1=ALU.add),
                            r=[("BO", hk), ("BO", ok), "CW"], w=[("BO", ok)])
                act(lambda e: e.activation(out=SG[:, 0:TO], in_=GP_[:, 0:TO], func=AF.Silu), r=[("BO", "gp")], w=[("BO", "sg")])
                dve(lambda e, jj=jj: e.tensor_tensor(out=BQ[:, jj * T4: jj * T4 + TO], in0=SG[:, 0:TO], in1=AP_[:, 0:TO], op=ALU.mult),
                     r=[("BO", "sg"), ("BO", "ap")], w=[("BQ", jj)])
            for dp in range(8):
                slot = load_block(w_f["w_down"][l][dp * P:(dp + 1) * P, pair0 * 256:(pair0 + gsz) * 256], gsz * 256,
                                  extra_r=[("wf", "w_down", l)])
                for half in range(2):
                    dc = dp * 2 + half
                    tile = pst([0, 1] if dc % 2 == 0 else [2, 3], dc % 2)

                    def actf(ka, s, n):
                        off = 0 if s < TL else 2
                        return BQ[:, ka * T4 + off + s: ka * T4 + off + s + n]
                    linear_fm(slot, [(i, i) for i in range(gsz)], half * P, actf, tile, lambda ka: ("BQ", ka))
                    for (pap, pk, s, n, kd) in tile:
                        wch = 0 if kd == "x" else 1
                        dve(lambda e, pap=pap, s=s, n=n, dc=dc, wch=wch: e.scalar_tensor_tensor(
                            out=HX[:, dc, s:s + n], in0=pap, scalar=DER[:, wch, 5, dc:dc + 1], in1=HX[:, dc, s:s + n], op0=ALU.mult, op1=ALU.add),
                            r=[pk, ("HX", dc), "DER"], w=[("HX", dc)])
            pair0 += gsz
        S.reset("BQ")
        S.reset("BO")

    if getattr(cfg, "dbg", None) is None:
        dma("sp", st_out, outT, HX[:, :, 0:TL], r=[("HX", kc) for kc in range(KC)], w=["outT"])
        S.add("sp", None, ["outT"], [])
    else:
        S.add("sp", None, [("outT", ch) for ch in range(KC)], [])
    S.emit()
    stack.close()
    return nc


_CACHE = {}


def run(cfg, inputs):
    maps = prep_inputs(cfg, inputs)
    key = (cfg.S, cfg.DFF, cfg.L)
    if key not in _CACHE:
        _CACHE[key] = build(cfg)
    nc = _CACHE[key]
    res = run_bass_kernel_spmd(nc, maps, core_ids=list(range(NR)))
    outs = []
    for c in range(NR):
        o = np.asarray(res.results[c]["outT"], np.float32)
        outs.append(o.transpose(2, 1, 0).reshape(cfg.TL, D))
    return np.concatenate(outs, 0)[None].astype(np.float32)


def kernel(**inputs):
    cfg = Cfg()
    return run(cfg, inputs)
```

```python
import math
from contextlib import ExitStack
import numpy as np
import ml_dtypes
import concourse.bass as bass
import concourse.mybir as mybir
from concourse.bass_utils import run_bass_kernel_spmd

F32 = mybir.dt.float32
BF16 = mybir.dt.bfloat16
AF = mybir.ActivationFunctionType
ALU = mybir.AluOpType
NPBF = ml_dtypes.bfloat16

NR = 8
P = 128
D = 2048
KC = 16
GRID_W = 64
CTX = 256
TC = CTX // NR
HD = 128
NEG = -30000.0
EPS = 1e-6
LIM = 30000


class Cfg:
    def __init__(self, S=8192, DFF=5504, L=4):
        self.S, self.DFF, self.L = S, DFF, L
        self.TL = S // NR
        self.RPC = self.TL // GRID_W
        self.ROWS = S // GRID_W
        self.T = self.TL + TC
        self.T4 = self.T + 4
        self.FC = DFF // P
        self.JT = self.TL // P
        self.NKT = NR * self.JT + CTX // P
        self.NTB = (self.RPC + 8) // 2
        self.E = 2 * self.RPC + 6
        self.EMIN = -self.RPC - 2
        self.DIN = 4608 + 3 * D
        self.NB_IN = self.DIN // 256
        self.LS = [(s, min(512, self.TL - s)) for s in range(0, self.TL, 512)]
        gs = []
        left = self.FC
        ng = (self.FC + 15) // 16
        for g in range(ng):
            n = (left + (ng - g) - 1) // (ng - g)
            gs.append(n)
            left -= n
        self.GS = gs


class Stream:
    def __init__(self, name, kind, serial=False):
        self.name, self.kind, self.count, self.sems, self.serial = name, kind, 0, {}, serial


class Op:
    __slots__ = ("q", "fn", "stream", "n", "needed", "edeps", "ddeps", "idx")


class Sched:
    def __init__(self, nc, stack):
        self.nc, self.stack = nc, stack
        self.queues = {q: [] for q in ("pe", "act", "dve", "pool", "sp")}
        self.es = {q: Stream(q, "eng") for q in ("pe", "act", "dve", "pool", "sp")}
        self.res = {}
        self.base = {}
        self.nsem = 0

    def sem(self, stream, e):
        if e not in stream.sems:
            self.nsem += 1
            stream.sems[e] = self.stack.enter_context(self.nc.semaphore(f"s_{stream.name}_{e}"))
        return stream.sems[e]

    def dstream(self, name, kind="dma", serial=False):
        return Stream(name, kind, serial)

    def _state(self, key):
        st = self.res.get(key)
        if st is None:
            b = self.base.get(key[0] if isinstance(key, tuple) else key)
            st = [dict(b) if b else {}, {}]
            self.res[key] = st
        return st

    def reset(self, buf):
        b = dict(self.base.get(buf, {}))
        for key in [k for k in self.res if (k[0] if isinstance(k, tuple) else k) == buf]:
            st = self.res.pop(key)
            for d in (st[0], st[1]):
                for s, o in d.items():
                    if s not in b or b[s].idx < o.idx:
                        b[s] = o
        self.base[buf] = b

    def add(self, q, fn, r=(), w=(), stream=None):
        op = Op()
        op.q, op.fn = q, fn
        op.stream = stream if stream is not None else self.es[q]
        op.needed = False
        op.n = None
        op.idx = len(self.queues[q]) + 1
        ed, dd = {}, {}

        def need(o):
            s = o.stream
            if s.kind == "eng":
                if s is op.stream and q == "pe":
                    return
                if s not in ed or ed[s].idx < o.idx:
                    ed[s] = o
                o.needed = True
            elif s.kind == "cc":
                dd[s] = max(dd.get(s, 0), o.n)
            else:
                dd[s] = s.count

        for k in r:
            st = self._state(k)
            for o in st[0].values():
                need(o)
        for k in w:
            st = self._state(k)
            for o in st[0].values():
                need(o)
            for o in st[1].values():
                need(o)
        if op.stream.kind != "eng":
            if op.stream.serial and op.stream.count > 0:
                dd[op.stream] = op.stream.count
            op.stream.count += 1
            op.n = op.stream.count
            op.idx = op.n
        for k in r:
            self._state(k)[1][op.stream] = op
        for k in w:
            self.res[k] = [{op.stream: op}, {}]
        op.edeps, op.ddeps = ed, dd
        self.queues[q].append(op)
        return op

    def _waits(self, stream, n):
        mult = 16 if stream.kind == "dma" else 1
        e, v = (n - 1) // LIM, (n - 1) % LIM + 1
        out = []
        if stream.kind != "eng":
            for ee in range(e):
                out.append((self.sem(stream, ee), LIM * mult))
        out.append((self.sem(stream, e), v * mult))
        return out

    def _emit_q(self, q, eng):
        waited = {}
        if q in ("sp", "pool"):
            pid = eng.partition_id()
            if not hasattr(self, "spv"):
                self.spv = {}
            self.spv[q] = {
                -1: eng.snap(((pid + (NR - 1)) % NR), min_val=0, max_val=NR - 1),
                0: eng.snap(pid + 0, min_val=0, max_val=NR - 1),
                1: eng.snap(((pid + 1) % NR), min_val=0, max_val=NR - 1),
            }
        for op in self.queues[q]:
            ws = []
            for s, o in op.edeps.items():
                ws += self._waits(s, o.n)
            for s, n in op.ddeps.items():
                if n > 0:
                    ws += self._waits(s, n)
            for sem, val in ws:
                key = id(sem)
                if waited.get(key, 0) < val:
                    eng.wait_ge(sem, val)
                    waited[key] = val
            if op.fn is None:
                continue
            ins = op.fn(eng)
            s = op.stream
            if s.kind == "dma":
                ins.then_inc(self.sem(s, (op.n - 1) // LIM), 16)
            elif s.kind == "cc":
                ins.then_inc(self.sem(s, (op.n - 1) // LIM))
            elif op.needed:
                ins.then_inc(self.sem(s, (op.n - 1) // LIM), 1)

    def emit(self):
        for q, ops in self.queues.items():
            cnt = 0
            for op in ops:
                if op.stream.kind == "eng" and op.needed:
                    cnt += 1
                    op.n = cnt
        for q, ops in self.queues.items():
            for op in ops:
                for s, o in op.edeps.items():
                    self._waits(s, o.n)
                for s, n in op.ddeps.items():
                    if n > 0:
                        self._waits(s, n)
                if op.fn is not None and (op.stream.kind != "eng" or op.needed):
                    self.sem(op.stream, (op.n - 1) // LIM)
        nc = self.nc
        with nc.Block() as block:
            @block.tensor
            def _(e):
                self._emit_q("pe", e)

            @block.scalar
            def _(e):
                self._emit_q("act", e)

            @block.vector
            def _(e):
                self._emit_q("dve", e)

            @block.gpsimd
            def _(e):
                self._emit_q("pool", e)

            @block.sync
            def _(e):
                self._emit_q("sp", e)


def fm(v):
    v = np.asarray(v, np.float32)
    n = v.shape[-1] // P
    v = v.reshape(v.shape[:-1] + (n, P))
    return np.ascontiguousarray(np.moveaxis(v, -1, 0))


def tile_w(w, cols=256):
    K, N = w.shape
    kc, nb = K // P, N // cols
    return np.ascontiguousarray(w.reshape(kc, P, nb, cols).transpose(2, 1, 0, 3)).reshape(nb * P, kc * cols)


def rope_tables(cfg, core, dim):
    nf = dim // 4
    inv = (10000.0 ** (-np.arange(nf, dtype=np.float32) / np.float32(nf))).astype(np.float32)
    t = core * cfg.TL + np.arange(cfg.TL)
    pos = np.stack([t // GRID_W, t % GRID_W], -1).astype(np.float32)
    cos = np.ones((P, cfg.T), np.float32)
    sin = np.zeros((P, cfg.T), np.float32)
    for p in range(P):
        d = p % dim
        axis = d // (dim // 2)
        f = d % nf
        ang = (pos[:, axis] * inv[f]).astype(np.float32)
        cos[p, :cfg.TL] = np.cos(ang)
        sin[p, :cfg.TL] = np.sin(ang)
    return cos, sin


def rot_matrix(dim):
    q = dim // 4
    R = np.zeros((P, P), np.float32)
    for i in range(P):
        d = i % dim
        half = (d % (dim // 2)) // q
        if half == 0:
            R[i + q, i] = -1.0
        else:
            R[i - q, i] = 1.0
    return R


def prep_inputs(cfg, inp):
    L, TL, T = cfg.L, cfg.TL, cfg.T
    f32 = np.float32
    x = np.asarray(inp["x"], f32)[0]
    ctx = np.asarray(inp["ctx"], f32)[0]
    shared = {}
    cvec = np.stack([fm(np.asarray(inp["c"], f32)[0]), fm(np.asarray(inp["c_ctx"], f32))], -1)
    shared["cvec"] = np.ascontiguousarray(cvec)
    shared["bmod"] = np.ascontiguousarray(fm(np.asarray(inp["b_mod"], f32)))
    def dup64(v):
        v = np.asarray(v, f32)
        return np.concatenate([v, v], -1)
    gcols = [fm(inp["g_norm1"]), fm(inp["g_norm2"])]
    singles = [np.asarray(inp["gq_a"], f32), np.asarray(inp["gk_a"], f32), np.asarray(inp["gq_b"], f32),
               np.asarray(inp["gk_b"], f32), dup64(inp["gq_c"]), dup64(inp["gk_c"]),
               np.asarray(inp["g_subln_c"], f32)]
    gs = np.stack([s.T for s in singles], -1)
    shared["gvec"] = np.ascontiguousarray(np.concatenate(gcols + [gs], -1))
    lam = np.zeros((P, L, 4), f32)
    for i, k in enumerate(["lam_q1", "lam_k1", "lam_q2", "lam_k2"]):
        lam[:64, :, i] = np.asarray(inp[k], f32).T
    shared["lamv"] = lam
    FC, DFF = cfg.FC, cfg.DFF
    perm = np.concatenate([np.concatenate([np.arange(j * P, (j + 1) * P), DFF + np.arange(j * P, (j + 1) * P)])
                           for j in range(FC)])
    cw = np.asarray(inp["conv_w"], f32)[:, :, perm]
    cb = np.asarray(inp["conv_b"], f32)[:, perm]
    shared["convw"] = np.ascontiguousarray(fm(cw))
    shared["convb"] = np.ascontiguousarray(fm(cb))
    shared["ident"] = np.eye(P, dtype=f32).astype(NPBF)
    shared["ones"] = np.ones((P, P), f32).astype(NPBF)
    oc = np.zeros((P, P), f32)
    oc[:64, :64] = 1
    oc[64:, 64:] = 1
    shared["onesc"] = oc.astype(NPBF)
    shared["rota"] = rot_matrix(128).astype(NPBF)
    shared["rotc"] = rot_matrix(64).astype(NPBF)
    rm = np.zeros((cfg.RPC, TL), f32)
    for i in range(cfg.RPC):
        rm[i, i * 64:(i + 1) * 64] = 1
    shared["rmq"] = rm.astype(NPBF)
    rpb = np.asarray(inp["rpb_b"], f32)
    K2 = cfg.E + 1
    dr_top = 1 - cfg.EMIN
    jc = np.arange(64)[:, None]
    qc = np.arange(64)[None, :]
    cs = np.clip(qc - 8, 0, 64 - 16)
    colok = (jc >= cs) & (jc < cs + 16)
    dcidx = np.clip(jc - qc + 15, 0, 30)
    ttr = np.full((L, 4, K2, 64, 64), NEG, f32)
    for kk in range(K2):
        dr = dr_top - kk
        if -7 <= dr <= 7:
            blk = rpb[:, :, dr + 7, :][:, :, dcidx]
            ttr[:, :, kk] = np.where(colok[None, None], blk, f32(NEG))
    shared["ttr"] = np.ascontiguousarray(ttr.transpose(0, 1, 3, 2, 4))
    w_in = np.asarray(inp["w_in"], f32)
    w_o = np.asarray(inp["w_o"], f32)
    w_up = np.asarray(inp["w_up"], f32)[:, :, perm]
    w_down = np.asarray(inp["w_down"], f32)
    wbr = [np.asarray(inp[k], f32) for k in ("w_br_a", "w_br_b", "w_br_c")]
    tw = {"w_in": [], "w_o": [], "w_up": [], "w_br": [], "w_down": []}
    for l in range(L):
        tw["w_in"].append(tile_w(w_in[l]))
        tw["w_o"].append(tile_w(w_o[l]))
        tw["w_up"].append(tile_w(w_up[l]))
        parts = [w.reshape(-1, P, 8, 256).transpose(2, 1, 0, 3) for w in (wbr[0][l], wbr[1][l], wbr[2][l])]
        tw["w_br"].append(np.ascontiguousarray(np.concatenate(parts, 2)).reshape(8 * P, 16 * 256))
        tw["w_down"].append(np.ascontiguousarray(
            w_down[l].reshape(FC, P, 8, 256).transpose(2, 1, 0, 3)).reshape(8 * P, FC * 256))
    w_mod = np.asarray(inp["w_mod"], f32)
    maps = []
    for c in range(NR):
        m = dict(shared)
        m["xT"] = np.ascontiguousarray(x[c * TL:(c + 1) * TL].reshape(TL, KC, P).transpose(2, 1, 0))
        m["cT"] = np.ascontiguousarray(ctx[c * TC:(c + 1) * TC].reshape(TC, KC, P).transpose(2, 1, 0))
        m["wmod"] = np.ascontiguousarray(w_mod[:, :, c * 1536:(c + 1) * 1536])
        ca, sa = rope_tables(cfg, c, 128)
        cc_, sc_ = rope_tables(cfg, c, 64)
        m["ropet"] = np.ascontiguousarray(np.stack([ca, sa, cc_, sc_], 1))
        lm = np.full((cfg.RPC, cfg.NTB * P), NEG, f32)
        for i in range(cfg.RPC):
            r = c * cfg.RPC + i
            rs = min(max(r - 4, 0), cfg.ROWS - 8)
            for ar in range(rs, rs + 8):
                rel = ar - (c * cfg.RPC - 4)
                lm[i, rel * 64:(rel + 1) * 64] = 0.0
        m["lmk"] = lm.astype(NPBF)
        fl = np.ones((P, 2), f32)
        if c == 0:
            fl[:, 0] = 0
        if c == NR - 1:
            fl[:, 1] = 0
        m["hflag"] = fl
        for k, lst in tw.items():
            rows = lst[0].shape[0] // NR
            m[k + "_s"] = np.ascontiguousarray(np.stack([a[c * rows:(c + 1) * rows] for a in lst], 0))
        maps.append(m)
    return maps


def build(cfg):
    L, TL, T, T4, FC, JT = cfg.L, cfg.TL, cfg.T, cfg.T4, cfg.FC, cfg.JT
    RPC, NTB, E, NKT = cfg.RPC, cfg.NTB, cfg.E, cfg.NKT
    nc = bass.Bass("TRN2", target_bir_lowering=False)
    stack = ExitStack()

    def din(name, shape, dt=F32):
        return nc.dram_tensor(name, list(shape), dt, kind="ExternalInput").ap()

    def dint(name, shape, dt=BF16):
        return nc.dram_tensor(name, list(shape), dt, kind="Internal").ap()

    xT = din("xT", [P, KC, TL])
    cT = din("cT", [P, KC, TC])
    cvec = din("cvec", [P, KC, 2])
    bmod = din("bmod", [P, L, 96])
    gvec = din("gvec", [P, L, 39])
    lamv = din("lamv", [P, L, 4])
    convw = din("convw", [P, L, 3, 2 * FC])
    convb = din("convb", [P, L, 2 * FC])
    ident_d = din("ident", [P, P], BF16)
    ones_d = din("ones", [P, P], BF16)
    onesc_d = din("onesc", [P, P], BF16)
    rota_d = din("rota", [P, P], BF16)
    rotc_d = din("rotc", [P, P], BF16)
    rmq_d = din("rmq", [RPC, TL], BF16)
    ttr = din("ttr", [L, 4, 64, E + 1, 64])
    wmod = din("wmod", [L, D, 1536])
    ropet = din("ropet", [P, 4, T])
    lmk_d = din("lmk", [RPC, NTB * P], BF16)
    hflag_d = din("hflag", [P, 2])
    wshapes = {"w_in": (cfg.NB_IN * P, KC * 256), "w_o": (8 * P, KC * 256), "w_up": (FC * P, KC * 256),
               "w_br": (8 * P, 16 * 256), "w_down": (8 * P, FC * 256)}
    w_s = {k: din(k + "_s", [L, v[0] // NR, v[1]]) for k, v in wshapes.items()}
    outT = nc.dram_tensor("outT", [P, KC, TL], F32, kind="ExternalOutput").ap()

    w_b = {k: [dint(f"{k}_b{l}", [v[0] // NR, v[1]]) for l in range(L)] for k, v in wshapes.items()}
    w_f = {k: [dint(f"{k}_f{l}", [v[0], v[1]]) for l in range(L)] for k, v in wshapes.items()}
    modp_d = dint("modp", [P, L * 24], F32)
    moda_d = dint("moda", [NR * P, L * 24], F32)
    kb_d = dint("kb", [10 * P, T])
    kall_d = dint("kall", [NR * 10 * P, T])
    vb_d = dint("vb", [10 * P, JT * P])
    vall_d = dint("vall", [NR * 10 * P, JT * P])
    vcb_d = dint("vcb", [10 * TC, P])
    vcall_d = dint("vcall", [NR * 10 * TC, P])
    gsp_d = dint("gsp", [48 * P, T])
    hb_d = dint("hb", [P, 64])
    hall_d = dint("hall", [NR * P, 64])

    sb = lambda name, shape, dt: stack.enter_context(nc.sbuf_tensor(name, list(shape), dt))
    HX = sb("HX", [P, KC, T], F32)
    BA = sb("BA", [P, KC * T4], BF16)
    BQ = sb("BQ", [P, KC * T4], BF16)
    BO = sb("BO", [P, KC * T4], BF16)
    WR = [sb(f"WR{i}", [P, 4096], BF16) for i in range(2)]
    SCQ = 2048
    SC = sb("SC", [P, SCQ], F32)
    CON = sb("CON", [P, 5 * P], BF16)
    MOD = sb("MOD", [P, 2, L, 96], F32)
    GV = sb("GV", [P, L, 39], F32)
    LAMV = sb("LAMV", [P, L, 4], F32)
    SML = sb("SML", [P, 64], F32)
    CVS = sb("CVS", [P, KC, 2], F32)
    DER = sb("DER", [P, 2, 6, KC], F32)
    CW = sb("CW", [P, 4, 2 * FC], F32)
    HFL = sb("HFL", [P, 2], F32)
    EPSB = sb("EPSB", [P, 4], F32)
    HST = sb("HST", [P, KC, 4], BF16)
    HST2 = sb("HST2", [P, 2, 64], BF16)
    ps = [stack.enter_context(nc.psum_tensor(f"ps{i}", [P, 512], F32)) for i in range(8)]

    ident, ones, onesc, rota, rotc = (CON[:, i * P:(i + 1) * P] for i in range(5))
    S = Sched(nc, stack)
    st_ld = [S.dstream(f"wr{i}") for i in range(2)]
    st_misc = S.dstream("misc", serial=True)
    st_cast = {k: S.dstream("cast_" + k) for k in ("w_in", "w_br", "w_o", "w_up", "w_down")}
    st_cc = S.dstream("cc", "cc")
    st_kst = [S.dstream(f"kst{i}") for i in range(2)]
    st_vst = [S.dstream(f"vst{i}") for i in range(2)]
    st_gst = [S.dstream(f"gst{i}") for i in range(2)]
    st_kvl = S.dstream("kvld")
    st_gb = [S.dstream(f"gb{i}") for i in range(2)]
    st_gt = S.dstream("gt")
    st_h = S.dstream("halo", serial=True)
    st_out = S.dstream("out")
    ttrb_d = dint("ttrb", [L * 4 * 64, (E + 1) * 64])
    kloc_d = dint("kloc", [3 * 10 * P, T])
    vloc_d = dint("vloc", [3 * 10 * P, JT * P])
    st_loc = {"sp": S.dstream("loc_sp"), "pool": S.dstream("loc_pool")}
    st_hq = {"sp": st_h, "pool": S.dstream("halo_pool", serial=True)}

    def pe(fn, r=(), w=()):
        return S.add("pe", fn, r, w)

    def act(fn, r=(), w=()):
        return S.add("act", fn, r, w)

    def dve(fn, r=(), w=()):
        return S.add("dve", fn, r, w)

    def pool(fn, r=(), w=()):
        return S.add("pool", fn, r, w)

    def dma(q, stream, out, in_, r=(), w=()):
        return S.add(q, lambda e: e.dma_start(out=out, in_=in_), r, w, stream=stream)

    def ag(in_ap, out_ap, r=(), w=()):
        return S.add("pool", lambda e: e.collective_compute(
            "AllGather", ALU.bypass, replica_groups=[list(range(NR))],
            ins=[in_ap.opt()], outs=[out_ap.opt()]), r, w, stream=st_cc)

    SPL = [(s, n, "x") for (s, n) in cfg.LS] + [(TL, TC, "c")]
    NLS = len(cfg.LS)

    def BAu(kc, s, n):
        return BA[:, kc * T4 + s: kc * T4 + s + n]

    def BQc(ch, s, n):
        return BQ[:, ch * T4 + s: ch * T4 + s + n]

    def BOc(ch, s, n):
        return BO[:, ch * T4 + s: ch * T4 + s + n]

    for i_, v_ in enumerate((EPS * D, EPS * 128.0, EPS * 64.0)):
        S.add("pool", lambda e, i_=i_, v_=v_: e.memset(EPSB[:, i_:i_ + 1], float(v_)), [], [("EPSB", i_)])
    for i, src in enumerate((ident_d, ones_d, onesc_d, rota_d, rotc_d)):
        dma("sp", st_misc, CON[:, i * P:(i + 1) * P], src, w=[("CON", i)])
    dma("sp", st_misc, GV[:], gvec, w=["GV"])
    dma("sp", st_misc, LAMV[:], lamv, w=["LAMV"])
    dma("sp", st_misc, CVS[:], cvec, w=["CVS"])
    dma("sp", st_misc, HFL[:], hflag_d, w=["HFL"])
    dma("sp", st_misc, HX[:, :, 0:TL], xT, w=[("HX", kc) for kc in range(KC)])
    dma("sp", st_misc, HX[:, :, TL:T], cT, w=[("HX", kc) for kc in range(KC)])

    worder = ["w_in", "w_br", "w_o", "w_up", "w_down"]

    def issue_casts(l):
        for k in worder:
            dma("pool", st_cast[k], w_b[k][l], w_s[k][l], w=[("wb", k, l)])

    def issue_wag(l):
        for k in worder:
            ag(w_b[k][l], w_f[k][l], r=[("wb", k, l)], w=[("wf", k, l)])

    issue_casts(0)
    dma("pool", S.dstream("ttrc"), ttrb_d, ttr.rearrange("l h j k q -> (l h j) (k q)"), w=["ttrb_d"])

    SIL = SC[:, 0:KC * 2]
    act(lambda e: e.activation(out=SIL, in_=CVS[:].rearrange("p k w -> p (k w)"), func=AF.Sigmoid), r=["CVS"], w=[("SC", "sil")])
    dve(lambda e: e.tensor_tensor(out=SIL, in0=SIL, in1=CVS[:].rearrange("p k w -> p (k w)"), op=ALU.mult),
        r=[("SC", "sil"), "CVS"], w=[("SC", "sil")])
    MODP = SC[:, 64:64 + L * 24]
    WMv = [w.bitcast(F32) for w in WR]
    mm_list = [(l, j) for l in range(L) for j in range(12)]

    def mm_issue(i):
        l_, j_ = mm_list[i]
        sl_ = i % 2
        dma("sp", st_ld[sl_], WMv[sl_][:, 0:KC * P].rearrange("p (k n) -> p k n", k=KC),
            wmod[l_].rearrange("(k p) n -> p k n", p=P)[:, :, j_ * P:(j_ + 1) * P], w=[("WR", sl_)])
    mm_issue(0)
    for blk, (l, j) in enumerate(mm_list):
        if blk + 1 < len(mm_list):
            mm_issue(blk + 1)
        sl = blk % 2
        pcol = (blk % 16) * 2
        for kc in range(KC):
            pe(lambda e, sl=sl, kc=kc, pcol=pcol: e.matmul(
                ps[7][:, pcol:pcol + 2], WMv[sl][:, kc * P:(kc + 1) * P], SIL[:, kc * 2:kc * 2 + 2],
                start=(kc == 0), stop=(kc == KC - 1)),
               r=[("WR", sl), ("SC", "sil")], w=[("ps", 7)])
        act(lambda e, pcol=pcol, l=l, j=j: e.activation(
            out=MODP[:, (l * 12 + j) * 2:(l * 12 + j) * 2 + 2], in_=ps[7][:, pcol:pcol + 2], func=AF.Copy),
            r=[("ps", 7)], w=[("SC", "modp")])
    wr_ctr_init = len(mm_list)
    dma("sp", st_misc, modp_d, MODP, r=[("SC", "modp")], w=["modp_d"])
    ag(modp_d, moda_d, r=["modp_d"], w=["moda_d"])
    MODA = SC[:, 512:512 + NR * L * 24].rearrange("p (r f) -> p r f", r=NR)
    dma("sp", st_misc, MODA, moda_d.rearrange("(r p) f -> p r f", p=P), r=["moda_d"], w=[("SC", "moda")])
    BMS = SC[:, 1400:1400 + L * 96].rearrange("p (l c) -> p l c", l=L)
    dma("sp", st_misc, BMS, bmod, w=[("SC", "bms")])
    for l in range(L):
        for r_ in range(NR):
            for wch in range(2):
                src = MODA[:, r_, l * 24 + wch: l * 24 + 24: 2]
                dve(lambda e, l=l, r_=r_, wch=wch, src=src: e.tensor_tensor(
                    out=MOD[:, wch, l, r_ * 12:(r_ + 1) * 12], in0=src, in1=BMS[:, l, r_ * 12:(r_ + 1) * 12], op=ALU.add),
                    r=[("SC", "moda"), ("SC", "bms")], w=["MOD"])
    issue_wag(0)

    def layer_scalars(l):
        lam_init = 0.8 - 0.6 * math.exp(-0.3 * l)
        sq = math.sqrt(128.0)
        mults = [sq, sq, sq * (128.0 ** -0.5), sq, 8.0, 8.0, sq * (1.0 - lam_init)]
        for i, m_ in enumerate(mults):
            dve(lambda e, i=i, m_=m_: e.tensor_scalar(out=SML[:, i:i + 1], in0=GV[:, l, 32 + i:33 + i], scalar1=float(m_),
                                                      scalar2=None, op0=ALU.mult), r=["GV"], w=[("SML", i)])
        dve(lambda e: e.tensor_tensor(out=SML[:, 8:10], in0=LAMV[:, l, 0:4:2], in1=LAMV[:, l, 1:4:2], op=ALU.mult),
            r=["LAMV"], w=[("SML", 8)])
        dve(lambda e: e.tensor_copy(out=SML[:, 16:18].bitcast(BF16)[:, 0:2], in_=SML[:, 8:10]), r=[("SML", 8)], w=[("SML", 16)])
        lamb = SML[:, 16:18].bitcast(BF16)
        dve(lambda e: e.tensor_copy(out=SML[:, 10:12], in_=lamb[:, 0:2]), r=[("SML", 16)], w=[("SML", 10)])
        dve(lambda e: e.tensor_tensor(out=SML[:, 12:14], in0=SML[:, 8:10], in1=SML[:, 10:12], op=ALU.subtract),
            r=[("SML", 8), ("SML", 10)], w=[("SML", 12)])
        dve(lambda e: e.tensor_copy(out=lamb[:, 2:4], in_=SML[:, 12:14]), r=[("SML", 12)], w=[("SML", 17)])
        pe(lambda e: e.matmul(ps[7][:, 64:66], ones, lamb[:, 0:2], start=True, stop=False), r=[("SML", 16), ("CON", 1)], w=[("ps", 7)])
        pe(lambda e: e.matmul(ps[7][:, 64:66], ones, lamb[:, 2:4], start=False, stop=True), r=[("SML", 17), ("CON", 1)], w=[("ps", 7)])
        act(lambda e: e.activation(out=SML[:, 20:22], in_=ps[7][:, 64:66], func=AF.Exp), r=[("ps", 7)], w=[("SML", 20)])
        dve(lambda e: e.scalar_tensor_tensor(out=SML[:, 7:8], in0=SML[:, 21:22], scalar=float(-lam_init), in1=SML[:, 20:21],
                                             op0=ALU.add, op1=ALU.subtract), r=[("SML", 20)], w=[("SML", 7)])
        for wch in range(2):
            for which, (sc_i, sh_i, ga_i, goff) in enumerate([(1, 0, 2, 0), (4, 3, 5, 16)]):
                dve(lambda e, wch=wch, which=which, sc_i=sc_i, goff=goff: e.scalar_tensor_tensor(
                    out=DER[:, wch, which * 3 + 0, :], in0=MOD[:, wch, l, sc_i * 16:(sc_i + 1) * 16], scalar=1.0,
                    in1=GV[:, l, goff:goff + 16], op0=ALU.add, op1=ALU.mult), r=["MOD", "GV"], w=["DER"])
                dve(lambda e, wch=wch, which=which, sh_i=sh_i: e.tensor_copy(
                    out=DER[:, wch, which * 3 + 1, :], in_=MOD[:, wch, l, sh_i * 16:(sh_i + 1) * 16]), r=["MOD"], w=["DER"])
                dve(lambda e, wch=wch, which=which, ga_i=ga_i: e.tensor_copy(
                    out=DER[:, wch, which * 3 + 2, :], in_=MOD[:, wch, l, ga_i * 16:(ga_i + 1) * 16]), r=["MOD"], w=["DER"])
        dma("sp", st_misc, CW[:, 0:3, :], convw[:, l], w=["CW"])
        dma("sp", st_misc, CW[:, 3, :], convb[:, l], w=["CW"])

    wr_ctr = [wr_ctr_init]

    def load_block(src_ap, nel, extra_r=()):
        sl = wr_ctr[0] % 2
        wr_ctr[0] += 1
        dma("sp", st_ld[sl], WR[sl][:, 0:nel], src_ap, r=list(extra_r), w=[("WR", sl)])
        return sl

    def ring_blocks(descs):
        slots = {}

        def issue(i):
            src, nel, er = descs[i]
            slots[i] = load_block(src, nel, er)
        issue(0)
        for i in range(len(descs)):
            if i + 1 < len(descs):
                issue(i + 1)
            yield i, slots[i]

    CTXB = 6

    def pst(banks, slot):
        out = []
        for i, (s, n, kind) in enumerate(SPL):
            if kind == "x":
                out.append((ps[banks[i]][:, 0:n], ("ps", banks[i]), s, n, kind))
            else:
                out.append((ps[CTXB][:, slot * 32: slot * 32 + n], ("ps", CTXB), s, n, kind))
        return out

    def rmsnorm_modulate(l, which, dst_fn):
        SQ = SC[:, 0:T // 2 + 16].bitcast(BF16)
        RSTD = SC[:, 600:600 + T]
        TMP = SC[:, 600 + T + 8: 600 + 2 * T + 8] if 600 + 2 * T + 8 <= SCQ else None
        tile = pst([4, 5], 8 + which)
        for kc in range(KC):
            act(lambda e, kc=kc: e.activation(out=SQ[:, 0:T], in_=HX[:, kc, :], func=AF.Square), r=[("HX", kc)], w=[("SC", "sq")])
            for (pap, pk, s, n, kind) in tile:
                pe(lambda e, pap=pap, s=s, n=n, kc=kc: e.matmul(pap, ones, SQ[:, s:s + n], start=(kc == 0), stop=(kc == KC - 1)),
                   r=[("SC", "sq"), ("CON", 1)], w=[pk])
        for (pap, pk, s, n, kind) in tile:
            act(lambda e, pap=pap, s=s, n=n: e.activation(out=RSTD[:, s:s + n], in_=pap, func=AF.Sqrt, bias=EPSB[:, 0:1], scale=1.0),
                r=[pk], w=[("SC", "rstd")])
            dve(lambda e, s=s, n=n: e.reciprocal(out=RSTD[:, s:s + n], in_=RSTD[:, s:s + n]), r=[("SC", "rstd")], w=[("SC", "rstd")])
        sqd = math.sqrt(float(D))
        for kc in range(KC):
            for (s, n, kind) in SPL:
                wch = 0 if kind == "x" else 1
                d = dst_fn(kc, s, n)
                tmp = BO[:, 0:2 * T].bitcast(F32)[:, s:s + n] if which == 0 else BO[:, 0:2 * T].bitcast(F32)[:, s:s + n]
                dve(lambda e, kc=kc, s=s, n=n, wch=wch, tmp=tmp: e.scalar_tensor_tensor(
                    out=tmp, in0=HX[:, kc, s:s + n], scalar=DER[:, wch, which * 3, kc:kc + 1], in1=RSTD[:, s:s + n],
                    op0=ALU.mult, op1=ALU.mult), r=[("HX", kc), ("SC", "rstd"), "DER"], w=[("BO", "tmp", s)])
                act(lambda e, kc=kc, s=s, n=n, wch=wch, tmp=tmp, d=d: e.activation(
                    out=d, in_=tmp, func=AF.Identity, bias=DER[:, wch, which * 3 + 1, kc:kc + 1], scale=float(sqd)),
                    r=[("BO", "tmp", s), "DER"], w=[("BA", kc)])

    def linear_fm(slot, kcs, wcol0, act_fn, tile, act_keys):
        nk = len(kcs)
        for part in (tile[:-1], tile[-1:]):
            for i, (kw, ka) in enumerate(kcs):
                for (pap, pk, s, n, kind) in part:
                    pe(lambda e, pap=pap, kw=kw, ka=ka, s=s, n=n, i=i: e.matmul(
                        pap, WR[slot][:, kw * 256 + wcol0: kw * 256 + wcol0 + P], act_fn(ka, s, n), start=(i == 0), stop=(i == nk - 1)),
                       r=[("WR", slot), act_keys(ka)], w=[pk])

    for l in range(L):
        S.reset("ps")
        layer_scalars(l)
        S.reset("ps")
        S.reset("SC")
        if l + 1 < L:
            issue_casts(l + 1)
        S.reset("BA")
        S.reset("BO")
        rmsnorm_modulate(l, 0, BAu)
        S.reset("BO")
        S.reset("SC")
        BOF = BO[:].bitcast(F32)
        RAW = BOF[:, 0:T]
        RSTD2 = BOF[:, T:2 * T]
        T1 = RSTD2
        T2 = RAW
        ROPE = BOF[:, 2 * T: 6 * T].rearrange("p (a t) -> p a t", a=4)
        dma("sp", st_misc, ROPE, ropet, w=[("BO", "rope")])
        KST = [BO[:, 12 * T + i * T: 12 * T + (i + 1) * T] for i in range(2)]
        assert 14 * T <= KC * T4
        SCB = SC[:].bitcast(BF16)
        SQ2 = SCB[:, 0:T]
        XN = SCB[:, T:2 * T]
        VST = [SCB[:, 2 * T + i * 256: 2 * T + (i + 1) * 256] for i in range(2)]
        assert 2 * T + 512 <= 2 * SCQ

        roles = []
        for i in range(8):
            roles.append(("q", "a", i))
        for i in range(2):
            roles.append(("k", "a", i))
        roles += [("v", None, None)] * 2
        for i in range(4):
            roles.append(("q", "b", 8 + i))
        for i in range(4):
            roles.append(("k", "b", 2 + i))
        roles += [("v", None, None)] * 4
        for i in range(4):
            roles.append(("q", "c", 12 + i))
        for i in range(4):
            roles.append(("k", "c", 6 + i))
        roles += [("v", None, None)] * 4
        for i in range(48):
            roles.append(("g", None, i))
        vhead = {5: 0, 10: 2, 11: 4, 16: 6, 17: 8}

        kst_ctr = [0]
        par = [0]
        descs = [(w_f["w_in"][l][b * P:(b + 1) * P, :], 4096, [("wf", "w_in", l)]) for b in range(cfg.NB_IN)]
        for b, slot in ring_blocks(descs):
            if b in vhead:
                for tt in range(JT + 1):
                    s0, m = (tt * P, P) if tt < JT else (TL, TC)
                    bank = 7 if tt % 2 == 0 else 4
                    for kc in range(KC):
                        pe(lambda e, kc=kc, s0=s0, m=m, slot=slot, bank=bank: e.matmul(
                            ps[bank][0:m, 0:256], BAu(kc, s0, m), WR[slot][:, kc * 256:(kc + 1) * 256], start=(kc == 0), stop=(kc == KC - 1)),
                           r=[("WR", slot), ("BA", kc)], w=[("ps", bank)])
                    vs = VST[tt % 2]
                    vkey = ("SC", "vst", tt % 2)
                    act(lambda e, m=m, vs=vs, bank=bank: e.activation(out=vs[0:m, :], in_=ps[bank][0:m, 0:256], func=AF.Copy), r=[("ps", bank)], w=[vkey])
                    h0 = vhead[b]
                    if tt < JT:
                        dst = vb_d.rearrange("(h p) (j f) -> p h j f", p=P, f=P)[:, h0:h0 + 2, tt, :]
                        dma("sp", st_vst[tt % 2], dst, vs[:, :].rearrange("p (h f) -> p h f", h=2), r=[vkey], w=[("vb_d", b, tt)])
                    else:
                        dst = vcb_d.rearrange("(h i) f -> i h f", i=TC)[:, h0:h0 + 2, :]
                        dma("sp", st_vst[tt % 2], dst, vs[0:TC, :].rearrange("p (h f) -> p h f", h=2), r=[vkey], w=[("vcb_d", b)])
                if b == 17:
                    ag(kb_d, kall_d, r=[("kb_d", i) for i in range(10)], w=["kall_d"])
                    ag(vb_d, vall_d, r=[("vb_d", b_, t_) for b_ in vhead for t_ in range(JT)], w=["vall_d"])
                    ag(vcb_d, vcall_d, r=[("vcb_d", b_) for b_ in vhead], w=["vcall_d"])
                    if l + 1 < L:
                        issue_wag(l + 1)
                continue
            for half in range(2):
                ch = 2 * b + half
                kind, rk, idx = roles[ch]
                p_ = par[0] % 2
                par[0] += 1
                tile = pst([0, 1] if p_ == 0 else [2, 3], p_)
                linear_fm(slot, [(kc, kc) for kc in range(KC)], half * P, BAu, tile, lambda ka: ("BA", ka))
                if kind == "g":
                    gs_ = KST[kst_ctr[0] % 2]
                    gkey = ("BO", "kst", kst_ctr[0] % 2)
                    gstream = st_gst[kst_ctr[0] % 2]
                    kst_ctr[0] += 1
                    for (pap, pk, s, n, kd) in tile:
                        act(lambda e, pap=pap, s=s, n=n, gs_=gs_: e.activation(out=gs_[:, s:s + n], in_=pap, func=AF.Sigmoid), r=[pk], w=[gkey])
                    dma("sp", gstream, gsp_d[idx * P:(idx + 1) * P, :], gs_[:, 0:T], r=[gkey], w=[("gsp_d", idx)])
                    continue
                onesm = onesc if rk == "c" else ones
                onek = ("CON", 2) if rk == "c" else ("CON", 1)
                dim = 64.0 if rk == "c" else 128.0
                gi = {("q", "a"): 0, ("k", "a"): 1, ("q", "b"): 2, ("k", "b"): 3, ("q", "c"): 4, ("k", "c"): 5}[(kind, rk)]
                for (pap, pk, s, n, kd) in tile:
                    act(lambda e, pap=pap, s=s, n=n: e.activation(out=SQ2[:, s:s + n], in_=pap, func=AF.Square), r=[pk], w=[("SC", "sq2")])
                    act(lambda e, pap=pap, s=s, n=n: e.activation(out=RAW[:, s:s + n], in_=pap, func=AF.Copy), r=[pk], w=[("BO", "raw")])
                t2 = pst([4, 5], 2)
                for (pap, pk, s, n, kd) in t2:
                    pe(lambda e, pap=pap, s=s, n=n, onesm=onesm: e.matmul(pap, onesm, SQ2[:, s:s + n], start=True, stop=True),
                       r=[("SC", "sq2"), onek], w=[pk])
                    act(lambda e, pap=pap, s=s, n=n, dim=dim: e.activation(out=RSTD2[:, s:s + n], in_=pap, func=AF.Sqrt,
                                                                          bias=EPSB[:, (1 if dim == 128.0 else 2):(2 if dim == 128.0 else 3)], scale=1.0),
                        r=[pk], w=[("BO", "rstd2")])
                    dve(lambda e, s=s, n=n: e.reciprocal(out=RSTD2[:, s:s + n], in_=RSTD2[:, s:s + n]), r=[("BO", "rstd2")], w=[("BO", "rstd2")])
                if kind == "q":
                    dest, dkey = (lambda s, n, idx=idx: BQc(idx, s, n)), ("BQ", idx)
                else:
                    ks_ = KST[kst_ctr[0] % 2]
                    dkey = ("BO", "kst", kst_ctr[0] % 2)
                    kstream = st_kst[kst_ctr[0] % 2]
                    kst_ctr[0] += 1
                    dest = (lambda s, n, ks_=ks_: ks_[:, s:s + n])
                if rk == "b":
                    for (s, n, kd) in SPL:
                        dve(lambda e, s=s, n=n, gi=gi, dest=dest: e.scalar_tensor_tensor(
                            out=dest(s, n), in0=RAW[:, s:s + n], scalar=SML[:, gi:gi + 1], in1=RSTD2[:, s:s + n], op0=ALU.mult, op1=ALU.mult),
                            r=[("BO", "raw"), ("BO", "rstd2"), ("SML", gi)], w=[dkey])
                else:
                    rotm, rotk = (rota, ("CON", 3)) if rk == "a" else (rotc, ("CON", 4))
                    ct, st_ = (0, 1) if rk == "a" else (2, 3)
                    for (s, n, kd) in SPL:
                        dve(lambda e, s=s, n=n, gi=gi: e.scalar_tensor_tensor(
                            out=XN[:, s:s + n], in0=RAW[:, s:s + n], scalar=SML[:, gi:gi + 1], in1=RSTD2[:, s:s + n], op0=ALU.mult, op1=ALU.mult),
                            r=[("BO", "raw"), ("BO", "rstd2"), ("SML", gi)], w=[("SC", "xn")])
                    t3 = pst([4, 5], 3)
                    for (pap, pk, s, n, kd) in t3:
                        pe(lambda e, pap=pap, s=s, n=n, rotm=rotm: e.matmul(pap, rotm, XN[:, s:s + n], start=True, stop=True),
                           r=[("SC", "xn"), rotk], w=[pk])
                        dve(lambda e, s=s, n=n, ct=ct: e.tensor_tensor(out=T1[:, s:s + n], in0=XN[:, s:s + n], in1=ROPE[:, ct, s:s + n], op=ALU.mult),
                             r=[("SC", "xn"), ("BO", "rope")], w=[("BO", "rstd2")])
                        dve(lambda e, pap=pap, s=s, n=n, st_=st_: e.tensor_tensor(out=T2[:, s:s + n], in0=pap, in1=ROPE[:, st_, s:s + n], op=ALU.mult),
                            r=[pk, ("BO", "rope")], w=[("BO", "raw")])
                        dve(lambda e, s=s, n=n, dest=dest: e.tensor_tensor(out=dest(s, n), in0=T1[:, s:s + n], in1=T2[:, s:s + n], op=ALU.add),
                             r=[("BO", "rstd2"), ("BO", "raw")], w=[dkey])
                if kind == "k":
                    dma("sp", kstream, kb_d[idx * P:(idx + 1) * P, :], ks_[:, 0:T], r=[dkey], w=[("kb_d", idx)])

        dq = "sp" if l < 2 else "pool"
        for i_, which in enumerate((-1, 0, 1)):
            S.add(dq, lambda e, i_=i_, which=which, dq=dq: e.dma_start(
                out=kloc_d[i_ * 10 * P:(i_ + 1) * 10 * P, :], in_=kall_d[bass.ds(S.spv[dq][which] * (10 * P), 10 * P), :]),
                ["kall_d"], [("kloc", i_)], stream=st_loc[dq])
            S.add(dq, lambda e, i_=i_, which=which, dq=dq: e.dma_start(
                out=vloc_d[i_ * 10 * P:(i_ + 1) * 10 * P, :], in_=vall_d[bass.ds(S.spv[dq][which] * (10 * P), 10 * P), :]),
                ["vall_d"], [("vloc", i_)], stream=st_loc[dq])

        S.reset("ps")
        S.reset("BA")
        S.reset("BO")
        S.reset("SC")
        NK = NR * T
        KSB = BA[:, 0:NK]
        VSB = BA[:, NK:NK + NKT * P].rearrange("p (t f) -> p t f", f=P)
        PT = SC[:, 0:1024].bitcast(BF16)
        REC = SC[:, 1024:1536]
        kall_v = kall_d.rearrange("(r h p) t -> h p r t", h=10, p=P)
        vall_v = vall_d.rearrange("(r h p) (j f) -> h p r j f", h=10, p=P, f=P)
        vcall_v = vcall_d.rearrange("(r h i) f -> h r i f", h=10, i=TC)

        def load_kv(h):
            dma("sp", st_kvl, KSB[:, 0:NR * TL].rearrange("p (r t) -> p r t", r=NR), kall_v[h, :, :, 0:TL], r=["kall_d"], w=[("BA", "k")])
            dma("sp", st_kvl, KSB[:, NR * TL:NK].rearrange("p (r t) -> p r t", r=NR), kall_v[h, :, :, TL:T], r=["kall_d"], w=[("BA", "k")])
            dma("sp", st_kvl, VSB[:, 0:NR * JT, :].rearrange("p (r j) f -> p r j f", r=NR), vall_v[h], r=["vall_d"], w=[("BA", "v")])
            for r_ in range(NR):
                dma("sp", st_kvl, VSB[(r_ % 4) * TC:(r_ % 4 + 1) * TC, NR * JT + r_ // 4, :], vcall_v[h, r_], r=["vcall_d"], w=[("BA", "v")])

        def attn_pass(qfn, qkey, prange, scale, ktiles, kfn, vfn, groups, bias_fn, out_fn):
            p0, p1 = prange
            ng = len(groups)
            sb_ = lambda g, buf: ps[g * 2 + buf]
            accb = lambda g: ps[4 + g]
            denb = lambda g: ps[6 + g]
            nkt = len(ktiles)

            def qk(i):
                kt = ktiles[i]
                for g, (qs, qn) in enumerate(groups):
                    has_bias = bias_fn is not None and bias_fn(kt, g, None) is not None
                    pe(lambda e, g=g, qs=qs, qn=qn, kt=kt, i=i, has_bias=has_bias: e.matmul(
                        sb_(g, i % 2)[:, 0:qn], kfn(kt)[p0:p1, :], qfn(qs, qn)[p0:p1, :], start=True, stop=not has_bias),
                       r=[("BA", "k"), qkey], w=[("ps", g * 2 + i % 2)])
                    if has_bias:
                        for (lh, rh, keys, last) in bias_fn(kt, g, (qs, qn)):
                            pe(lambda e, g=g, qn=qn, i=i, lh=lh, rh=rh, last=last: e.matmul(
                                sb_(g, i % 2)[:, 0:qn], lh, rh, start=False, stop=last), r=keys, w=[("ps", g * 2 + i % 2)])
                    act(lambda e, g=g, qn=qn, i=i: e.activation(out=PT[:, (g * 2 + i % 2) * 512:(g * 2 + i % 2) * 512 + qn],
                                                               in_=sb_(g, i % 2)[:, 0:qn], func=AF.Exp, scale=float(scale)),
                        r=[("ps", g * 2 + i % 2)], w=[("SC", "pt", g, i % 2)])

            def pv(i):
                kt = ktiles[i]
                for g, (qs, qn) in enumerate(groups):
                    pt = PT[:, (g * 2 + i % 2) * 512:(g * 2 + i % 2) * 512 + qn]
                    pe(lambda e, g=g, qn=qn, kt=kt, i=i, pt=pt: e.matmul(accb(g)[:, 0:qn], vfn(kt), pt, start=(i == 0), stop=(i == nkt - 1)),
                       r=[("BA", "v"), ("SC", "pt", g, i % 2)], w=[("ps", 4 + g)])
                    pe(lambda e, g=g, qn=qn, i=i, pt=pt: e.matmul(denb(g)[:, 0:qn], ones, pt, start=(i == 0), stop=(i == nkt - 1)),
                       r=[("CON", 1), ("SC", "pt", g, i % 2)], w=[("ps", 6 + g)])

            qk(0)
            for i in range(nkt):
                if i + 1 < nkt:
                    qk(i + 1)
                pv(i)
            for g, (qs, qn) in enumerate(groups):
                dve(lambda e, g=g, qn=qn: e.reciprocal(out=REC[:, 0:qn], in_=denb(g)[:, 0:qn]), r=[("ps", 6 + g)], w=[("SC", "rec")])
                out_fn(g, qs, qn, accb(g)[:, 0:qn], REC[:, 0:qn], ("ps", 4 + g))

        lat_groups = [(s, n) for (s, n) in cfg.LS]
        lat_sets = [lat_groups[i:i + 2] for i in range(0, len(lat_groups), 2)]
        all_kt = list(range(NKT))
        ctx_kt = list(range(NR * JT, NKT))
        kfn_std = lambda kt: KSB[:, kt * P:(kt + 1) * P]
        vfn_std = lambda kt: VSB[:, kt, :]

        def out_plain(och):
            def f(g, qs, qn, acc, rec, acck):
                dve(lambda e: e.tensor_tensor(out=BOc(och, qs, qn), in0=acc, in1=rec, op=ALU.mult),
                    r=[acck, ("SC", "rec")], w=[("BO", och)])
            return f

        def out_diff(och, comp):
            def f(g, qs, qn, acc, rec, acck):
                if comp == 0:
                    dve(lambda e: e.tensor_tensor(out=BOc(och, qs, qn), in0=acc, in1=rec, op=ALU.mult),
                        r=[acck, ("SC", "rec")], w=[("BO", och)])
                else:
                    dve(lambda e: e.tensor_tensor(out=REC[:, 0:qn], in0=acc, in1=rec, op=ALU.mult),
                        r=[acck, ("SC", "rec")], w=[("SC", "rec")])
                    dve(lambda e: e.scalar_tensor_tensor(out=BOc(och, qs, qn), in0=REC[:, 0:qn], scalar=SML[:, 7:8], in1=BOc(och, qs, qn),
                                                         op0=ALU.mult, op1=ALU.add), r=[("SC", "rec"), ("BO", och), ("SML", 7)], w=[("BO", och)])
            return f

        sc_a = 128.0 ** -0.5
        for kvh in range(2):
            load_kv(kvh)
            for qh in range(kvh * 4, kvh * 4 + 4):
                qfn = lambda s, n, qh=qh: BQc(qh, s, n)
                for gset in lat_sets:
                    attn_pass(qfn, ("BQ", qh), (0, P), sc_a, all_kt, kfn_std, vfn_std, gset, None, out_plain(qh))
                attn_pass(qfn, ("BQ", qh), (0, P), sc_a, ctx_kt, kfn_std, vfn_std, [(TL, TC)], None, out_plain(qh))
        sc_c = 64.0 ** -0.5
        for h in range(4):
            load_kv(6 + h)
            qfn = lambda s, n, h=h: BQc(12 + h, s, n)
            for comp in range(2):
                pr = (comp * 64, comp * 64 + 64)
                for gset in lat_sets:
                    attn_pass(qfn, ("BQ", 12 + h), pr, sc_c, all_kt, kfn_std, vfn_std, gset, None, out_diff(12 + h, comp))
                attn_pass(qfn, ("BQ", 12 + h), pr, sc_c, ctx_kt, kfn_std, vfn_std, [(TL, TC)], None, out_diff(12 + h, comp))
        S.reset("BA")
        NLK = (RPC + 8) * 64
        KB = BA[:, 0:NLK + CTX]
        ob = NLK + CTX
        VB = BA[:, ob: ob + (NTB + 2) * P].rearrange("p (t f) -> p t f", f=P)
        ob += (NTB + 2) * P
        GT = BA[:, ob: ob + E * 64].rearrange("p (e q) -> p e q", q=64)
        ob += E * 64
        RMQ = BA[0:RPC, ob: ob + TL]
        ob += TL
        LMK = BA[0:RPC, ob: ob + NTB * P]
        ob += NTB * P
        assert ob <= KC * T4
        dma("sp", st_kvl, RMQ, rmq_d, w=[("BA", "rmq")])
        dma("sp", st_kvl, LMK, lmk_d, w=[("BA", "lmk")])
        kall_r = kall_d.rearrange("(r h p) t -> r h p t", h=10, p=P)
        vall_r = vall_d.rearrange("(r h p) (j f) -> r h p j f", h=10, p=P, f=P)
        for h in range(4):
            hh = 2 + h

            def kld(dst, which, c0, c1, hh=hh):
                i_ = which + 1
                dma("sp", st_kvl, dst, kloc_d[(i_ * 10 + hh) * P:(i_ * 10 + hh + 1) * P, c0:c1], r=[("kloc", i_)], w=[("BA", "k")])

            def vld(dst, which, j0, j1, hh=hh):
                i_ = which + 1
                dma("sp", st_kvl, dst.rearrange("p j f -> p (j f)"), vloc_d[(i_ * 10 + hh) * P:(i_ * 10 + hh + 1) * P, j0 * P:j1 * P],
                    r=[("vloc", i_)], w=[("BA", "v")])
            kld(KB[:, 0:256], -1, TL - 256, TL)
            kld(KB[:, 256:256 + TL], 0, 0, TL)
            kld(KB[:, 256 + TL:NLK], 1, 0, 256)
            dma("sp", st_kvl, KB[:, NLK:NLK + CTX].rearrange("p (r t) -> p r t", r=NR), kall_v[hh, :, :, TL:T], r=["kall_d"], w=[("BA", "k")])
            vld(VB[:, 0:2, :], -1, JT - 2, JT)
            vld(VB[:, 2:2 + JT, :], 0, 0, JT)
            vld(VB[:, 2 + JT:NTB, :], 1, 0, 2)
            for r_ in range(NR):
                dma("sp", st_kvl, VB[(r_ % 4) * TC:(r_ % 4 + 1) * TC, NTB + r_ // 4, :], vcall_v[hh, r_], r=["vcall_d"], w=[("BA", "v")])
            ttrb_v = ttrb_d.rearrange("(l h j) (k q) -> l h j k q", l=L, h=4, q=64)
            for jr in range(2):
                dma("sp", st_gt, GT[jr * 64:(jr + 1) * 64, :, :], ttrb_v[l, h, :, 1 - jr:1 - jr + E, :],
                    r=["ttrb_d"], w=[("BA", "gt")])
            qfn = lambda s, n, h=h: BQc(8 + h, s, n)
            kfn_b = lambda kt: KB[:, kt * P:(kt + 1) * P]
            vfn_b = lambda kt: VB[:, kt, :]
            for gset in lat_sets:
                r0 = gset[0][0] // 64
                r1 = (gset[-1][0] + gset[-1][1]) // 64
                kts = list(range(r0 // 2, (r1 - 1 + 8) // 2 + 1)) + [NTB, NTB + 1]

                def bias_fn(kt, g, q, gset=gset):
                    if kt >= NTB:
                        return None
                    if q is None:
                        return True
                    qs, qn = q
                    i0 = qs // 64
                    e0 = i0 + 4 - 2 * kt - cfg.EMIN
                    nrow = qn // 64
                    return [(LMK[:, kt * P:(kt + 1) * P], RMQ[:, qs:qs + qn], [("BA", "rmq"), ("BA", "lmk")], False),
                            (ident, GT[:, e0:e0 + nrow, :].rearrange("p e q -> p (e q)"), [("BA", "gt"), ("CON", 0)], True)]
                attn_pass(qfn, ("BQ", 8 + h), (0, P), 1.0, kts, kfn_b, vfn_b, gset, bias_fn, out_plain(8 + h))
            attn_pass(qfn, ("BQ", 8 + h), (0, P), 1.0, [NTB, NTB + 1], kfn_b, vfn_b, [(TL, TC)], None, out_plain(8 + h))

        if getattr(cfg, "dbg", None) == "attn":
            for ch in range(KC):
                dma("pool", S.dstream(f"dbg{ch}"), outT[:, ch, :], BOc(ch, 0, TL), r=[("BO", ch)], w=[("outT", ch)])
            break
        S.reset("ps")
        S.reset("BA")
        S.reset("SC")
        BAF = BA[:].bitcast(F32)
        SQ3 = BA[:, 0:T]
        RS3 = BAF[:, T:2 * T]
        for h in range(4):
            och = 12 + h
            act(lambda e, och=och: e.activation(out=SQ3, in_=BOc(och, 0, T), func=AF.Square), r=[("BO", och)], w=[("BA", "sq3")])
            t4 = pst([4, 5], 4)
            for (pap, pk, s, n, kd) in t4:
                pe(lambda e, pap=pap, s=s, n=n: e.matmul(pap, ones, SQ3[:, s:s + n], start=True, stop=True), r=[("BA", "sq3"), ("CON", 1)], w=[pk])
                act(lambda e, pap=pap, s=s, n=n: e.activation(out=RS3[:, s:s + n], in_=pap, func=AF.Sqrt, bias=EPSB[:, 1:2], scale=1.0),
                    r=[pk], w=[("BA", "rs3")])
                dve(lambda e, s=s, n=n: e.reciprocal(out=RS3[:, s:s + n], in_=RS3[:, s:s + n]), r=[("BA", "rs3")], w=[("BA", "rs3")])
            dve(lambda e, och=och: e.scalar_tensor_tensor(out=BOc(och, 0, T), in0=BOc(och, 0, T), scalar=SML[:, 6:7], in1=RS3[:, 0:T],
                                                          op0=ALU.mult, op1=ALU.mult), r=[("BO", och), ("BA", "rs3"), ("SML", 6)], w=[("BO", och)])
        gb0 = 4 * T
        GB = [BA[:, gb0 + i * 3 * T: gb0 + (i + 1) * 3 * T].rearrange("p (j t) -> p j t", j=3) for i in range(2)]
        tb0 = (gb0 + 6 * T + 1) // 2 + 1
        MT = [BAF[:, tb0 + i * T: tb0 + (i + 1) * T] for i in range(3)]
        assert 2 * (tb0 + 3 * T) <= KC * T4
        gsp_v = gsp_d.rearrange("(j c p) t -> c p j t", j=3, p=P)
        brk = [(0, 8, 0), (8, 4, 8), (12, 4, 12)]
        for dp in range(8):
            slot = load_block(w_f["w_br"][l][dp * P:(dp + 1) * P, :], 4096, extra_r=[("wf", "w_br", l)])
            for half in range(2):
                dc = dp * 2 + half
                gbi = dc % 2
                dma("sp", st_gb[gbi], GB[gbi], gsp_v[dc], r=[("gsp_d", j_ * 16 + dc) for j_ in range(3)], w=[("BA", "gb", gbi)])
                tiles = [pst([0, 1], 0), pst([2, 3], 1), pst([4, 5], 2)]
                for j, (k0, nk, oc0) in enumerate(brk):
                    linear_fm(slot, [(k0 + i, oc0 + i) for i in range(nk)], half * P, lambda ka, s, n: BOc(ka, s, n), tiles[j], lambda ka: ("BO", ka))
                for si, (s, n, kd) in enumerate(SPL):
                    for j in range(3):
                        pap, pk = tiles[j][si][0], tiles[j][si][1]
                        dve(lambda e, pap=pap, j=j, s=s, n=n, gbi=gbi: e.tensor_tensor(out=MT[j][:, s:s + n], in0=pap, in1=GB[gbi][:, j, s:s + n], op=ALU.mult),
                            r=[pk, ("BA", "gb", gbi)], w=[("BA", "mt", j)])
                    dve(lambda e, s=s, n=n: e.tensor_tensor(out=MT[0][:, s:s + n], in0=MT[0][:, s:s + n], in1=MT[1][:, s:s + n], op=ALU.add),
                         r=[("BA", "mt", 0), ("BA", "mt", 1)], w=[("BA", "mt", 0)])
                    dve(lambda e, s=s, n=n, dc=dc: e.tensor_tensor(out=BQc(dc, s, n), in0=MT[0][:, s:s + n], in1=MT[2][:, s:s + n], op=ALU.add),
                         r=[("BA", "mt", 0), ("BA", "mt", 2)], w=[("BQ", dc)])
        for dp in range(8):
            slot = load_block(w_f["w_o"][l][dp * P:(dp + 1) * P, :], 4096, extra_r=[("wf", "w_o", l)])
            for half in range(2):
                dc = dp * 2 + half
                tile = pst([0, 1] if dc % 2 == 0 else [2, 3], dc % 2)
                linear_fm(slot, [(kc, kc) for kc in range(KC)], half * P, lambda ka, s, n: BQc(ka, s, n), tile, lambda ka: ("BQ", ka))
                for (pap, pk, s, n, kd) in tile:
                    wch = 0 if kd == "x" else 1
                    dve(lambda e, pap=pap, s=s, n=n, dc=dc, wch=wch: e.scalar_tensor_tensor(
                        out=HX[:, dc, s:s + n], in0=pap, scalar=DER[:, wch, 2, dc:dc + 1], in1=HX[:, dc, s:s + n], op0=ALU.mult, op1=ALU.add),
                        r=[pk, ("HX", dc), "DER"], w=[("HX", dc)])

        S.reset("BA")
        S.reset("BO")
        S.reset("SC")

        def vx_dst(kc, s, n):
            off = 1 if s < TL else 3
            return BA[:, kc * T4 + off + s: kc * T4 + off + s + n]
        rmsnorm_modulate(l, 1, vx_dst)
        BA3 = BA[:].rearrange("p (k t) -> p k t", k=KC)
        for i, col in enumerate((1, TL, TL + 3, TL + 2 + TC)):
            dve(lambda e, i=i, col=col: e.tensor_copy(out=HST[:, :, i:i + 1], in_=BA3[:, :, col:col + 1]),
                r=[("BA", kc) for kc in range(KC)], w=["HST"])
        dma("sp", st_h, hb_d, HST[:].rearrange("p k c -> p (k c)"), r=["HST"], w=["hb_d"])
        ag(hb_d, hall_d, r=["hb_d"], w=["hall_d"])
        S.add(dq, lambda e, dq=dq: e.dma_start(out=HST2[:, 0, :], in_=hall_d[bass.ds(S.spv[dq][-1] * P, P), :]),
              ["hall_d"], [("HST2", 0)], stream=st_hq[dq])
        S.add(dq, lambda e, dq=dq: e.dma_start(out=HST2[:, 1, :], in_=hall_d[bass.ds(S.spv[dq][1] * P, P), :]),
              ["hall_d"], [("HST2", 1)], stream=st_hq[dq])
        H2 = HST2[:].rearrange("p w (k c) -> p w k c", c=4)
        for (col, wsel, c_) in ((0, 0, 1), (TL + 1, 1, 0), (TL + 2, 0, 3), (TL + 3 + TC, 1, 2)):
            dve(lambda e, col=col, wsel=wsel, c_=c_: e.tensor_scalar(out=BA3[:, :, col:col + 1], in0=H2[:, wsel, :, c_:c_ + 1],
                                                                      scalar1=HFL[:, wsel:wsel + 1], scalar2=None, op0=ALU.mult),
                r=[("HST2", wsel), "HFL"], w=[("BA", kc) for kc in range(KC)])

        FS = [(0, min(512, T4))]
        while FS[-1][0] + FS[-1][1] < T4:
            s_ = FS[-1][0] + FS[-1][1]
            FS.append((s_, min(512, T4 - s_)))
        NF = len(FS)
        assert NF <= 3
        BOF = BO[:].bitcast(F32)
        HBA = BOF[:, 0:T4]
        HBG = BOF[:, T4:2 * T4]
        AP_ = BOF[:, 2 * T4:3 * T4]
        GP_ = BOF[:, 3 * T4:4 * T4]
        SG = BOF[:, 4 * T4:5 * T4]
        TO = T + 2

        def ffn_tile(par_):
            banks = [0, 1, 2] if par_ == 0 else [3, 4, 5]
            return [(ps[banks[i]][:, 0:n], ("ps", banks[i]), s, n) for i, (s, n) in enumerate(FS)]
        pair0 = 0
        for g, gsz in enumerate(cfg.GS):
            for jj in range(gsz):
                j = pair0 + jj
                slot = load_block(w_f["w_up"][l][j * P:(j + 1) * P, :], 4096, extra_r=[("wf", "w_up", l)])
                for half, (HB, hk, OUT, ok) in enumerate(((HBA, "hba", AP_, "ap"), (HBG, "hbg", GP_, "gp"))):
                    tile = ffn_tile(half)
                    for kc in range(KC):
                        for (pap, pk, s, n) in tile:
                            pe(lambda e, pap=pap, kc=kc, s=s, n=n, slot=slot, half=half: e.matmul(
                                pap, WR[slot][:, kc * 256 + half * P: kc * 256 + half * P + P], BA[:, kc * T4 + s: kc * T4 + s + n],
                                start=(kc == 0), stop=(kc == KC - 1)), r=[("WR", slot), ("BA", kc)], w=[pk])
                    for (pap, pk, s, n) in tile:
                        act(lambda e, pap=pap, s=s, n=n, HB=HB: e.activation(out=HB[:, s:s + n], in_=pap, func=AF.Copy), r=[pk], w=[("BO", hk)])
                    ci = 2 * j + half
                    dve(lambda e, HB=HB, OUT=OUT, ci=ci: e.tensor_scalar(out=OUT[:, 0:TO], in0=HB[:, 0:TO], scalar1=CW[:, 0, ci:ci + 1],
                                                                         scalar2=CW[:, 3, ci:ci + 1], op0=ALU.mult, op1=ALU.add),
                        r=[("BO", hk), "CW"], w=[("BO", ok)])
                    for tap in (1, 2):
                        dve(lambda e, HB=HB, OUT=OUT, ci=ci, tap=tap: e.scalar_tensor_tensor(
                            out=OUT[:, 0:TO], in0=HB[:, tap:tap + TO], scalar=CW[:, tap, ci:ci + 1], in1=OUT[:, 0:TO], op0=ALU.mult, op1=ALU.add),
                            r=[("BO", hk), ("BO", ok), "CW"], w=[("BO", ok)])
                act(lambda e: e.activation(out=SG[:, 0:TO], in_=GP_[:, 0:TO], func=AF.Silu), r=[("BO", "gp")], w=[("BO", "sg")])
                dve(lambda e, jj=jj: e.tensor_tensor(out=BQ[:, jj * T4: jj * T4 + TO], in0=SG[:, 0:TO], in1=AP_[:, 0:TO], op=ALU.mult),
                     r=[("BO", "sg"), ("BO", "ap")], w=[("BQ", jj)])
            for dp in range(8):
                slot = load_block(w_f["w_down"][l][dp * P:(dp + 1) * P, pair0 * 256:(pair0 + gsz) * 256], gsz * 256,
                                  extra_r=[("wf", "w_down", l)])
                for half in range(2):
                    dc = dp * 2 + half
                    tile = pst([0, 1] if dc % 2 == 0 else [2, 3], dc % 2)

                    def actf(ka, s, n):
                        off = 0 if s < TL else 2
                        return BQ[:, ka * T4 + off + s: ka * T4 + off + s + n]
                    linear_fm(slot, [(i, i) for i in range(gsz)], half * P, actf, tile, lambda ka: ("BQ", ka))
                    for (pap, pk, s, n, kd) in tile:
                        wch = 0 if kd == "x" else 1
                        dve(lambda e, pap=pap, s=s, n=n, dc=dc, wch=wch: e.scalar_tensor_tensor(
                            out=HX[:, dc, s:s + n], in0=pap, scalar=DER[:, wch, 5, dc:dc + 1], in1=HX[:, dc, s:s + n], op0=ALU.mult, op1=ALU.add),
                            r=[pk, ("HX", dc), "DER"], w=[("HX", dc)])
            pair0 += gsz
        S.reset("BQ")
        S.reset("BO")

    if getattr(cfg, "dbg", None) is None:
        dma("sp", st_out, outT, HX[:, :, 0:TL], r=[("HX", kc) for kc in range(KC)], w=["outT"])
        S.add("sp", None, ["outT"], [])
    else:
        S.add("sp", None, [("outT", ch) for ch in range(KC)], [])
    S.emit()
    stack.close()
    return nc


_CACHE = {}


def run(cfg, inputs):
    maps = prep_inputs(cfg, inputs)
    key = (cfg.S, cfg.DFF, cfg.L)
    if key not in _CACHE:
        _CACHE[key] = build(cfg)
    nc = _CACHE[key]
    res = run_bass_kernel_spmd(nc, maps, core_ids=list(range(NR)))
    outs = []
    for c in range(NR):
        o = np.asarray(res.results[c]["outT"], np.float32)
        outs.append(o.transpose(2, 1, 0).reshape(cfg.TL, D))
    return np.concatenate(outs, 0)[None].astype(np.float32)


def kernel(**inputs):
    cfg = Cfg()
    return run(cfg, inputs)
```

```python
import math
from contextlib import ExitStack
import numpy as np
import ml_dtypes
import concourse.bass as bass
import concourse.mybir as mybir
from concourse.bass_utils import run_bass_kernel_spmd

F32 = mybir.dt.float32
BF16 = mybir.dt.bfloat16
AF = mybir.ActivationFunctionType
ALU = mybir.AluOpType
NPBF = ml_dtypes.bfloat16

NR = 8
P = 128
D = 2048
KC = 16
GRID_W = 64
CTX = 256
TC = CTX // NR
HD = 128
NEG = -30000.0
EPS = 1e-6
LIM = 30000


class Cfg:
    def __init__(self, S=8192, DFF=5504, L=4):
        self.S, self.DFF, self.L = S, DFF, L
        self.TL = S // NR
        self.RPC = self.TL // GRID_W
        self.ROWS = S // GRID_W
        self.T = self.TL + TC
        self.T4 = self.T + 4
        self.FC = DFF // P
        self.JT = self.TL // P
        self.NKT = NR * self.JT + CTX // P
        self.NTB = (self.RPC + 8) // 2
        self.E = 2 * self.RPC + 6
        self.EMIN = -self.RPC - 2
        self.DIN = 4608 + 3 * D
        self.NB_IN = self.DIN // 256
        self.LS = [(s, min(512, self.TL - s)) for s in range(0, self.TL, 512)]
        gs = []
        left = self.FC
        ng = (self.FC + 15) // 16
        for g in range(ng):
            n = (left + (ng - g) - 1) // (ng - g)
            gs.append(n)
            left -= n
        self.GS = gs


class Stream:
    def __init__(self, name, kind, serial=False):
        self.name, self.kind, self.count, self.sems, self.serial = name, kind, 0, {}, serial


class Op:
    __slots__ = ("q", "fn", "stream", "n", "needed", "edeps", "ddeps", "idx")


class Sched:
    def __init__(self, nc, stack):
        self.nc, self.stack = nc, stack
        self.queues = {q: [] for q in ("pe", "act", "dve", "pool", "sp")}
        self.es = {q: Stream(q, "eng") for q in ("pe", "act", "dve", "pool", "sp")}
        self.res = {}
        self.base = {}
        self.nsem = 0

    def sem(self, stream, e):
        if e not in stream.sems:
            self.nsem += 1
            stream.sems[e] = self.stack.enter_context(self.nc.semaphore(f"s_{stream.name}_{e}"))
        return stream.sems[e]

    def dstream(self, name, kind="dma", serial=False):
        return Stream(name, kind, serial)

    def _state(self, key):
        st = self.res.get(key)
        if st is None:
            b = self.base.get(key[0] if isinstance(key, tuple) else key)
            st = [dict(b) if b else {}, {}]
            self.res[key] = st
        return st

    def reset(self, buf):
        b = dict(self.base.get(buf, {}))
        for key in [k for k in self.res if (k[0] if isinstance(k, tuple) else k) == buf]:
            st = self.res.pop(key)
            for d in (st[0], st[1]):
                for s, o in d.items():
                    if s not in b or b[s].idx < o.idx:
                        b[s] = o
        self.base[buf] = b

    def add(self, q, fn, r=(), w=(), stream=None):
        op = Op()
        op.q, op.fn = q, fn
        op.stream = stream if stream is not None else self.es[q]
        op.needed = False
        op.n = None
        op.idx = len(self.queues[q]) + 1
        ed, dd = {}, {}

        def need(o):
            s = o.stream
            if s.kind == "eng":
                if s is op.stream and q == "pe":
                    return
                if s not in ed or ed[s].idx < o.idx:
                    ed[s] = o
                o.needed = True
            elif s.kind == "cc":
                dd[s] = max(dd.get(s, 0), o.n)
            else:
                dd[s] = s.count

        for k in r:
            st = self._state(k)
            for o in st[0].values():
                need(o)
        for k in w:
            st = self._state(k)
            for o in st[0].values():
                need(o)
            for o in st[1].values():
                need(o)
        if op.stream.kind != "eng":
            if op.stream.serial and op.stream.count > 0:
                dd[op.stream] = op.stream.count
            op.stream.count += 1
            op.n = op.stream.count
            op.idx = op.n
        for k in r:
            self._state(k)[1][op.stream] = op
        for k in w:
            self.res[k] = [{op.stream: op}, {}]
        op.edeps, op.ddeps = ed, dd
        self.queues[q].append(op)
        return op

    def _waits(self, stream, n):
        mult = 16 if stream.kind == "dma" else 1
        e, v = (n - 1) // LIM, (n - 1) % LIM + 1
        out = []
        if stream.kind != "eng":
            for ee in range(e):
                out.append((self.sem(stream, ee), LIM * mult))
        out.append((self.sem(stream, e), v * mult))
        return out

    def _emit_q(self, q, eng):
        waited = {}
        if q in ("sp", "pool"):
            pid = eng.partition_id()
            if not hasattr(self, "spv"):
                self.spv = {}
            self.spv[q] = {
                -1: eng.snap(((pid + (NR - 1)) % NR), min_val=0, max_val=NR - 1),
                0: eng.snap(pid + 0, min_val=0, max_val=NR - 1),
                1: eng.snap(((pid + 1) % NR), min_val=0, max_val=NR - 1),
            }
        for op in self.queues[q]:
            ws = []
            for s, o in op.edeps.items():
                ws += self._waits(s, o.n)
            for s, n in op.ddeps.items():
                if n > 0:
                    ws += self._waits(s, n)
            for sem, val in ws:
                key = id(sem)
                if waited.get(key, 0) < val:
                    eng.wait_ge(sem, val)
                    waited[key] = val
            if op.fn is None:
                continue
            ins = op.fn(eng)
            s = op.stream
            if s.kind == "dma":
                ins.then_inc(self.sem(s, (op.n - 1) // LIM), 16)
            elif s.kind == "cc":
                ins.then_inc(self.sem(s, (op.n - 1) // LIM))
            elif op.needed:
                ins.then_inc(self.sem(s, (op.n - 1) // LIM), 1)

    def emit(self):
        for q, ops in self.queues.items():
            cnt = 0
            for op in ops:
                if op.stream.kind == "eng" and op.needed:
                    cnt += 1
                    op.n = cnt
        for q, ops in self.queues.items():
            for op in ops:
                for s, o in op.edeps.items():
                    self._waits(s, o.n)
                for s, n in op.ddeps.items():
                    if n > 0:
                        self._waits(s, n)
                if op.fn is not None and (op.stream.kind != "eng" or op.needed):
                    self.sem(op.stream, (op.n - 1) // LIM)
        nc = self.nc
        with nc.Block() as block:
            @block.tensor
            def _(e):
                self._emit_q("pe", e)

            @block.scalar
            def _(e):
                self._emit_q("act", e)

            @block.vector
            def _(e):
                self._emit_q("dve", e)

            @block.gpsimd
            def _(e):
                self._emit_q("pool", e)

            @block.sync
            def _(e):
                self._emit_q("sp", e)


def fm(v):
    v = np.asarray(v, np.float32)
    n = v.shape[-1] // P
    v = v.reshape(v.shape[:-1] + (n, P))
    return np.ascontiguousarray(np.moveaxis(v, -1, 0))


def tile_w(w, cols=256):
    K, N = w.shape
    kc, nb = K // P, N // cols
    return np.ascontiguousarray(w.reshape(kc, P, nb, cols).transpose(2, 1, 0, 3)).reshape(nb * P, kc * cols)


def rope_tables(cfg, core, dim):
    nf = dim // 4
    inv = (10000.0 ** (-np.arange(nf, dtype=np.float32) / np.float32(nf))).astype(np.float32)
    t = core * cfg.TL + np.arange(cfg.TL)
    pos = np.stack([t // GRID_W, t % GRID_W], -1).astype(np.float32)
    cos = np.ones((P, cfg.T), np.float32)
    sin = np.zeros((P, cfg.T), np.float32)
    for p in range(P):
        d = p % dim
        axis = d // (dim // 2)
        f = d % nf
        ang = (pos[:, axis] * inv[f]).astype(np.float32)
        cos[p, :cfg.TL] = np.cos(ang)
        sin[p, :cfg.TL] = np.sin(ang)
    return cos, sin


def rot_matrix(dim):
    q = dim // 4
    R = np.zeros((P, P), np.float32)
    for i in range(P):
        d = i % dim
        half = (d % (dim // 2)) // q
        if half == 0:
            R[i + q, i] = -1.0
        else:
            R[i - q, i] = 1.0
    return R


def prep_inputs(cfg, inp):
    L, TL, T = cfg.L, cfg.TL, cfg.T
    f32 = np.float32
    x = np.asarray(inp["x"], f32)[0]
    ctx = np.asarray(inp["ctx"], f32)[0]
    shared = {}
    cvec = np.stack([fm(np.asarray(inp["c"], f32)[0]), fm(np.asarray(inp["c_ctx"], f32))], -1)
    shared["cvec"] = np.ascontiguousarray(cvec)
    shared["bmod"] = np.ascontiguousarray(fm(np.asarray(inp["b_mod"], f32)))
    def dup64(v):
        v = np.asarray(v, f32)
        return np.concatenate([v, v], -1)
    gcols = [fm(inp["g_norm1"]), fm(inp["g_norm2"])]
    singles = [np.asarray(inp["gq_a"], f32), np.asarray(inp["gk_a"], f32), np.asarray(inp["gq_b"], f32),
               np.asarray(inp["gk_b"], f32), dup64(inp["gq_c"]), dup64(inp["gk_c"]),
               np.asarray(inp["g_subln_c"], f32)]
    gs = np.stack([s.T for s in singles], -1)
    shared["gvec"] = np.ascontiguousarray(np.concatenate(gcols + [gs], -1))
    lam = np.zeros((P, L, 4), f32)
    for i, k in enumerate(["lam_q1", "lam_k1", "lam_q2", "lam_k2"]):
        lam[:64, :, i] = np.asarray(inp[k], f32).T
    shared["lamv"] = lam
    FC, DFF = cfg.FC, cfg.DFF
    perm = np.concatenate([np.concatenate([np.arange(j * P, (j + 1) * P), DFF + np.arange(j * P, (j + 1) * P)])
                           for j in range(FC)])
    cw = np.asarray(inp["conv_w"], f32)[:, :, perm]
    cb = np.asarray(inp["conv_b"], f32)[:, perm]
    shared["convw"] = np.ascontiguousarray(fm(cw))
    shared["convb"] = np.ascontiguousarray(fm(cb))
    shared["ident"] = np.eye(P, dtype=f32).astype(NPBF)
    shared["ones"] = np.ones((P, P), f32).astype(NPBF)
    oc = np.zeros((P, P), f32)
    oc[:64, :64] = 1
    oc[64:, 64:] = 1
    shared["onesc"] = oc.astype(NPBF)
    shared["rota"] = rot_matrix(128).astype(NPBF)
    shared["rotc"] = rot_matrix(64).astype(NPBF)
    rm = np.zeros((cfg.RPC, TL), f32)
    for i in range(cfg.RPC):
        rm[i, i * 64:(i + 1) * 64] = 1
    shared["rmq"] = rm.astype(NPBF)
    rpb = np.asarray(inp["rpb_b"], f32)
    K2 = cfg.E + 1
    dr_top = 1 - cfg.EMIN
    jc = np.arange(64)[:, None]
    qc = np.arange(64)[None, :]
    cs = np.clip(qc - 8, 0, 64 - 16)
    colok = (jc >= cs) & (jc < cs + 16)
    dcidx = np.clip(jc - qc + 15, 0, 30)
    ttr = np.full((L, 4, K2, 64, 64), NEG, f32)
    for kk in range(K2):
        dr = dr_top - kk
        if -7 <= dr <= 7:
            blk = rpb[:, :, dr + 7, :][:, :, dcidx]
            ttr[:, :, kk] = np.where(colok[None, None], blk, f32(NEG))
    shared["ttr"] = np.ascontiguousarray(ttr.transpose(0, 1, 3, 2, 4))
    w_in = np.asarray(inp["w_in"], f32)
    w_o = np.asarray(inp["w_o"], f32)
    w_up = np.asarray(inp["w_up"], f32)[:, :, perm]
    w_down = np.asarray(inp["w_down"], f32)
    wbr = [np.asarray(inp[k], f32) for k in ("w_br_a", "w_br_b", "w_br_c")]
    tw = {"w_in": [], "w_o": [], "w_up": [], "w_br": [], "w_down": []}
    for l in range(L):
        tw["w_in"].append(tile_w(w_in[l]))
        tw["w_o"].append(tile_w(w_o[l]))
        tw["w_up"].append(tile_w(w_up[l]))
        parts = [w.reshape(-1, P, 8, 256).transpose(2, 1, 0, 3) for w in (wbr[0][l], wbr[1][l], wbr[2][l])]
        tw["w_br"].append(np.ascontiguousarray(np.concatenate(parts, 2)).reshape(8 * P, 16 * 256))
        tw["w_down"].append(np.ascontiguousarray(
            w_down[l].reshape(FC, P, 8, 256).transpose(2, 1, 0, 3)).reshape(8 * P, FC * 256))
    w_mod = np.asarray(inp["w_mod"], f32)
    maps = []
    for c in range(NR):
        m = dict(shared)
        m["xT"] = np.ascontiguousarray(x[c * TL:(c + 1) * TL].reshape(TL, KC, P).transpose(2, 1, 0))
        m["cT"] = np.ascontiguousarray(ctx[c * TC:(c + 1) * TC].reshape(TC, KC, P).transpose(2, 1, 0))
        m["wmod"] = np.ascontiguousarray(w_mod[:, :, c * 1536:(c + 1) * 1536])
        ca, sa = rope_tables(cfg, c, 128)
        cc_, sc_ = rope_tables(cfg, c, 64)
        m["ropet"] = np.ascontiguousarray(np.stack([ca, sa, cc_, sc_], 1))
        lm = np.full((cfg.RPC, cfg.NTB * P), NEG, f32)
        for i in range(cfg.RPC):
            r = c * cfg.RPC + i
            rs = min(max(r - 4, 0), cfg.ROWS - 8)
            for ar in range(rs, rs + 8):
                rel = ar - (c * cfg.RPC - 4)
                lm[i, rel * 64:(rel + 1) * 64] = 0.0
        m["lmk"] = lm.astype(NPBF)
        fl = np.ones((P, 2), f32)
        if c == 0:
            fl[:, 0] = 0
        if c == NR - 1:
            fl[:, 1] = 0
        m["hflag"] = fl
        for k, lst in tw.items():
            rows = lst[0].shape[0] // NR
            m[k + "_s"] = np.ascontiguousarray(np.stack([a[c * rows:(c + 1) * rows] for a in lst], 0))
        maps.append(m)
    return maps


def build(cfg):
    L, TL, T, T4, FC, JT = cfg.L, cfg.TL, cfg.T, cfg.T4, cfg.FC, cfg.JT
    RPC, NTB, E, NKT = cfg.RPC, cfg.NTB, cfg.E, cfg.NKT
    nc = bass.Bass("TRN2", target_bir_lowering=False)
    stack = ExitStack()

    def din(name, shape, dt=F32):
        return nc.dram_tensor(name, list(shape), dt, kind="ExternalInput").ap()

    def dint(name, shape, dt=BF16):
        return nc.dram_tensor(name, list(shape), dt, kind="Internal").ap()

    xT = din("xT", [P, KC, TL])
    cT = din("cT", [P, KC, TC])
    cvec = din("cvec", [P, KC, 2])
    bmod = din("bmod", [P, L, 96])
    gvec = din("gvec", [P, L, 39])
    lamv = din("lamv", [P, L, 4])
    convw = din("convw", [P, L, 3, 2 * FC])
    convb = din("convb", [P, L, 2 * FC])
    ident_d = din("ident", [P, P], BF16)
    ones_d = din("ones", [P, P], BF16)
    onesc_d = din("onesc", [P, P], BF16)
    rota_d = din("rota", [P, P], BF16)
    rotc_d = din("rotc", [P, P], BF16)
    rmq_d = din("rmq", [RPC, TL], BF16)
    ttr = din("ttr", [L, 4, 64, E + 1, 64])
    wmod = din("wmod", [L, D, 1536])
    ropet = din("ropet", [P, 4, T])
    lmk_d = din("lmk", [RPC, NTB * P], BF16)
    hflag_d = din("hflag", [P, 2])
    wshapes = {"w_in": (cfg.NB_IN * P, KC * 256), "w_o": (8 * P, KC * 256), "w_up": (FC * P, KC * 256),
               "w_br": (8 * P, 16 * 256), "w_down": (8 * P, FC * 256)}
    w_s = {k: din(k + "_s", [L, v[0] // NR, v[1]]) for k, v in wshapes.items()}
    outT = nc.dram_tensor("outT", [P, KC, TL], F32, kind="ExternalOutput").ap()

    w_b = {k: [dint(f"{k}_b{l}", [v[0] // NR, v[1]]) for l in range(L)] for k, v in wshapes.items()}
    w_f = {k: [dint(f"{k}_f{l}", [v[0], v[1]]) for l in range(L)] for k, v in wshapes.items()}
    modp_d = dint("modp", [P, L * 24], F32)
    moda_d = dint("moda", [NR * P, L * 24], F32)
    kb_d = dint("kb", [10 * P, T])
    kall_d = dint("kall", [NR * 10 * P, T])
    vb_d = dint("vb", [10 * P, JT * P])
    vall_d = dint("vall", [NR * 10 * P, JT * P])
    vcb_d = dint("vcb", [10 * TC, P])
    vcall_d = dint("vcall", [NR * 10 * TC, P])
    gsp_d = dint("gsp", [48 * P, T])
    hb_d = dint("hb", [P, 64])
    hall_d = dint("hall", [NR * P, 64])

    sb = lambda name, shape, dt: stack.enter_context(nc.sbuf_tensor(name, list(shape), dt))
    HX = sb("HX", [P, KC, T], F32)
    BA = sb("BA", [P, KC * T4], BF16)
    BQ = sb("BQ", [P, KC * T4], BF16)
    BO = sb("BO", [P, KC * T4], BF16)
    WR = [sb(f"WR{i}", [P, 4096], BF16) for i in range(2)]
    SCQ = 2048
    SC = sb("SC", [P, SCQ], F32)
    CON = sb("CON", [P, 5 * P], BF16)
    MOD = sb("MOD", [P, 2, L, 96], F32)
    GV = sb("GV", [P, L, 39], F32)
    LAMV = sb("LAMV", [P, L, 4], F32)
    SML = sb("SML", [P, 64], F32)
    CVS = sb("CVS", [P, KC, 2], F32)
    DER = sb("DER", [P, 2, 6, KC], F32)
    CW = sb("CW", [P, 4, 2 * FC], F32)
    HFL = sb("HFL", [P, 2], F32)
    EPSB = sb("EPSB", [P, 4], F32)
    HST = sb("HST", [P, KC, 4], BF16)
    HST2 = sb("HST2", [P, 2, 64], BF16)
    ps = [stack.enter_context(nc.psum_tensor(f"ps{i}", [P, 512], F32)) for i in range(8)]

    ident, ones, onesc, rota, rotc = (CON[:, i * P:(i + 1) * P] for i in range(5))
    S = Sched(nc, stack)
    st_ld = [S.dstream(f"wr{i}") for i in range(2)]
    st_misc = S.dstream("misc", serial=True)
    st_cast = {k: S.dstream("cast_" + k) for k in ("w_in", "w_br", "w_o", "w_up", "w_down")}
    st_cc = S.dstream("cc", "cc")
    st_kst = [S.dstream(f"kst{i}") for i in range(2)]
    st_vst = [S.dstream(f"vst{i}") for i in range(2)]
    st_gst = [S.dstream(f"gst{i}") for i in range(2)]
    st_kvl = S.dstream("kvld")
    st_gb = [S.dstream(f"gb{i}") for i in range(2)]
    st_gt = S.dstream("gt")
    st_h = S.dstream("halo", serial=True)
    st_out = S.dstream("out")
    ttrb_d = dint("ttrb", [L * 4 * 64, (E + 1) * 64])
    kloc_d = dint("kloc", [3 * 10 * P, T])
    vloc_d = dint("vloc", [3 * 10 * P, JT * P])
    st_loc = {"sp": S.dstream("loc_sp"), "pool": S.dstream("loc_pool")}
    st_hq = {"sp": st_h, "pool": S.dstream("halo_pool", serial=True)}

    def pe(fn, r=(), w=()):
        return S.add("pe", fn, r, w)

    def act(fn, r=(), w=()):
        return S.add("act", fn, r, w)

    def dve(fn, r=(), w=()):
        return S.add("dve", fn, r, w)

    def pool(fn, r=(), w=()):
        return S.add("pool", fn, r, w)

    def dma(q, stream, out, in_, r=(), w=()):
        return S.add(q, lambda e: e.dma_start(out=out, in_=in_), r, w, stream=stream)

    def ag(in_ap, out_ap, r=(), w=()):
        return S.add("pool", lambda e: e.collective_compute(
            "AllGather", ALU.bypass, replica_groups=[list(range(NR))],
            ins=[in_ap.opt()], outs=[out_ap.opt()]), r, w, stream=st_cc)

    SPL = [(s, n, "x") for (s, n) in cfg.LS] + [(TL, TC, "c")]
    NLS = len(cfg.LS)

    def BAu(kc, s, n):
        return BA[:, kc * T4 + s: kc * T4 + s + n]

    def BQc(ch, s, n):
        return BQ[:, ch * T4 + s: ch * T4 + s + n]

    def BOc(ch, s, n):
        return BO[:, ch * T4 + s: ch * T4 + s + n]

    for i_, v_ in enumerate((EPS * D, EPS * 128.0, EPS * 64.0)):
        S.add("pool", lambda e, i_=i_, v_=v_: e.memset(EPSB[:, i_:i_ + 1], float(v_)), [], [("EPSB", i_)])
    for i, src in enumerate((ident_d, ones_d, onesc_d, rota_d, rotc_d)):
        dma("sp", st_misc, CON[:, i * P:(i + 1) * P], src, w=[("CON", i)])
    dma("sp", st_misc, GV[:], gvec, w=["GV"])
    dma("sp", st_misc, LAMV[:], lamv, w=["LAMV"])
    dma("sp", st_misc, CVS[:], cvec, w=["CVS"])
    dma("sp", st_misc, HFL[:], hflag_d, w=["HFL"])
    dma("sp", st_misc, HX[:, :, 0:TL], xT, w=[("HX", kc) for kc in range(KC)])
    dma("sp", st_misc, HX[:, :, TL:T], cT, w=[("HX", kc) for kc in range(KC)])

    worder = ["w_in", "w_br", "w_o", "w_up", "w_down"]

    def issue_casts(l):
        for k in worder:
            dma("pool", st_cast[k], w_b[k][l], w_s[k][l], w=[("wb", k, l)])

    def issue_wag(l):
        for k in worder:
            ag(w_b[k][l], w_f[k][l], r=[("wb", k, l)], w=[("wf", k, l)])

    issue_casts(0)
    dma("pool", S.dstream("ttrc"), ttrb_d, ttr.rearrange("l h j k q -> (l h j) (k q)"), w=["ttrb_d"])

    SIL = SC[:, 0:KC * 2]
    act(lambda e: e.activation(out=SIL, in_=CVS[:].rearrange("p k w -> p (k w)"), func=AF.Sigmoid), r=["CVS"], w=[("SC", "sil")])
    dve(lambda e: e.tensor_tensor(out=SIL, in0=SIL, in1=CVS[:].rearrange("p k w -> p (k w)"), op=ALU.mult),
        r=[("SC", "sil"), "CVS"], w=[("SC", "sil")])
    MODP = SC[:, 64:64 + L * 24]
    WMv = [w.bitcast(F32) for w in WR]
    mm_list = [(l, j) for l in range(L) for j in range(12)]

    def mm_issue(i):
        l_, j_ = mm_list[i]
        sl_ = i % 2
        dma("sp", st_ld[sl_], WMv[sl_][:, 0:KC * P].rearrange("p (k n) -> p k n", k=KC),
            wmod[l_].rearrange("(k p) n -> p k n", p=P)[:, :, j_ * P:(j_ + 1) * P], w=[("WR", sl_)])
    mm_issue(0)
    for blk, (l, j) in enumerate(mm_list):
        if blk + 1 < len(mm_list):
            mm_issue(blk + 1)
        sl = blk % 2
        pcol = (blk % 16) * 2
        for kc in range(KC):
            pe(lambda e, sl=sl, kc=kc, pcol=pcol: e.matmul(
                ps[7][:, pcol:pcol + 2], WMv[sl][:, kc * P:(kc + 1) * P], SIL[:, kc * 2:kc * 2 + 2],
                start=(kc == 0), stop=(kc == KC - 1)),
               r=[("WR", sl), ("SC", "sil")], w=[("ps", 7)])
        act(lambda e, pcol=pcol, l=l, j=j: e.activation(
            out=MODP[:, (l * 12 + j) * 2:(l * 12 + j) * 2 + 2], in_=ps[7][:, pcol:pcol + 2], func=AF.Copy),
            r=[("ps", 7)], w=[("SC", "modp")])
    wr_ctr_init = len(mm_list)
    dma("sp", st_misc, modp_d, MODP, r=[("SC", "modp")], w=["modp_d"])
    ag(modp_d, moda_d, r=["modp_d"], w=["moda_d"])
    MODA = SC[:, 512:512 + NR * L * 24].rearrange("p (r f) -> p r f", r=NR)
    dma("sp", st_misc, MODA, moda_d.rearrange("(r p) f -> p r f", p=P), r=["moda_d"], w=[("SC", "moda")])
    BMS = SC[:, 1400:1400 + L * 96].rearrange("p (l c) -> p l c", l=L)
    dma("sp", st_misc, BMS, bmod, w=[("SC", "bms")])
    for l in range(L):
        for r_ in range(NR):
            for wch in range(2):
                src = MODA[:, r_, l * 24 + wch: l * 24 + 24: 2]
                dve(lambda e, l=l, r_=r_, wch=wch, src=src: e.tensor_tensor(
                    out=MOD[:, wch, l, r_ * 12:(r_ + 1) * 12], in0=src, in1=BMS[:, l, r_ * 12:(r_ + 1) * 12], op=ALU.add),
                    r=[("SC", "moda"), ("SC", "bms")], w=["MOD"])
    issue_wag(0)

    def layer_scalars(l):
        lam_init = 0.8 - 0.6 * math.exp(-0.3 * l)
        sq = math.sqrt(128.0)
        mults = [sq, sq, sq * (128.0 ** -0.5), sq, 8.0, 8.0, sq * (1.0 - lam_init)]
        for i, m_ in enumerate(mults):
            dve(lambda e, i=i, m_=m_: e.tensor_scalar(out=SML[:, i:i + 1], in0=GV[:, l, 32 + i:33 + i], scalar1=float(m_),
                                                      scalar2=None, op0=ALU.mult), r=["GV"], w=[("SML", i)])
        dve(lambda e: e.tensor_tensor(out=SML[:, 8:10], in0=LAMV[:, l, 0:4:2], in1=LAMV[:, l, 1:4:2], op=ALU.mult),
            r=["LAMV"], w=[("SML", 8)])
        dve(lambda e: e.tensor_copy(out=SML[:, 16:18].bitcast(BF16)[:, 0:2], in_=SML[:, 8:10]), r=[("SML", 8)], w=[("SML", 16)])
        lamb = SML[:, 16:18].bitcast(BF16)
        dve(lambda e: e.tensor_copy(out=SML[:, 10:12], in_=lamb[:, 0:2]), r=[("SML", 16)], w=[("SML", 10)])
        dve(lambda e: e.tensor_tensor(out=SML[:, 12:14], in0=SML[:, 8:10], in1=SML[:, 10:12], op=ALU.subtract),
            r=[("SML", 8), ("SML", 10)], w=[("SML", 12)])
        dve(lambda e: e.tensor_copy(out=lamb[:, 2:4], in_=SML[:, 12:14]), r=[("SML", 12)], w=[("SML", 17)])
        pe(lambda e: e.matmul(ps[7][:, 64:66], ones, lamb[:, 0:2], start=True, stop=False), r=[("SML", 16), ("CON", 1)], w=[("ps", 7)])
        pe(lambda e: e.matmul(ps[7][:, 64:66], ones, lamb[:, 2:4], start=False, stop=True), r=[("SML", 17), ("CON", 1)], w=[("ps", 7)])
        act(lambda e: e.activation(out=SML[:, 20:22], in_=ps[7][:, 64:66], func=AF.Exp), r=[("ps", 7)], w=[("SML", 20)])
        dve(lambda e: e.scalar_tensor_tensor(out=SML[:, 7:8], in0=SML[:, 21:22], scalar=float(-lam_init), in1=SML[:, 20:21],
                                             op0=ALU.add, op1=ALU.subtract), r=[("SML", 20)], w=[("SML", 7)])
        for wch in range(2):
            for which, (sc_i, sh_i, ga_i, goff) in enumerate([(1, 0, 2, 0), (4, 3, 5, 16)]):
                dve(lambda e, wch=wch, which=which, sc_i=sc_i, goff=goff: e.scalar_tensor_tensor(
                    out=DER[:, wch, which * 3 + 0, :], in0=MOD[:, wch, l, sc_i * 16:(sc_i + 1) * 16], scalar=1.0,
                    in1=GV[:, l, goff:goff + 16], op0=ALU.add, op1=ALU.mult), r=["MOD", "GV"], w=["DER"])
                dve(lambda e, wch=wch, which=which, sh_i=sh_i: e.tensor_copy(
                    out=DER[:, wch, which * 3 + 1, :], in_=MOD[:, wch, l, sh_i * 16:(sh_i + 1) * 16]), r=["MOD"], w=["DER"])
                dve(lambda e, wch=wch, which=which, ga_i=ga_i: e.tensor_copy(
                    out=DER[:, wch, which * 3 + 2, :], in_=MOD[:, wch, l, ga_i * 16:(ga_i + 1) * 16]), r=["MOD"], w=["DER"])
        dma("sp", st_misc, CW[:, 0:3, :], convw[:, l], w=["CW"])
        dma("sp", st_misc, CW[:, 3, :], convb[:, l], w=["CW"])

    wr_ctr = [wr_ctr_init]

    def load_block(src_ap, nel, extra_r=()):
        sl = wr_ctr[0] % 2
        wr_ctr[0] += 1
        dma("sp", st_ld[sl], WR[sl][:, 0:nel], src_ap, r=list(extra_r), w=[("WR", sl)])
        return sl

    def ring_blocks(descs):
        slots = {}

        def issue(i):
            src, nel, er = descs[i]
            slots[i] = load_block(src, nel, er)
        issue(0)
        for i in range(len(descs)):
            if i + 1 < len(descs):
                issue(i + 1)
            yield i, slots[i]

    CTXB = 6

    def pst(banks, slot):
        out = []
        for i, (s, n, kind) in enumerate(SPL):
            if kind == "x":
                out.append((ps[banks[i]][:, 0:n], ("ps", banks[i]), s, n, kind))
            else:
                out.append((ps[CTXB][:, slot * 32: slot * 32 + n], ("ps", CTXB), s, n, kind))
        return out

    def rmsnorm_modulate(l, which, dst_fn):
        SQ = SC[:, 0:T // 2 + 16].bitcast(BF16)
        RSTD = SC[:, 600:600 + T]
        TMP = SC[:, 600 + T + 8: 600 + 2 * T + 8] if 600 + 2 * T + 8 <= SCQ else None
        tile = pst([4, 5], 8 + which)
        for kc in range(KC):
            act(lambda e, kc=kc: e.activation(out=SQ[:, 0:T], in_=HX[:, kc, :], func=AF.Square), r=[("HX", kc)], w=[("SC", "sq")])
            for (pap, pk, s, n, kind) in tile:
                pe(lambda e, pap=pap, s=s, n=n, kc=kc: e.matmul(pap, ones, SQ[:, s:s + n], start=(kc == 0), stop=(kc == KC - 1)),
                   r=[("SC", "sq"), ("CON", 1)], w=[pk])
        for (pap, pk, s, n, kind) in tile:
            act(lambda e, pap=pap, s=s, n=n: e.activation(out=RSTD[:, s:s + n], in_=pap, func=AF.Sqrt, bias=EPSB[:, 0:1], scale=1.0),
                r=[pk], w=[("SC", "rstd")])
            dve(lambda e, s=s, n=n: e.reciprocal(out=RSTD[:, s:s + n], in_=RSTD[:, s:s + n]), r=[("SC", "rstd")], w=[("SC", "rstd")])
        sqd = math.sqrt(float(D))
        for kc in range(KC):
            for (s, n, kind) in SPL:
                wch = 0 if kind == "x" else 1
                d = dst_fn(kc, s, n)
                tmp = BO[:, 0:2 * T].bitcast(F32)[:, s:s + n] if which == 0 else BO[:, 0:2 * T].bitcast(F32)[:, s:s + n]
                dve(lambda e, kc=kc, s=s, n=n, wch=wch, tmp=tmp: e.scalar_tensor_tensor(
                    out=tmp, in0=HX[:, kc, s:s + n], scalar=DER[:, wch, which * 3, kc:kc + 1], in1=RSTD[:, s:s + n],
                    op0=ALU.mult, op1=ALU.mult), r=[("HX", kc), ("SC", "rstd"), "DER"], w=[("BO", "tmp", s)])
                act(lambda e, kc=kc, s=s, n=n, wch=wch, tmp=tmp, d=d: e.activation(
                    out=d, in_=tmp, func=AF.Identity, bias=DER[:, wch, which * 3 + 1, kc:kc + 1], scale=float(sqd)),
                    r=[("BO", "tmp", s), "DER"], w=[("BA", kc)])

    def linear_fm(slot, kcs, wcol0, act_fn, tile, act_keys):
        nk = len(kcs)
        for part in (tile[:-1], tile[-1:]):
            for i, (kw, ka) in enumerate(kcs):
                for (pap, pk, s, n, kind) in part:
                    pe(lambda e, pap=pap, kw=kw, ka=ka, s=s, n=n, i=i: e.matmul(
                        pap, WR[slot][:, kw * 256 + wcol0: kw * 256 + wcol0 + P], act_fn(ka, s, n), start=(i == 0), stop=(i == nk - 1)),
                       r=[("WR", slot), act_keys(ka)], w=[pk])

    for l in range(L):
        S.reset("ps")
        layer_scalars(l)
        S.reset("ps")
        S.reset("SC")
        if l + 1 < L:
            issue_casts(l + 1)
        S.reset("BA")
        S.reset("BO")
        rmsnorm_modulate(l, 0, BAu)
        S.reset("BO")
        S.reset("SC")
        BOF = BO[:].bitcast(F32)
        RAW = BOF[:, 0:T]
        RSTD2 = BOF[:, T:2 * T]
        T1 = RSTD2
        T2 = RAW
        ROPE = BOF[:, 2 * T: 6 * T].rearrange("p (a t) -> p a t", a=4)
        dma("sp", st_misc, ROPE, ropet, w=[("BO", "rope")])
        KST = [BO[:, 12 * T + i * T: 12 * T + (i + 1) * T] for i in range(2)]
        assert 14 * T <= KC * T4
        SCB = SC[:].bitcast(BF16)
        SQ2 = SCB[:, 0:T]
        XN = SCB[:, T:2 * T]
        VST = [SCB[:, 2 * T + i * 256: 2 * T + (i + 1) * 256] for i in range(2)]
        assert 2 * T + 512 <= 2 * SCQ

        roles = []
        for i in range(8):
            roles.append(("q", "a", i))
        for i in range(2):
            roles.append(("k", "a", i))
        roles += [("v", None, None)] * 2
        for i in range(4):
            roles.append(("q", "b", 8 + i))
        for i in range(4):
            roles.append(("k", "b", 2 + i))
        roles += [("v", None, None)] * 4
        for i in range(4):
            roles.append(("q", "c", 12 + i))
        for i in range(4):
            roles.append(("k", "c", 6 + i))
        roles += [("v", None, None)] * 4
        for i in range(48):
            roles.append(("g", None, i))
        vhead = {5: 0, 10: 2, 11: 4, 16: 6, 17: 8}

        kst_ctr = [0]
        par = [0]
        descs = [(w_f["w_in"][l][b * P:(b + 1) * P, :], 4096, [("wf", "w_in", l)]) for b in range(cfg.NB_IN)]
        for b, slot in ring_blocks(descs):
            if b in vhead:
                for tt in range(JT + 1):
                    s0, m = (tt * P, P) if tt < JT else (TL, TC)
                    bank = 7 if tt % 2 == 0 else 4
                    for kc in range(KC):
                        pe(lambda e, kc=kc, s0=s0, m=m, slot=slot, bank=bank: e.matmul(
                            ps[bank][0:m, 0:256], BAu(kc, s0, m), WR[slot][:, kc * 256:(kc + 1) * 256], start=(kc == 0), stop=(kc == KC - 1)),
                           r=[("WR", slot), ("BA", kc)], w=[("ps", bank)])
                    vs = VST[tt % 2]
                    vkey = ("SC", "vst", tt % 2)
                    act(lambda e, m=m, vs=vs, bank=bank: e.activation(out=vs[0:m, :], in_=ps[bank][0:m, 0:256], func=AF.Copy), r=[("ps", bank)], w=[vkey])
                    h0 = vhead[b]
                    if tt < JT:
                        dst = vb_d.rearrange("(h p) (j f) -> p h j f", p=P, f=P)[:, h0:h0 + 2, tt, :]
                        dma("sp", st_vst[tt % 2], dst, vs[:, :].rearrange("p (h f) -> p h f", h=2), r=[vkey], w=[("vb_d", b, tt)])
                    else:
                        dst = vcb_d.rearrange("(h i) f -> i h f", i=TC)[:, h0:h0 + 2, :]
                        dma("sp", st_vst[tt % 2], dst, vs[0:TC, :].rearrange("p (h f) -> p h f", h=2), r=[vkey], w=[("vcb_d", b)])
                continue
            for half in range(2):
                ch = 2 * b + half
                kind, rk, idx = roles[ch]
                p_ = par[0] % 2
                par[0] += 1
                tile = pst([0, 1] if p_ == 0 else [2, 3], p_)
                linear_fm(slot, [(kc, kc) for kc in range(KC)], half * P, BAu, tile, lambda ka: ("BA", ka))
                if kind == "g":
                    gs_ = KST[kst_ctr[0] % 2]
                    gkey = ("BO", "kst", kst_ctr[0] % 2)
                    gstream = st_gst[kst_ctr[0] % 2]
                    kst_ctr[0] += 1
                    for (pap, pk, s, n, kd) in tile:
                        act(lambda e, pap=pap, s=s, n=n, gs_=gs_: e.activation(out=gs_[:, s:s + n], in_=pap, func=AF.Sigmoid), r=[pk], w=[gkey])
                    dma("sp", gstream, gsp_d[idx * P:(idx + 1) * P, :], gs_[:, 0:T], r=[gkey], w=[("gsp_d", idx)])
                    continue
                onesm = onesc if rk == "c" else ones
                onek = ("CON", 2) if rk == "c" else ("CON", 1)
                dim = 64.0 if rk == "c" else 128.0
                gi = {("q", "a"): 0, ("k", "a"): 1, ("q", "b"): 2, ("k", "b"): 3, ("q", "c"): 4, ("k", "c"): 5}[(kind, rk)]
                for (pap, pk, s, n, kd) in tile:
                    act(lambda e, pap=pap, s=s, n=n: e.activation(out=SQ2[:, s:s + n], in_=pap, func=AF.Square), r=[pk], w=[("SC", "sq2")])
                    act(lambda e, pap=pap, s=s, n=n: e.activation(out=RAW[:, s:s + n], in_=pap, func=AF.Copy), r=[pk], w=[("BO", "raw")])
                t2 = pst([4, 5], 2)
                for (pap, pk, s, n, kd) in t2:
                    pe(lambda e, pap=pap, s=s, n=n, onesm=onesm: e.matmul(pap, onesm, SQ2[:, s:s + n], start=True, stop=True),
                       r=[("SC", "sq2"), onek], w=[pk])
                    act(lambda e, pap=pap, s=s, n=n, dim=dim: e.activation(out=RSTD2[:, s:s + n], in_=pap, func=AF.Sqrt,
                                                                          bias=EPSB[:, (1 if dim == 128.0 else 2):(2 if dim == 128.0 else 3)], scale=1.0),
                        r=[pk], w=[("BO", "rstd2")])
                    dve(lambda e, s=s, n=n: e.reciprocal(out=RSTD2[:, s:s + n], in_=RSTD2[:, s:s + n]), r=[("BO", "rstd2")], w=[("BO", "rstd2")])
                if kind == "q":
                    dest, dkey = (lambda s, n, idx=idx: BQc(idx, s, n)), ("BQ", idx)
                else:
                    ks_ = KST[kst_ctr[0] % 2]
                    dkey = ("BO", "kst", kst_ctr[0] % 2)
                    kstream = st_kst[kst_ctr[0] % 2]
                    kst_ctr[0] += 1
                    dest = (lambda s, n, ks_=ks_: ks_[:, s:s + n])
                if rk == "b":
                    for (s, n, kd) in SPL:
                        dve(lambda e, s=s, n=n, gi=gi, dest=dest: e.scalar_tensor_tensor(
                            out=dest(s, n), in0=RAW[:, s:s + n], scalar=SML[:, gi:gi + 1], in1=RSTD2[:, s:s + n], op0=ALU.mult, op1=ALU.mult),
                            r=[("BO", "raw"), ("BO", "rstd2"), ("SML", gi)], w=[dkey])
                else:
                    rotm, rotk = (rota, ("CON", 3)) if rk == "a" else (rotc, ("CON", 4))
                    ct, st_ = (0, 1) if rk == "a" else (2, 3)
                    for (s, n, kd) in SPL:
                        dve(lambda e, s=s, n=n, gi=gi: e.scalar_tensor_tensor(
                            out=XN[:, s:s + n], in0=RAW[:, s:s + n], scalar=SML[:, gi:gi + 1], in1=RSTD2[:, s:s + n], op0=ALU.mult, op1=ALU.mult),
                            r=[("BO", "raw"), ("BO", "rstd2"), ("SML", gi)], w=[("SC", "xn")])
                    t3 = pst([4, 5], 3)
                    for (pap, pk, s, n, kd) in t3:
                        pe(lambda e, pap=pap, s=s, n=n, rotm=rotm: e.matmul(pap, rotm, XN[:, s:s + n], start=True, stop=True),
                           r=[("SC", "xn"), rotk], w=[pk])
                        dve(lambda e, s=s, n=n, ct=ct: e.tensor_tensor(out=T1[:, s:s + n], in0=XN[:, s:s + n], in1=ROPE[:, ct, s:s + n], op=ALU.mult),
                             r=[("SC", "xn"), ("BO", "rope")], w=[("BO", "rstd2")])
                        dve(lambda e, pap=pap, s=s, n=n, st_=st_: e.tensor_tensor(out=T2[:, s:s + n], in0=pap, in1=ROPE[:, st_, s:s + n], op=ALU.mult),
                            r=[pk, ("BO", "rope")], w=[("BO", "raw")])
                        dve(lambda e, s=s, n=n, dest=dest: e.tensor_tensor(out=dest(s, n), in0=T1[:, s:s + n], in1=T2[:, s:s + n], op=ALU.add),
                             r=[("BO", "rstd2"), ("BO", "raw")], w=[dkey])
                if kind == "k":
                    dma("sp", kstream, kb_d[idx * P:(idx + 1) * P, :], ks_[:, 0:T], r=[dkey], w=[("kb_d", idx)])

        ag(kb_d, kall_d, r=[("kb_d", i) for i in range(10)], w=["kall_d"])
        ag(vb_d, vall_d, r=[("vb_d", b_, t_) for b_ in vhead for t_ in range(JT)], w=["vall_d"])
        ag(vcb_d, vcall_d, r=[("vcb_d", b_) for b_ in vhead], w=["vcall_d"])
        if l + 1 < L:
            issue_wag(l + 1)
        dq = "sp" if l < 2 else "pool"
        for i_, which in enumerate((-1, 0, 1)):
            S.add(dq, lambda e, i_=i_, which=which, dq=dq: e.dma_start(
                out=kloc_d[i_ * 10 * P:(i_ + 1) * 10 * P, :], in_=kall_d[bass.ds(S.spv[dq][which] * (10 * P), 10 * P), :]),
                ["kall_d"], [("kloc", i_)], stream=st_loc[dq])
            S.add(dq, lambda e, i_=i_, which=which, dq=dq: e.dma_start(
                out=vloc_d[i_ * 10 * P:(i_ + 1) * 10 * P, :], in_=vall_d[bass.ds(S.spv[dq][which] * (10 * P), 10 * P), :]),
                ["vall_d"], [("vloc", i_)], stream=st_loc[dq])

        S.reset("ps")
        S.reset("BA")
        S.reset("BO")
        S.reset("SC")
        NK = NR * T
        KSB = BA[:, 0:NK]
        VSB = BA[:, NK:NK + NKT * P].rearrange("p (t f) -> p t f", f=P)
        PT = SC[:, 0:1024].bitcast(BF16)
        REC = SC[:, 1024:1536]
        kall_v = kall_d.rearrange("(r h p) t -> h p r t", h=10, p=P)
        vall_v = vall_d.rearrange("(r h p) (j f) -> h p r j f", h=10, p=P, f=P)
        vcall_v = vcall_d.rearrange("(r h i) f -> h r i f", h=10, i=TC)

        def load_kv(h):
            dma("sp", st_kvl, KSB[:, 0:NR * TL].rearrange("p (r t) -> p r t", r=NR), kall_v[h, :, :, 0:TL], r=["kall_d"], w=[("BA", "k")])
            dma("sp", st_kvl, KSB[:, NR * TL:NK].rearrange("p (r t) -> p r t", r=NR), kall_v[h, :, :, TL:T], r=["kall_d"], w=[("BA", "k")])
            dma("sp", st_kvl, VSB[:, 0:NR * JT, :].rearrange("p (r j) f -> p r j f", r=NR), vall_v[h], r=["vall_d"], w=[("BA", "v")])
            for r_ in range(NR):
                dma("sp", st_kvl, VSB[(r_ % 4) * TC:(r_ % 4 + 1) * TC, NR * JT + r_ // 4, :], vcall_v[h, r_], r=["vcall_d"], w=[("BA", "v")])

        def attn_pass(qfn, qkey, prange, scale, ktiles, kfn, vfn, groups, bias_fn, out_fn):
            p0, p1 = prange
            ng = len(groups)
            sb_ = lambda g, buf: ps[g * 2 + buf]
            accb = lambda g: ps[4 + g]
            denb = lambda g: ps[6 + g]
            nkt = len(ktiles)

            def qk(i):
                kt = ktiles[i]
                for g, (qs, qn) in enumerate(groups):
                    has_bias = bias_fn is not None and bias_fn(kt, g, None) is not None
                    pe(lambda e, g=g, qs=qs, qn=qn, kt=kt, i=i, has_bias=has_bias: e.matmul(
                        sb_(g, i % 2)[:, 0:qn], kfn(kt)[p0:p1, :], qfn(qs, qn)[p0:p1, :], start=True, stop=not has_bias),
                       r=[("BA", "k"), qkey], w=[("ps", g * 2 + i % 2)])
                    if has_bias:
                        for (lh, rh, keys, last) in bias_fn(kt, g, (qs, qn)):
                            pe(lambda e, g=g, qn=qn, i=i, lh=lh, rh=rh, last=last: e.matmul(
                                sb_(g, i % 2)[:, 0:qn], lh, rh, start=False, stop=last), r=keys, w=[("ps", g * 2 + i % 2)])
                    act(lambda e, g=g, qn=qn, i=i: e.activation(out=PT[:, (g * 2 + i % 2) * 512:(g * 2 + i % 2) * 512 + qn],
                                                               in_=sb_(g, i % 2)[:, 0:qn], func=AF.Exp, scale=float(scale)),
                        r=[("ps", g * 2 + i % 2)], w=[("SC", "pt", g, i % 2)])

            def pv(i):
                kt = ktiles[i]
                for g, (qs, qn) in enumerate(groups):
                    pt = PT[:, (g * 2 + i % 2) * 512:(g * 2 + i % 2) * 512 + qn]
                    pe(lambda e, g=g, qn=qn, kt=kt, i=i, pt=pt: e.matmul(accb(g)[:, 0:qn], vfn(kt), pt, start=(i == 0), stop=(i == nkt - 1)),
                       r=[("BA", "v"), ("SC", "pt", g, i % 2)], w=[("ps", 4 + g)])
                    pe(lambda e, g=g, qn=qn, i=i, pt=pt: e.matmul(denb(g)[:, 0:qn], ones, pt, start=(i == 0), stop=(i == nkt - 1)),
                       r=[("CON", 1), ("SC", "pt", g, i % 2)], w=[("ps", 6 + g)])

            qk(0)
            for i in range(nkt):
                if i + 1 < nkt:
                    qk(i + 1)
                pv(i)
            for g, (qs, qn) in enumerate(groups):
                dve(lambda e, g=g, qn=qn: e.reciprocal(out=REC[:, 0:qn], in_=denb(g)[:, 0:qn]), r=[("ps", 6 + g)], w=[("SC", "rec")])
                out_fn(g, qs, qn, accb(g)[:, 0:qn], REC[:, 0:qn], ("ps", 4 + g))

        lat_groups = [(s, n) for (s, n) in cfg.LS]
        lat_sets = [lat_groups[i:i + 2] for i in range(0, len(lat_groups), 2)]
        all_kt = list(range(NKT))
        ctx_kt = list(range(NR * JT, NKT))
        kfn_std = lambda kt: KSB[:, kt * P:(kt + 1) * P]
        vfn_std = lambda kt: VSB[:, kt, :]

        def out_plain(och):
            def f(g, qs, qn, acc, rec, acck):
                dve(lambda e: e.tensor_tensor(out=BOc(och, qs, qn), in0=acc, in1=rec, op=ALU.mult),
                    r=[acck, ("SC", "rec")], w=[("BO", och)])
            return f

        def out_diff(och, comp):
            def f(g, qs, qn, acc, rec, acck):
                if comp == 0:
                    dve(lambda e: e.tensor_tensor(out=BOc(och, qs, qn), in0=acc, in1=rec, op=ALU.mult),
                        r=[acck, ("SC", "rec")], w=[("BO", och)])
                else:
                    dve(lambda e: e.tensor_tensor(out=REC[:, 0:qn], in0=acc, in1=rec, op=ALU.mult),
                        r=[acck, ("SC", "rec")], w=[("SC", "rec")])
                    dve(lambda e: e.scalar_tensor_tensor(out=BOc(och, qs, qn), in0=REC[:, 0:qn], scalar=SML[:, 7:8], in1=BOc(och, qs, qn),
                                                         op0=ALU.mult, op1=ALU.add), r=[("SC", "rec"), ("BO", och), ("SML", 7)], w=[("BO", och)])
            return f

        sc_a = 128.0 ** -0.5
        for kvh in range(2):
            load_kv(kvh)
            for qh in range(kvh * 4, kvh * 4 + 4):
                qfn = lambda s, n, qh=qh: BQc(qh, s, n)
                for gset in lat_sets:
                    attn_pass(qfn, ("BQ", qh), (0, P), sc_a, all_kt, kfn_std, vfn_std, gset, None, out_plain(qh))
                attn_pass(qfn, ("BQ", qh), (0, P), sc_a, ctx_kt, kfn_std, vfn_std, [(TL, TC)], None, out_plain(qh))
        sc_c = 64.0 ** -0.5
        for h in range(4):
            load_kv(6 + h)
            qfn = lambda s, n, h=h: BQc(12 + h, s, n)
            for comp in range(2):
                pr = (comp * 64, comp * 64 + 64)
                for gset in lat_sets:
                    attn_pass(qfn, ("BQ", 12 + h), pr, sc_c, all_kt, kfn_std, vfn_std, gset, None, out_diff(12 + h, comp))
                attn_pass(qfn, ("BQ", 12 + h), pr, sc_c, ctx_kt, kfn_std, vfn_std, [(TL, TC)], None, out_diff(12 + h, comp))
        S.reset("BA")
        NLK = (RPC + 8) * 64
        KB = BA[:, 0:NLK + CTX]
        ob = NLK + CTX
        VB = BA[:, ob: ob + (NTB + 2) * P].rearrange("p (t f) -> p t f", f=P)
        ob += (NTB + 2) * P
        GT = BA[:, ob: ob + E * 64].rearrange("p (e q) -> p e q", q=64)
        ob += E * 64
        RMQ = BA[0:RPC, ob: ob + TL]
        ob += TL
        LMK = BA[0:RPC, ob: ob + NTB * P]
        ob += NTB * P
        assert ob <= KC * T4
        dma("sp", st_kvl, RMQ, rmq_d, w=[("BA", "rmq")])
        dma("sp", st_kvl, LMK, lmk_d, w=[("BA", "lmk")])
        kall_r = kall_d.rearrange("(r h p) t -> r h p t", h=10, p=P)
        vall_r = vall_d.rearrange("(r h p) (j f) -> r h p j f", h=10, p=P, f=P)
        for h in range(4):
            hh = 2 + h

            def kld(dst, which, c0, c1, hh=hh):
                i_ = which + 1
                dma("sp", st_kvl, dst, kloc_d[(i_ * 10 + hh) * P:(i_ * 10 + hh + 1) * P, c0:c1], r=[("kloc", i_)], w=[("BA", "k")])

            def vld(dst, which, j0, j1, hh=hh):
                i_ = which + 1
                dma("sp", st_kvl, dst.rearrange("p j f -> p (j f)"), vloc_d[(i_ * 10 + hh) * P:(i_ * 10 + hh + 1) * P, j0 * P:j1 * P],
                    r=[("vloc", i_)], w=[("BA", "v")])
            kld(KB[:, 0:256], -1, TL - 256, TL)
            kld(KB[:, 256:256 + TL], 0, 0, TL)
            kld(KB[:, 256 + TL:NLK], 1, 0, 256)
            dma("sp", st_kvl, KB[:, NLK:NLK + CTX].rearrange("p (r t) -> p r t", r=NR), kall_v[hh, :, :, TL:T], r=["kall_d"], w=[("BA", "k")])
            vld(VB[:, 0:2, :], -1, JT - 2, JT)
            vld(VB[:, 2:2 + JT, :], 0, 0, JT)
            vld(VB[:, 2 + JT:NTB, :], 1, 0, 2)
            for r_ in range(NR):
                dma("sp", st_kvl, VB[(r_ % 4) * TC:(r_ % 4 + 1) * TC, NTB + r_ // 4, :], vcall_v[hh, r_], r=["vcall_d"], w=[("BA", "v")])
            ttrb_v = ttrb_d.rearrange("(l h j) (k q) -> l h j k q", l=L, h=4, q=64)
            for jr in range(2):
                dma("sp", st_gt, GT[jr * 64:(jr + 1) * 64, :, :], ttrb_v[l, h, :, 1 - jr:1 - jr + E, :],
                    r=["ttrb_d"], w=[("BA", "gt")])
            qfn = lambda s, n, h=h: BQc(8 + h, s, n)
            kfn_b = lambda kt: KB[:, kt * P:(kt + 1) * P]
            vfn_b = lambda kt: VB[:, kt, :]
            for gset in lat_sets:
                r0 = gset[0][0] // 64
                r1 = (gset[-1][0] + gset[-1][1]) // 64
                kts = list(range(r0 // 2, (r1 - 1 + 8) // 2 + 1)) + [NTB, NTB + 1]

                def bias_fn(kt, g, q, gset=gset):
                    if kt >= NTB:
                        return None
                    if q is None:
                        return True
                    qs, qn = q
                    i0 = qs // 64
                    e0 = i0 + 4 - 2 * kt - cfg.EMIN
                    nrow = qn // 64
                    return [(LMK[:, kt * P:(kt + 1) * P], RMQ[:, qs:qs + qn], [("BA", "rmq"), ("BA", "lmk")], False),
                            (ident, GT[:, e0:e0 + nrow, :].rearrange("p e q -> p (e q)"), [("BA", "gt"), ("CON", 0)], True)]
                attn_pass(qfn, ("BQ", 8 + h), (0, P), 1.0, kts, kfn_b, vfn_b, gset, bias_fn, out_plain(8 + h))
            attn_pass(qfn, ("BQ", 8 + h), (0, P), 1.0, [NTB, NTB + 1], kfn_b, vfn_b, [(TL, TC)], None, out_plain(8 + h))

        if getattr(cfg, "dbg", None) == "attn":
            for ch in range(KC):
                dma("pool", S.dstream(f"dbg{ch}"), outT[:, ch, :], BOc(ch, 0, TL), r=[("BO", ch)], w=[("outT", ch)])
            break
        S.reset("ps")
        S.reset("BA")
        S.reset("SC")
        BAF = BA[:].bitcast(F32)
        SQ3 = BA[:, 0:T]
        RS3 = BAF[:, T:2 * T]
        for h in range(4):
            och = 12 + h
            act(lambda e, och=och: e.activation(out=SQ3, in_=BOc(och, 0, T), func=AF.Square), r=[("BO", och)], w=[("BA", "sq3")])
            t4 = pst([4, 5], 4)
            for (pap, pk, s, n, kd) in t4:
                pe(lambda e, pap=pap, s=s, n=n: e.matmul(pap, ones, SQ3[:, s:s + n], start=True, stop=True), r=[("BA", "sq3"), ("CON", 1)], w=[pk])
                act(lambda e, pap=pap, s=s, n=n: e.activation(out=RS3[:, s:s + n], in_=pap, func=AF.Sqrt, bias=EPSB[:, 1:2], scale=1.0),
                    r=[pk], w=[("BA", "rs3")])
                dve(lambda e, s=s, n=n: e.reciprocal(out=RS3[:, s:s + n], in_=RS3[:, s:s + n]), r=[("BA", "rs3")], w=[("BA", "rs3")])
            dve(lambda e, och=och: e.scalar_tensor_tensor(out=BOc(och, 0, T), in0=BOc(och, 0, T), scalar=SML[:, 6:7], in1=RS3[:, 0:T],
                                                          op0=ALU.mult, op1=ALU.mult), r=[("BO", och), ("BA", "rs3"), ("SML", 6)], w=[("BO", och)])
        gb0 = 4 * T
        GB = [BA[:, gb0 + i * 3 * T: gb0 + (i + 1) * 3 * T].rearrange("p (j t) -> p j t", j=3) for i in range(2)]
        tb0 = (gb0 + 6 * T + 1) // 2 + 1
        MT = [BAF[:, tb0 + i * T: tb0 + (i + 1) * T] for i in range(3)]
        assert 2 * (tb0 + 3 * T) <= KC * T4
        gsp_v = gsp_d.rearrange("(j c p) t -> c p j t", j=3, p=P)
        brk = [(0, 8, 0), (8, 4, 8), (12, 4, 12)]
        for dp in range(8):
            slot = load_block(w_f["w_br"][l][dp * P:(dp + 1) * P, :], 4096, extra_r=[("wf", "w_br", l)])
            for half in range(2):
                dc = dp * 2 + half
                gbi = dc % 2
                dma("sp", st_gb[gbi], GB[gbi], gsp_v[dc], r=[("gsp_d", j_ * 16 + dc) for j_ in range(3)], w=[("BA", "gb", gbi)])
                tiles = [pst([0, 1], 0), pst([2, 3], 1), pst([4, 5], 2)]
                for j, (k0, nk, oc0) in enumerate(brk):
                    linear_fm(slot, [(k0 + i, oc0 + i) for i in range(nk)], half * P, lambda ka, s, n: BOc(ka, s, n), tiles[j], lambda ka: ("BO", ka))
                for si, (s, n, kd) in enumerate(SPL):
                    for j in range(3):
                        pap, pk = tiles[j][si][0], tiles[j][si][1]
                        dve(lambda e, pap=pap, j=j, s=s, n=n, gbi=gbi: e.tensor_tensor(out=MT[j][:, s:s + n], in0=pap, in1=GB[gbi][:, j, s:s + n], op=ALU.mult),
                            r=[pk, ("BA", "gb", gbi)], w=[("BA", "mt", j)])
                    dve(lambda e, s=s, n=n: e.tensor_tensor(out=MT[0][:, s:s + n], in0=MT[0][:, s:s + n], in1=MT[1][:, s:s + n], op=ALU.add),
                         r=[("BA", "mt", 0), ("BA", "mt", 1)], w=[("BA", "mt", 0)])
                    dve(lambda e, s=s, n=n, dc=dc: e.tensor_tensor(out=BQc(dc, s, n), in0=MT[0][:, s:s + n], in1=MT[2][:, s:s + n], op=ALU.add),
                         r=[("BA", "mt", 0), ("BA", "mt", 2)], w=[("BQ", dc)])
        for dp in range(8):
            slot = load_block(w_f["w_o"][l][dp * P:(dp + 1) * P, :], 4096, extra_r=[("wf", "w_o", l)])
            for half in range(2):
                dc = dp * 2 + half
                tile = pst([0, 1] if dc % 2 == 0 else [2, 3], dc % 2)
                linear_fm(slot, [(kc, kc) for kc in range(KC)], half * P, lambda ka, s, n: BQc(ka, s, n), tile, lambda ka: ("BQ", ka))
                for (pap, pk, s, n, kd) in tile:
                    wch = 0 if kd == "x" else 1
                    dve(lambda e, pap=pap, s=s, n=n, dc=dc, wch=wch: e.scalar_tensor_tensor(
                        out=HX[:, dc, s:s + n], in0=pap, scalar=DER[:, wch, 2, dc:dc + 1], in1=HX[:, dc, s:s + n], op0=ALU.mult, op1=ALU.add),
                        r=[pk, ("HX", dc), "DER"], w=[("HX", dc)])

        S.reset("BA")
        S.reset("BO")
        S.reset("SC")

        def vx_dst(kc, s, n):
            off = 1 if s < TL else 3
            return BA[:, kc * T4 + off + s: kc * T4 + off + s + n]
        rmsnorm_modulate(l, 1, vx_dst)
        BA3 = BA[:].rearrange("p (k t) -> p k t", k=KC)
        for i, col in enumerate((1, TL, TL + 3, TL + 2 + TC)):
            dve(lambda e, i=i, col=col: e.tensor_copy(out=HST[:, :, i:i + 1], in_=BA3[:, :, col:col + 1]),
                r=[("BA", kc) for kc in range(KC)], w=["HST"])
        dma("sp", st_h, hb_d, HST[:].rearrange("p k c -> p (k c)"), r=["HST"], w=["hb_d"])
        ag(hb_d, hall_d, r=["hb_d"], w=["hall_d"])
        S.add(dq, lambda e, dq=dq: e.dma_start(out=HST2[:, 0, :], in_=hall_d[bass.ds(S.spv[dq][-1] * P, P), :]),
              ["hall_d"], [("HST2", 0)], stream=st_hq[dq])
        S.add(dq, lambda e, dq=dq: e.dma_start(out=HST2[:, 1, :], in_=hall_d[bass.ds(S.spv[dq][1] * P, P), :]),
              ["hall_d"], [("HST2", 1)], stream=st_hq[dq])
        H2 = HST2[:].rearrange("p w (k c) -> p w k c", c=4)
        for (col, wsel, c_) in ((0, 0, 1), (TL + 1, 1, 0), (TL + 2, 0, 3), (TL + 3 + TC, 1, 2)):
            dve(lambda e, col=col, wsel=wsel, c_=c_: e.tensor_scalar(out=BA3[:, :, col:col + 1], in0=H2[:, wsel, :, c_:c_ + 1],
                                                                      scalar1=HFL[:, wsel:wsel + 1], scalar2=None, op0=ALU.mult),
                r=[("HST2", wsel), "HFL"], w=[("BA", kc) for kc in range(KC)])

        FS = [(0, min(512, T4))]
        while FS[-1][0] + FS[-1][1] < T4:
            s_ = FS[-1][0] + FS[-1][1]
            FS.append((s_, min(512, T4 - s_)))
        NF = len(FS)
        assert NF <= 3
        BOF = BO[:].bitcast(F32)
        HBA = BOF[:, 0:T4]
        HBG = BOF[:, T4:2 * T4]
        AP_ = BOF[:, 2 * T4:3 * T4]
        GP_ = BOF[:, 3 * T4:4 * T4]
        SG = BOF[:, 4 * T4:5 * T4]
        TO = T + 2

        def ffn_tile(par_):
            banks = [0, 1, 2] if par_ == 0 else [3, 4, 5]
            return [(ps[banks[i]][:, 0:n], ("ps", banks[i]), s, n) for i, (s, n) in enumerate(FS)]
        pair0 = 0
        for g, gsz in enumerate(cfg.GS):
            for jj in range(gsz):
                j = pair0 + jj
                slot = load_block(w_f["w_up"][l][j * P:(j + 1) * P, :], 4096, extra_r=[("wf", "w_up", l)])
                for half, (HB, hk, OUT, ok) in enumerate(((HBA, "hba", AP_, "ap"), (HBG, "hbg", GP_, "gp"))):
                    tile = ffn_tile(half)
                    for kc in range(KC):
                        for (pap, pk, s, n) in tile:
                            pe(lambda e, pap=pap, kc=kc, s=s, n=n, slot=slot, half=half: e.matmul(
                                pap, WR[slot][:, kc * 256 + half * P: kc * 256 + half * P + P], BA[:, kc * T4 + s: kc * T4 + s + n],
                                start=(kc == 0), stop=(kc == KC - 1)), r=[("WR", slot), ("BA", kc)], w=[pk])
                    for (pap, pk, s, n) in tile:
                        act(lambda e, pap=pap, s=s, n=n, HB=HB: e.activation(out=HB[:, s:s + n], in_=pap, func=AF.Copy), r=[pk], w=[("BO", hk)])
                    ci = 2 * j + half
                    dve(lambda e, HB=HB, OUT=OUT, ci=ci: e.tensor_scalar(out=OUT[:, 0:TO], in0=HB[:, 0:TO], scalar1=CW[:, 0, ci:ci + 1],
                                                                         scalar2=CW[:, 3, ci:ci + 1], op0=ALU.mult, op1=ALU.add),
                        r=[("BO", hk), "CW"], w=[("BO", ok)])
                    for tap in (1, 2):
                        dve(lambda e, HB=HB, OUT=OUT, ci=ci, tap=tap: e.scalar_tensor_tensor(
                            out=OUT[:, 0:TO], in0=HB[:, tap:tap + TO], scalar=CW[:, tap, ci:ci + 1], in1=OUT[:, 0:TO], op0=ALU.mult, op1=ALU.add),
                            r=[("BO", hk), ("BO", ok), "CW"], w=[("BO", ok)])
                act(lambda e: e.activation(out=SG[:, 0:TO], in_=GP_[:, 0:TO], func=AF.Silu), r=[("BO", "gp")], w=[("BO", "sg")])
                dve(lambda e, jj=jj: e.tensor_tensor(out=BQ[:, jj * T4: jj * T4 + TO], in0=SG[:, 0:TO], in1=AP_[:, 0:TO], op=ALU.mult),
                     r=[("BO", "sg"), ("BO", "ap")], w=[("BQ", jj)])
            for dp in range(8):
                slot = load_block(w_f["w_down"][l][dp * P:(dp + 1) * P, pair0 * 256:(pair0 + gsz) * 256], gsz * 256,
                                  extra_r=[("wf", "w_down", l)])
                for half in range(2):
                    dc = dp * 2 + half
                    tile = pst([0, 1] if dc % 2 == 0 else [2, 3], dc % 2)

                    def actf(ka, s, n):
                        off = 0 if s < TL else 2
                        return BQ[:, ka * T4 + off + s: ka * T4 + off + s + n]
                    linear_fm(slot, [(i, i) for i in range(gsz)], half * P, actf, tile, lambda ka: ("BQ", ka))
                    for (pap, pk, s, n, kd) in tile:
                        wch = 0 if kd == "x" else 1
                        dve(lambda e, pap=pap, s=s, n=n, dc=dc, wch=wch: e.scalar_tensor_tensor(
                            out=HX[:, dc, s:s + n], in0=pap, scalar=DER[:, wch, 5, dc:dc + 1], in1=HX[:, dc, s:s + n], op0=ALU.mult, op1=ALU.add),
                            r=[pk, ("HX", dc), "DER"], w=[("HX", dc)])
            pair0 += gsz
        S.reset("BQ")
        S.reset("BO")

    if getattr(cfg, "dbg", None) is None:
        dma("sp", st_out, outT, HX[:, :, 0:TL], r=[("HX", kc) for kc in range(KC)], w=["outT"])
        S.add("sp", None, ["outT"], [])
    else:
        S.add("sp", None, [("outT", ch) for ch in range(KC)], [])
    S.emit()
    stack.close()
    return nc


_CACHE = {}


def run(cfg, inputs):
    maps = prep_inputs(cfg, inputs)
    key = (cfg.S, cfg.DFF, cfg.L)
    if key not in _CACHE:
        _CACHE[key] = build(cfg)
    nc = _CACHE[key]
    res = run_bass_kernel_spmd(nc, maps, core_ids=list(range(NR)))
    outs = []
    for c in range(NR):
        o = np.asarray(res.results[c]["outT"], np.float32)
        outs.append(o.transpose(2, 1, 0).reshape(cfg.TL, D))
    return np.concatenate(outs, 0)[None].astype(np.float32)


def kernel(**inputs):
    cfg = Cfg()
    return run(cfg, inputs)
```
